# Optimizing a Trainium2 kernel written in Bass

```python
import math
import jax, jax.numpy as jnp
from jax import lax
import numpy as np

D_MODEL = 1024
BATCH = 16
SEQ = 2048
DEPTH = 4

MEM_LEN = 256
EPS = 1e-6
SSM_WIDTH = 384
SSM_GROUP = 16
SSM_GROUPS = SSM_WIDTH // SSM_GROUP
SSM_STATE = 64
DIFF_HEADS = 4
DIFF_HD = 64
DIFF_QK = DIFF_HEADS * 2 * DIFF_HD
DIFF_V = DIFF_HEADS * 2 * DIFF_HD
Q_BLOCK = 128
ROPE_THETA = 10000.0
LRU_WIDTH = 512
LRU_HEADS = 8
LRU_HD = LRU_WIDTH // LRU_HEADS
LRU_CONV = 4
LRU_C = 8.0
N_BRANCH = 3
IN_WIDTHS = (SSM_WIDTH, DIFF_QK, DIFF_QK, DIFF_V, LRU_WIDTH, LRU_WIDTH, N_BRANCH * D_MODEL)
D_IN = SSM_WIDTH + 2 * DIFF_QK + DIFF_V + 2 * LRU_WIDTH + N_BRANCH * D_MODEL
XATTN_HEADS = 4
XATTN_HD = D_MODEL // XATTN_HEADS
D_FF = 2816
FFN_CONV = 3

kernel_name = "hybrid_s5_diffattn_rglru_convffn"


def rmsnorm(x, g):
    x32 = x.astype(jnp.float32)
    y = x32 * lax.rsqrt(jnp.mean(x32 * x32, axis=-1, keepdims=True) + EPS)
    return (y * g.astype(jnp.float32)).astype(x.dtype)


def causal_dwconv(x, w, b):
    k_w = w.shape[0]
    L = x.shape[1]
    xp = jnp.pad(x, ((0, 0), (k_w - 1, 0), (0, 0)))
    y = xp[:, 0:L] * w[0]
    for j in range(1, k_w):
        y = y + xp[:, j:j + L] * w[j]
    return y + b


def rope_tables(positions):
    inv = jnp.exp(-math.log(ROPE_THETA) * jnp.arange(0, DIFF_HD, 2, dtype=jnp.float32) / DIFF_HD)
    ang = positions.astype(jnp.float32)[..., None] * inv
    return jnp.cos(ang)[:, :, None, None, :], jnp.sin(ang)[:, :, None, None, :]


def apply_rope(x, cos, sin):
    half = x.shape[-1] // 2
    x1, x2 = x[..., :half], x[..., half:]
    c = cos.astype(x.dtype)
    s = sin.astype(x.dtype)
    return jnp.concatenate([x1 * c - x2 * s, x2 * c + x1 * s], axis=-1)


def _complex_combine(e1, e2):
    a1r, a1i, b1r, b1i = e1
    a2r, a2i, b2r, b2i = e2
    return (a2r * a1r - a2i * a1i,
            a2r * a1i + a2i * a1r,
            a2r * b1r - a2i * b1i + b2r,
            a2r * b1i + a2i * b1r + b2i)


def _real_combine(e1, e2):
    a1, b1 = e1
    a2, b2 = e2
    return a2 * a1, a2 * b1 + b2


def s5_branch(u, lam_re, lam_im, log_step, b_re, b_im, c_re, c_im, d_skip, w_glu, b_glu):
    f32 = jnp.float32
    Bsz, L, _ = u.shape
    u32 = u.astype(f32)
    ug = u32.reshape(Bsz, L, SSM_GROUPS, SSM_GROUP)
    lr = lam_re.astype(f32)
    li = lam_im.astype(f32)
    dt = jnp.exp(log_step.astype(f32))[:, None]
    mag = jnp.exp(lr * dt)
    ab_r = mag * jnp.cos(li * dt)
    ab_i = mag * jnp.sin(li * dt)
    den = lr * lr + li * li
    nr = ab_r - 1.0
    f_r = (nr * lr + ab_i * li) / den
    f_i = (ab_i * lr - nr * li) / den
    br = b_re.astype(f32)
    bi = b_im.astype(f32)
    bb_r = f_r[..., None] * br - f_i[..., None] * bi
    bb_i = f_r[..., None] * bi + f_i[..., None] * br
    bu_r = jnp.einsum('gph,blgh->blgp', bb_r, ug)
    bu_i = jnp.einsum('gph,blgh->blgp', bb_i, ug)
    a_r = jnp.broadcast_to(ab_r, (1, L, SSM_GROUPS, SSM_STATE))
    a_i = jnp.broadcast_to(ab_i, (1, L, SSM_GROUPS, SSM_STATE))
    _, _, s_r, s_i = lax.associative_scan(_complex_combine, (a_r, a_i, bu_r, bu_i), axis=1)
    y = (jnp.einsum('ghp,blgp->blgh', c_re.astype(f32), s_r)
         - jnp.einsum('ghp,blgp->blgh', c_im.astype(f32), s_i))
    y = y.reshape(Bsz, L, SSM_WIDTH) + d_skip.astype(f32) * u32
    y = jax.nn.gelu(y)
    z = y @ w_glu.astype(f32) + b_glu.astype(f32)
    out = z[..., :SSM_WIDTH] * jax.nn.sigmoid(z[..., SSM_WIDTH:])
    return out.astype(u.dtype)


def diff_attention_branch(q, k, v, cos, sin, lq1, lk1, lq2, lk2, subln_g, lambda_init):
    Bsz, L, _ = q.shape
    q = apply_rope(q.reshape(Bsz, L, DIFF_HEADS, 2, DIFF_HD), cos, sin)
    k = apply_rope(k.reshape(Bsz, L, DIFF_HEADS, 2, DIFF_HD), cos, sin)
    v = v.reshape(Bsz, L, DIFF_HEADS, 2 * DIFF_HD)
    f32 = jnp.float32
    lam = (jnp.exp(jnp.sum(lq1.astype(f32) * lk1.astype(f32)))
           - jnp.exp(jnp.sum(lq2.astype(f32) * lk2.astype(f32))) + lambda_init)
    scale = DIFF_HD ** -0.5
    nb = L // Q_BLOCK
    qb = q.reshape(Bsz, nb, Q_BLOCK, DIFF_HEADS, 2, DIFF_HD).transpose(1, 0, 2, 3, 4, 5)
    key_pos = jnp.arange(L)

    def block(args):
        qblk, bidx = args
        s = jnp.einsum('bqhcd,bkhcd->bhcqk', qblk, k).astype(f32) * scale
        qpos = bidx * Q_BLOCK + jnp.arange(Q_BLOCK)
        mask = key_pos[None, :] <= qpos[:, None]
        s = jnp.where(mask, s, -jnp.inf)
        p = jax.nn.softmax(s, axis=-1)
        attn = p[:, :, 0] - lam * p[:, :, 1]
        return jnp.einsum('bhqk,bkhe->bqhe', attn.astype(v.dtype), v)

    o = lax.map(block, (qb, jnp.arange(nb)))
    o = o.transpose(1, 0, 2, 3, 4).reshape(Bsz, L, DIFF_HEADS, 2 * DIFF_HD)
    o = rmsnorm(o, subln_g) * (1.0 - lambda_init)
    return o.reshape(Bsz, L, DIFF_V)


def rglru_branch(xr, gr, conv_w, conv_b, wa, ba, wx, bx, lam):
    Bsz, L, _ = xr.shape
    f32 = jnp.float32
    xc = causal_dwconv(xr, conv_w, conv_b)
    xh = xc.reshape(Bsz, L, LRU_HEADS, LRU_HD)
    r = jax.nn.sigmoid(jnp.einsum('blhi,hij->blhj', xh, wa).reshape(Bsz, L, LRU_WIDTH) + ba)
    ig = jax.nn.sigmoid(jnp.einsum('blhi,hij->blhj', xh, wx).reshape(Bsz, L, LRU_WIDTH) + bx)
    log_a = -LRU_C * r.astype(f32) * jax.nn.softplus(-lam.astype(f32))
    a = jnp.exp(log_a)
    b = jnp.sqrt(-jnp.expm1(2.0 * log_a)) * (ig * xc).astype(f32)
    _, h = lax.associative_scan(_real_combine, (a, b), axis=1)
    return h.astype(xr.dtype) * jax.nn.gelu(gr)


def cross_attention(h, m, wq, wkv, wo):
    Bsz, L, _ = h.shape
    M = m.shape[1]
    q = (h @ wq).reshape(Bsz, L, XATTN_HEADS, XATTN_HD)
    kv = m @ wkv
    k = kv[..., :D_MODEL].reshape(Bsz, M, XATTN_HEADS, XATTN_HD)
    v = kv[..., D_MODEL:].reshape(Bsz, M, XATTN_HEADS, XATTN_HD)
    s = jnp.einsum('bqhd,bkhd->bhqk', q, k).astype(jnp.float32) * (XATTN_HD ** -0.5)
    p = jax.nn.softmax(s, axis=-1)
    o = jnp.einsum('bhqk,bkhd->bqhd', p.astype(v.dtype), v).reshape(Bsz, L, D_MODEL)
    return o @ wo


def conv_ffn(h, w_up, conv_w, conv_b, w_down):
    up = causal_dwconv(h @ w_up, conv_w, conv_b)
    val = up[..., :D_FF]
    gate = up[..., D_FF:]
    return (jax.nn.silu(gate) * val) @ w_down


def setup_inputs(seed: int = 0) -> dict:
    key = jax.random.key(seed)
    ks = iter(jax.random.split(key, 64))
    f32 = jnp.float32

    def nrm(shape, scale):
        return scale * jax.random.normal(next(ks), shape, f32)

    x = nrm((BATCH, SEQ, D_MODEL), 1.0)
    mem = nrm((BATCH, MEM_LEN, D_MODEL), 1.0)
    offs = jax.random.randint(next(ks), (BATCH, 1), 0, 4096, dtype=jnp.int32)
    positions = (offs + jnp.arange(SEQ, dtype=jnp.int32)[None, :]).astype(jnp.int32)
    norm_mix_g = 1.0 + nrm((DEPTH, D_MODEL), 0.05)
    w_in = nrm((DEPTH, D_MODEL, D_IN), D_MODEL ** -0.5)
    ssm_lambda_re = -0.5 + nrm((DEPTH, SSM_GROUPS, SSM_STATE), 0.01)
    ssm_lambda_im = math.pi * jnp.arange(SSM_STATE, dtype=f32)[None, None, :] + nrm((DEPTH, SSM_GROUPS, SSM_STATE), 0.01)
    ssm_log_step = jax.random.uniform(next(ks), (DEPTH, SSM_GROUPS), f32, math.log(1e-3), math.log(1e-1))
    ssm_b_re = nrm((DEPTH, SSM_GROUPS, SSM_STATE, SSM_GROUP), (2 * SSM_GROUP) ** -0.5)
    ssm_b_im = nrm((DEPTH, SSM_GROUPS, SSM_STATE, SSM_GROUP), (2 * SSM_GROUP) ** -0.5)
    ssm_c_re = nrm((DEPTH, SSM_GROUPS, SSM_GROUP, SSM_STATE), SSM_STATE ** -0.5)
    ssm_c_im = nrm((DEPTH, SSM_GROUPS, SSM_GROUP, SSM_STATE), SSM_STATE ** -0.5)
    ssm_d = nrm((DEPTH, SSM_WIDTH), 1.0)
    ssm_w_glu = nrm((DEPTH, SSM_WIDTH, 2 * SSM_WIDTH), SSM_WIDTH ** -0.5)
    ssm_b_glu = nrm((DEPTH, 2 * SSM_WIDTH), 0.01)
    diff_lq1 = nrm((DEPTH, DIFF_HD), 0.1)
    diff_lk1 = nrm((DEPTH, DIFF_HD), 0.1)
    diff_lq2 = nrm((DEPTH, DIFF_HD), 0.1)
    diff_lk2 = nrm((DEPTH, DIFF_HD), 0.1)
    diff_subln_g = 1.0 + nrm((DEPTH, 2 * DIFF_HD), 0.05)
    lru_conv_w = nrm((DEPTH, LRU_CONV, LRU_WIDTH), LRU_CONV ** -0.5)
    lru_conv_b = nrm((DEPTH, LRU_WIDTH), 0.01)
    lru_wa = nrm((DEPTH, LRU_HEADS, LRU_HD, LRU_HD), LRU_HD ** -0.5)
    lru_ba = nrm((DEPTH, LRU_WIDTH), 0.01)
    lru_wx = nrm((DEPTH, LRU_HEADS, LRU_HD, LRU_HD), LRU_HD ** -0.5)
    lru_bx = nrm((DEPTH, LRU_WIDTH), 0.01)
    a0 = jax.random.uniform(next(ks), (DEPTH, LRU_WIDTH), f32, 0.9, 0.999)
    s0 = a0 ** (1.0 / LRU_C)
    lru_lambda = jnp.log(s0) - jnp.log1p(-s0)
    w_br_ssm = nrm((DEPTH, SSM_WIDTH, D_MODEL), SSM_WIDTH ** -0.5)
    w_br_attn = nrm((DEPTH, DIFF_V, D_MODEL), DIFF_V ** -0.5)
    w_br_lru = nrm((DEPTH, LRU_WIDTH, D_MODEL), LRU_WIDTH ** -0.5)
    w_out = nrm((DEPTH, D_MODEL, D_MODEL), D_MODEL ** -0.5)
    norm_xattn_g = 1.0 + nrm((DEPTH, D_MODEL), 0.05)
    norm_mem_g = 1.0 + nrm((DEPTH, D_MODEL), 0.05)
    xattn_wq = nrm((DEPTH, D_MODEL, D_MODEL), D_MODEL ** -0.5)
    xattn_wkv = nrm((DEPTH, D_MODEL, 2 * D_MODEL), D_MODEL ** -0.5)
    xattn_wo = nrm((DEPTH, D_MODEL, D_MODEL), D_MODEL ** -0.5)
    norm_ffn_g = 1.0 + nrm((DEPTH, D_MODEL), 0.05)
    ffn_w_up = nrm((DEPTH, D_MODEL, 2 * D_FF), D_MODEL ** -0.5)
    ffn_conv_w = nrm((DEPTH, FFN_CONV, 2 * D_FF), FFN_CONV ** -0.5)
    ffn_conv_b = nrm((DEPTH, 2 * D_FF), 0.01)
    ffn_w_down = nrm((DEPTH, D_FF, D_MODEL), D_FF ** -0.5)
    final_norm_g = 1.0 + nrm((D_MODEL,), 0.05)
    return {"x": x, "mem": mem, "positions": positions, "norm_mix_g": norm_mix_g, "w_in": w_in,
            "ssm_lambda_re": ssm_lambda_re, "ssm_lambda_im": ssm_lambda_im, "ssm_log_step": ssm_log_step,
            "ssm_b_re": ssm_b_re, "ssm_b_im": ssm_b_im, "ssm_c_re": ssm_c_re, "ssm_c_im": ssm_c_im,
            "ssm_d": ssm_d, "ssm_w_glu": ssm_w_glu, "ssm_b_glu": ssm_b_glu,
            "diff_lq1": diff_lq1, "diff_lk1": diff_lk1, "diff_lq2": diff_lq2, "diff_lk2": diff_lk2,
            "diff_subln_g": diff_subln_g, "lru_conv_w": lru_conv_w, "lru_conv_b": lru_conv_b,
            "lru_wa": lru_wa, "lru_ba": lru_ba, "lru_wx": lru_wx, "lru_bx": lru_bx, "lru_lambda": lru_lambda,
            "w_br_ssm": w_br_ssm, "w_br_attn": w_br_attn, "w_br_lru": w_br_lru, "w_out": w_out,
            "norm_xattn_g": norm_xattn_g, "norm_mem_g": norm_mem_g, "xattn_wq": xattn_wq,
            "xattn_wkv": xattn_wkv, "xattn_wo": xattn_wo, "norm_ffn_g": norm_ffn_g, "ffn_w_up": ffn_w_up,
            "ffn_conv_w": ffn_conv_w, "ffn_conv_b": ffn_conv_b, "ffn_w_down": ffn_w_down,
            "final_norm_g": final_norm_g}


def reference(x, mem, positions, norm_mix_g, w_in, ssm_lambda_re, ssm_lambda_im, ssm_log_step,
              ssm_b_re, ssm_b_im, ssm_c_re, ssm_c_im, ssm_d, ssm_w_glu, ssm_b_glu,
              diff_lq1, diff_lk1, diff_lq2, diff_lk2, diff_subln_g, lru_conv_w, lru_conv_b,
              lru_wa, lru_ba, lru_wx, lru_bx, lru_lambda, w_br_ssm, w_br_attn, w_br_lru, w_out,
              norm_xattn_g, norm_mem_g, xattn_wq, xattn_wkv, xattn_wo, norm_ffn_g, ffn_w_up,
              ffn_conv_w, ffn_conv_b, ffn_w_down, final_norm_g):
    cos, sin = rope_tables(positions)
    for l in range(DEPTH):
        hn = rmsnorm(x, norm_mix_g[l])
        proj = hn @ w_in[l]
        parts = []
        start = 0
        for w in IN_WIDTHS:
            parts.append(proj[..., start:start + w])
            start += w
        u, q, k, v, xr, gr, gates = parts
        y_ssm = s5_branch(u, ssm_lambda_re[l], ssm_lambda_im[l], ssm_log_step[l], ssm_b_re[l], ssm_b_im[l],
                          ssm_c_re[l], ssm_c_im[l], ssm_d[l], ssm_w_glu[l], ssm_b_glu[l])
        lambda_init = 0.8 - 0.6 * math.exp(-0.3 * l)
        y_att = diff_attention_branch(q, k, v, cos, sin, diff_lq1[l], diff_lk1[l], diff_lq2[l], diff_lk2[l],
                                      diff_subln_g[l], lambda_init)
        y_lru = rglru_branch(xr, gr, lru_conv_w[l], lru_conv_b[l], lru_wa[l], lru_ba[l], lru_wx[l], lru_bx[l],
                             lru_lambda[l])
        g = jax.nn.sigmoid(gates)
        merged = (g[..., :D_MODEL] * (y_ssm @ w_br_ssm[l])
                  + g[..., D_MODEL:2 * D_MODEL] * (y_att @ w_br_attn[l])
                  + g[..., 2 * D_MODEL:] * (y_lru @ w_br_lru[l]))
        x = x + merged @ w_out[l]
        x = x + cross_attention(rmsnorm(x, norm_xattn_g[l]), rmsnorm(mem, norm_mem_g[l]),
                                xattn_wq[l], xattn_wkv[l], xattn_wo[l])
        x = x + conv_ffn(rmsnorm(x, norm_ffn_g[l]), ffn_w_up[l], ffn_conv_w[l], ffn_conv_b[l], ffn_w_down[l])
    return rmsnorm(x, final_norm_g)
```

```python
import math
import contextlib
import numpy as np
import concourse.bass as bass
import concourse.mybir as mybir
from concourse.bass_utils import run_bass_kernel_spmd

F32 = mybir.dt.float32
BF16 = mybir.dt.bfloat16
I32 = mybir.dt.int32
AF = mybir.ActivationFunctionType
ALU = mybir.AluOpType
AX = mybir.AxisListType

D_MODEL = 1024
SEQ = 2048
DEPTH = 4
MEM_LEN = 256
EPS = 1e-6
D_FF = 2816
NFF = 22
PI = math.pi


class View:
    __slots__ = ("t", "ap")

    def __init__(self, t, ap):
        self.t = t
        self.ap = ap

    def __getitem__(self, idx):
        return View(self.t, self.ap[idx])

    def bc(self, shape):
        return View(self.t, self.ap.to_broadcast(shape))


class T:
    __slots__ = ("ap", "name", "last_w", "readers", "lo", "hi")

    def __init__(self, ap, name="", lo=0, hi=0):
        self.ap = ap
        self.name = name
        self.last_w = None
        self.readers = []
        self.lo = lo
        self.hi = hi

    def __getitem__(self, idx):
        return View(self, self.ap[idx])

    @property
    def v(self):
        return View(self, self.ap)


def _ap(x):
    return x.ap if isinstance(x, View) else x


class Prog:
    ENGS = ("pe", "act", "dve", "pool", "sp")

    def __init__(self, nc):
        self.nc = nc
        self.streams = {e: [] for e in self.ENGS}
        self.dma_keys = {}

    def _collect(self, eng, reads, writes):
        deps = set()
        for t in reads:
            d = t.last_w
            if d is not None:
                if not (d[0] == "e" and d[1] == eng and eng == "pe"):
                    deps.add(d)
        for t in writes:
            d = t.last_w
            if d is not None and not (d[0] == "e" and d[1] == eng):
                deps.add(d)
            for d in t.readers:
                if not (d[0] == "e" and d[1] == eng):
                    deps.add(d)
        return deps

    def op(self, eng, fn, reads=(), writes=()):
        reads = [v.t for v in reads if isinstance(v, View)]
        writes = [v.t for v in writes if isinstance(v, View)]
        deps = self._collect(eng, reads, writes)
        seq = len(self.streams[eng])
        me = ("e", eng, seq)
        self.streams[eng].append({"fn": fn, "deps": deps, "dma": None})
        for t in reads:
            t.readers.append(me)
        for t in writes:
            t.last_w = me
            t.readers = []
        return me

    def dma(self, q, out, in_, key, **kw):
        reads = [in_.t] if isinstance(in_, View) else []
        writes = [out.t] if isinstance(out, View) else []
        if writes:
            key = "t_" + writes[0].name
        deps = self._collect("dma", reads, writes)
        deps = set(d for d in deps if not (d[0] == "d" and d[1] == key and writes))
        cnt = self.dma_keys.get(key, 0) + 1
        self.dma_keys[key] = cnt
        me = ("d", key, cnt * 16)
        o, i = _ap(out), _ap(in_)
        self.streams[q].append({
            "fn": (lambda e, o=o, i=i, kw=kw: e.dma_start(out=o, in_=i, **kw)),
            "deps": deps, "dma": me})
        for t in reads:
            t.readers.append(me)
        for t in writes:
            t.last_w = me
            t.readers = []
        return me

    def wait_all_dma(self, q, keys):
        deps = set(("d", k, self.dma_keys[k] * 16) for k in keys if k in self.dma_keys)
        self.streams[q].append({"fn": None, "deps": deps, "dma": None})

    def mm(self, out, lhsT, rhs, start=True, stop=True):
        return self.op("pe", lambda e: e.matmul(out.ap, lhsT.ap, rhs.ap, start=start, stop=stop),
                       reads=[lhsT, rhs], writes=[out])

    def tr(self, out, in_, ident):
        return self.op("pe", lambda e: e.transpose(out.ap, in_.ap, ident.ap), reads=[in_, ident], writes=[out])

    def act(self, out, in_, func, bias=None, scale=None, accum=None):
        kw = {}
        if bias is not None:
            kw["bias"] = _ap(bias)
        if scale is not None:
            kw["scale"] = _ap(scale)
        if accum is not None:
            kw["accum_out"] = _ap(accum)
        w = [out] + ([accum] if accum is not None else [])
        return self.op("act", lambda e: e.activation(out=out.ap, in_=in_.ap, func=func, **kw),
                       reads=[in_, bias, scale], writes=w)

    def tt(self, eng, out, in0, in1, op):
        return self.op(eng, lambda e: e.tensor_tensor(out=out.ap, in0=in0.ap, in1=in1.ap, op=op),
                       reads=[in0, in1], writes=[out])

    def ts(self, eng, out, in0, s1, op0, s2=None, op1=None):
        kw = {}
        if op1 is not None:
            kw["op1"] = op1
        return self.op(eng, lambda e: e.tensor_scalar(out=out.ap, in0=in0.ap, scalar1=_ap(s1), scalar2=_ap(s2),
                                                      op0=op0, **kw),
                       reads=[in0, s1, s2], writes=[out])

    def stt(self, eng, out, in0, scalar, in1, op0, op1):
        return self.op(eng, lambda e: e.scalar_tensor_tensor(out=out.ap, in0=in0.ap, scalar=_ap(scalar),
                                                             in1=in1.ap, op0=op0, op1=op1),
                       reads=[in0, scalar, in1], writes=[out])

    def scan(self, out, d0, d1, init, op0=ALU.mult, op1=ALU.add):
        return self.op("dve", lambda e: e.tensor_tensor_scan(out=out.ap, data0=d0.ap, data1=d1.ap,
                                                             initial=_ap(init), op0=op0, op1=op1),
                       reads=[d0, d1, init], writes=[out])

    def copy(self, eng, out, in_):
        if eng == "act":
            return self.op("act", lambda e: e.copy(out=out.ap, in_=in_.ap), reads=[in_], writes=[out])
        return self.op(eng, lambda e: e.tensor_copy(out=out.ap, in_=in_.ap), reads=[in_], writes=[out])

    def memset(self, eng, out, val):
        return self.op(eng, lambda e: e.memset(out.ap, val), writes=[out])

    def recip(self, out, in_):
        return self.op("dve", lambda e: e.reciprocal(out=out.ap, in_=in_.ap), reads=[in_], writes=[out])

    def rsum(self, out, in_):
        return self.op("dve", lambda e: e.reduce_sum(out=out.ap, in_=in_.ap, axis=AX.X), reads=[in_], writes=[out])

    def emit(self):
        nc = self.nc
        needed = {e: set() for e in self.ENGS}
        for e in self.ENGS:
            for ins in self.streams[e]:
                for d in ins["deps"]:
                    if d[0] == "e":
                        needed[d[1]].add(d[2])
        cum = {}
        for e in self.ENGS:
            c = 0
            arr = []
            nd = needed[e]
            for i in range(len(self.streams[e])):
                if i in nd:
                    c += 1
                arr.append(c)
            cum[e] = arr
        with contextlib.ExitStack() as st:
            esem = {e: st.enter_context(nc.semaphore("s_" + e)) for e in self.ENGS}
            dsem = {k: st.enter_context(nc.semaphore("d_" + str(k))) for k in self.dma_keys}
            block = st.enter_context(nc.Block())
            streams = self.streams

            def run_stream(e, eng):
                waited = {}
                nd = needed[e]
                for i, ins in enumerate(streams[e]):
                    w = {}
                    for d in ins["deps"]:
                        if d[0] == "e":
                            k = ("e", d[1])
                            v = cum[d[1]][d[2]]
                        else:
                            k = ("d", d[1])
                            v = d[2]
                        if v > w.get(k, 0):
                            w[k] = v
                    for k, v in w.items():
                        if waited.get(k, 0) >= v:
                            continue
                        waited[k] = v
                        eng.wait_ge(esem[k[1]] if k[0] == "e" else dsem[k[1]], v)
                    if ins["fn"] is None:
                        continue
                    bi = ins["fn"](eng)
                    if ins["dma"] is not None:
                        bi.then_inc(dsem[ins["dma"][1]], 16)
                    elif i in nd:
                        bi.then_inc(esem[e], 1)

            @block.tensor
            def _(eng):
                run_stream("pe", eng)

            @block.scalar
            def _(eng):
                run_stream("act", eng)

            @block.vector
            def _(eng):
                run_stream("dve", eng)

            @block.gpsimd
            def _(eng):
                run_stream("pool", eng)

            @block.sync
            def _(eng):
                run_stream("sp", eng)


class Arena:
    def __init__(self, nc, limit):
        self.nc = nc
        self.limit = limit
        self.top = 18432
        self.live = []
        self.dead = []
        self.n = 0

    def alloc(self, name, shape, dt):
        esz = 2 if dt == BF16 else 4
        nbytes = int(np.prod(shape[1:])) * esz
        nbytes = (nbytes + 31) // 32 * 32
        off = self.top
        self.top += nbytes
        assert off + nbytes <= self.limit, f"SBUF overflow at {name}: {off + nbytes}"
        self.n += 1
        h = self.nc.alloc_sbuf_tensor_at(f"{name}_{self.n}", list(shape), dt, offset=off)
        t = T(h.ap(), name, off, off + nbytes)
        keep = []
        for (lo, hi, deps) in self.dead:
            if lo < t.hi and t.lo < hi:
                t.readers.extend(deps)
                if t.lo <= lo and hi <= t.hi:
                    continue
            keep.append((lo, hi, deps))
        self.dead = keep
        self.live.append(t)
        return t

    def mark(self):
        return self.top

    def release(self, m):
        keep = []
        for t in self.live:
            if t.lo >= m:
                deps = list(t.readers)
                if t.last_w is not None:
                    deps.append(t.last_w)
                if deps:
                    self.dead.append((t.lo, t.hi, deps))
            else:
                keep.append(t)
        self.live = keep
        self.top = m


C_GMIX, C_GXA, C_GMEM, C_GFFN = 0, 8, 16, 24
C_SSMD, C_BGLU = 32, 35
C_LCW, C_LCB, C_LBA, C_LBX, C_LLAM = 41, 57, 61, 65, 69
C_FCW, C_FCB = 73, 205
C_LR, C_LI, C_LS = 249, 261, 273
NCOL = 288
K_ID, K_CM, K_T, K_INVF, K_SIGN, K_NPS = 0, 128, 256, 768, 769, 770
NCONST = 776
O_U, O_Q, O_QS, O_K, O_KS, O_V, O_XR, O_GR, O_G = 0, 384, 896, 1408, 1920, 2432, 2944, 3456, 3968
W_IN_EXT = 7040


class Builder:
    def __init__(self, nlayers=DEPTH, nseq=2, nblk=4, dbg=(), stop_after=None):
        self.L = nlayers
        self.NS = nseq
        self.NB = nblk
        self.dbg_names = dict(dbg)
        self.stop_after = stop_after
        nc = bass.Bass("TRN2", target_bir_lowering=False)
        self.nc = nc
        self.P = Prog(nc)
        L = nlayers

        def din(name, shape, dt=F32):
            return nc.dram_tensor(name, list(shape), dt, kind="ExternalInput").ap()
        self.xT = din("xT", [2, 128, 8, SEQ])
        self.memT = din("memT", [2, 128, 8, MEM_LEN])
        self.pos = din("pos", [2, SEQ], I32)
        self.consts = din("consts", [128, NCONST])
        self.cols = din("cols", [L, 128, NCOL])
        self.fng = din("fng", [128, 8])
        self.w_in = din("w_in", [L, 1024, W_IN_EXT])
        self.w_glu = din("w_glu", [L, 384, 768])
        self.w_br = din("w_br", [L, 1408, 1024])
        self.w_out = din("w_out", [L, 1024, 1024])
        self.wq = din("wq", [L, 1024, 1024])
        self.wkv = din("wkv", [L, 1024, 2048])
        self.wo = din("wo", [L, 1024, 1024])
        self.w_up = din("w_up", [L, 1024, 2 * D_FF])
        self.w_down = din("w_down", [L, D_FF, 1024])
        self.s5b = din("s5b", [L, 128, 2, 12, 16])
        self.s5c = din("s5c", [L, 128, 2, 12, 16])
        self.lruw = din("lruw", [L, 2, 8, 64, 64])
        self.dl = din("dl", [L, 4, 64])
        self.subg = din("subg", [L, 128])
        self.outT = nc.dram_tensor("outT", [2, 128, 8, SEQ], F32, kind="ExternalOutput").ap()
        self.dbg_out = {n: nc.dram_tensor("dbg_" + n, list(s), F32, kind="ExternalOutput").ap()
                        for n, s in self.dbg_names.items()}
        self.A = Arena(nc, 229376)
        self.banks = [T(nc.alloc_psum_tensor(f"psb{i}", [128, 512], F32).ap(), f"psb{i}") for i in range(8)]
        self.free_banks = list(range(8))
        self.wslot_i = 0
        self.build()
        self.P.emit()

    def bank(self):
        i = self.free_banks.pop(0)
        return self.banks[i]

    def put(self, *bs):
        for b in bs:
            self.free_banks.append(self.banks.index(b))

    def wload(self, src_ap, kc, mw):
        assert kc * mw <= 4096
        i = self.wslot_i
        self.wslot_i = (i + 1) % len(self.WS)
        slot = self.WS[i]
        v = View(slot, slot.ap[:, 0:kc * mw].rearrange("p (k m) -> p k m", k=kc))
        self.P.dma("pool", v, src_ap.rearrange("(k p) m -> p k m", p=128), f"w{i}")
        return v

    def dump(self, name, view):
        if name in self.dbg_out:
            self.P.dma("pool", self.dbg_out[name], view, "dbg")

    def build(self):
        P, A = self.P, self.A
        al = A.alloc
        self.X = al("X", [128, 8, SEQ], F32)
        self.CONST = al("CONST", [128, NCONST], F32)
        self.COLS = al("COLS", [128, NCOL], F32)
        self.FNG = al("FNG", [128, 8], F32)
        self.ONESB = al("ONESB", [128, 128], BF16)
        self.ONE1 = al("ONE1", [128, 128], BF16)
        self.CMB = al("CMB", [128, 128], BF16)
        self.WS = [al(f"WS{i}", [128, 4096], BF16) for i in range(3)]
        self.LB = al("LB", [128, 2, 12, 128], BF16)
        self.LC = al("LC", [128, 2, 12, 128], BF16)
        self.LW = al("LW", [128, 2, 4, 128], BF16)
        self.DER = al("DER", [128, 64], F32)
        self.SUBG = al("SUBG", [128, 128], F32)
        self.KX = al("KX", [128, 8, MEM_LEN], BF16)
        self.VX = al("VX", [128, 2, 1024], BF16)
        self.KC = al("KC", [128, 4, SEQ], BF16)
        self.VC = al("VC", [128, 16, 4, 130], BF16)
        self.CF = al("CF", [128, 44, 2], F32)
        self.XHALO = al("XHALO", [128, 4, 3], F32)
        self.S5ST = al("S5ST", [128, 12, 2], F32)
        self.LST = al("LST", [128, 4], F32)
        self.HN = al("HN", [128, 8, 512], BF16)
        self.YSSM = al("YSSM", [128, 3, 512], BF16)
        self.YATT = al("YATT", [128, 4, 512], BF16)
        self.YLRU = al("YLRU", [128, 4, 512], BF16)
        self.scr0 = A.mark()
        print("persistent bytes", self.scr0)

        P.dma("sp", self.CONST.v, self.consts, "c0")
        P.dma("sp", self.FNG.v, self.fng, "c0")
        P.memset("dve", self.ONESB.v, 1.0 / 1024.0)
        P.memset("dve", self.ONE1.v, 1.0)
        P.copy("dve", self.CMB.v, self.CONST[:, K_CM:K_CM + 128])
        P.memset("dve", self.VC.v, 1.0)
        P.memset("dve", self.LW.v, 0.0)
        P.memset("dve", self.LB.v, 0.0)
        P.memset("dve", self.LC.v, 0.0)

        for s in range(self.NS):
            P.dma("sp", self.X.v, self.xT[s], "x")
            for l in range(self.L):
                self.layer_setup(s, l)
                if self.stop_after == "setup":
                    break
                for b in range(self.NB):
                    self.mixer(s, l, b)
                    if self.stop_after in ("s5", "att", "lru", "merge"):
                        continue
                    self.xattn(s, l, b)
                    if self.stop_after == "xattn":
                        continue
                    self.ffn(s, l, b)
            if self.stop_after is None:
                for b in range(self.NB):
                    self.final(s, b)
            else:
                self.dump("X", self.X.v)
        P.wait_all_dma("sp", ["out", "dbg"])

    def col(self, c, n=1):
        return self.COLS[:, c:c + n]

    def rmsnorm(self, blk, gc, gt=None):
        P, A = self.P, self.A
        m = A.mark()
        SQ = [A.alloc(f"SQ{i}", [128, 512], BF16) for i in range(2)]
        RSTD = A.alloc("RSTD", [128, 512], F32)
        t0 = blk * 512
        ps = self.bank()
        for c in range(8):
            sq = SQ[c % 2]
            P.act(sq.v, self.X[:, c, t0:t0 + 512], AF.Square)
            P.mm(ps.v, self.ONESB.v, sq.v, start=(c == 0), stop=(c == 7))
        self.rstd(RSTD.v, ps.v)
        self.put(ps)
        gt = gt if gt is not None else self.COLS
        for c in range(8):
            P.stt("dve", self.HN[:, c, :], self.X[:, c, t0:t0 + 512], gt[:, gc + c:gc + c + 1], RSTD.v,
                  ALU.mult, ALU.mult)
        A.release(m)
        return RSTD

    def rstd(self, out, in_, scale=1.0):
        self.P.act(out, in_, AF.Sqrt, bias=self.EPSC, scale=scale)
        self.P.recip(out, out)

    def sincos(self, out_c, out_s, tcyc, tA, tB, sign=False):
        P = self.P
        MAGIC = 12582912.0
        P.ts("dve", tB, tcyc, MAGIC, ALU.add, MAGIC, ALU.subtract)
        P.tt("dve", tB, tcyc, tB, ALU.subtract)
        P.stt("dve", tA, tB, -1.0, tB, ALU.mult, ALU.max)
        if sign:
            P.act(out_s, tB, AF.Sin, scale=self.CONST[:, K_SIGN:K_SIGN + 1])
        else:
            P.act(out_s, tB, AF.Sin, scale=2 * PI)
        P.act(out_c, tA, AF.Sin, bias=self.HPI, scale=-2 * PI)

    def layer_setup(self, s, l):
        P, A = self.P, self.A
        m = A.mark()
        P.dma("sp", self.COLS.v, self.cols[l], "cols")
        for t in (self.S5ST, self.LST, self.XHALO, self.CF):
            P.memset("dve", t.v, 0.0)
        D = self.DER
        self.HPI = D[:, 60:61]
        self.EPSC = D[:, 61:62]
        P.memset("dve", self.HPI, PI / 2)
        P.memset("dve", self.EPSC, EPS)
        TH, RHO = D[:, 0:12], D[:, 12:24]
        self.TH, self.RHO = TH, RHO
        self.CCOL, self.C2COL, self.NLAM = D[:, 24:28], D[:, 28:32], D[:, 32:33]
        W = A.alloc("s5w", [128, 12, 12], F32)
        lr, li, ls = self.col(C_LR, 12), self.col(C_LI, 12), self.col(C_LS, 12)
        dt, a_, tmpa, tmpb = W[:, 0, :], W[:, 1, :], W[:, 2, :], W[:, 3, :]
        cs, sn, abr, abi = W[:, 4, :], W[:, 5, :], W[:, 6, :], W[:, 7, :]
        den, fr, fi, nr = W[:, 8, :], W[:, 9, :], W[:, 10, :], W[:, 11, :]
        P.act(dt, ls, AF.Exp)
        P.tt("dve", a_, lr, dt, ALU.mult)
        P.act(RHO, a_, AF.Exp)
        P.tt("dve", TH, li, dt, ALU.mult)
        P.ts("dve", TH, TH, 1.0 / (2 * PI), ALU.mult)
        self.sincos(cs, sn, TH, tmpa, tmpb)
        P.tt("dve", abr, RHO, cs, ALU.mult)
        P.tt("dve", abi, RHO, sn, ALU.mult)
        P.tt("dve", den, lr, lr, ALU.mult)
        P.tt("dve", tmpa, li, li, ALU.mult)
        P.tt("dve", den, den, tmpa, ALU.add)
        P.recip(den, den)
        P.ts("dve", nr, abr, -1.0, ALU.add)
        P.tt("dve", tmpa, nr, lr, ALU.mult)
        P.tt("dve", tmpb, abi, li, ALU.mult)
        P.tt("dve", tmpa, tmpa, tmpb, ALU.add)
        P.tt("dve", fr, tmpa, den, ALU.mult)
        P.tt("dve", tmpa, abi, lr, ALU.mult)
        P.tt("dve", tmpb, nr, li, ALU.mult)
        P.tt("dve", tmpa, tmpa, tmpb, ALU.subtract)
        P.tt("dve", fi, tmpa, den, ALU.mult)
        BC = A.alloc("s5bc", [128, 4, 12, 16], F32)
        P.dma("sp", BC[:, 0:2], self.s5b[l], "s5")
        P.dma("sp", BC[:, 2:4], self.s5c[l], "s5")
        BB = A.alloc("s5bb", [128, 2, 12, 16], F32)
        T1 = A.alloc("s5t1", [128, 12, 16], F32)
        frb, fib = fr.ap.unsqueeze(2).to_broadcast([128, 12, 16]), fi.ap.unsqueeze(2).to_broadcast([128, 12, 16])
        frb, fib = View(W, frb), View(W, fib)
        P.tt("dve", BB[:, 0], BC[:, 0], frb, ALU.mult)
        P.tt("dve", T1.v, BC[:, 1], fib, ALU.mult)
        P.tt("dve", BB[:, 0], BB[:, 0], T1.v, ALU.subtract)
        P.tt("dve", BB[:, 1], BC[:, 1], frb, ALU.mult)
        P.tt("dve", T1.v, BC[:, 0], fib, ALU.mult)
        P.tt("dve", BB[:, 1], BB[:, 1], T1.v, ALU.add)
        MP = [A.alloc(f"s5mp{i}", [128, 128], F32) for i in range(2)]
        for t in MP:
            P.memset("dve", t.v, 0.0)
        ident = self.CONST[:, K_ID:K_ID + 128]
        n = 0
        for ri in range(2):
            for pair in range(12):
                k = pair % 4
                mp = MP[n % 2]
                n += 1
                for gl in range(2):
                    P.copy("dve", mp[64 * gl:64 * gl + 64, 32 * k + 16 * gl:32 * k + 16 * gl + 16],
                           BB[64 * gl:64 * gl + 64, ri, pair, :])
                ps = self.bank()
                P.tr(ps[:, 0:128], mp.v, ident)
                P.copy("act", self.LB[:, ri, pair, :], ps[:, 0:128])
                self.put(ps)
                for gl in range(2):
                    P.memset("dve", mp[64 * gl:64 * gl + 64, 32 * k + 16 * gl:32 * k + 16 * gl + 16], 0.0)
                for gl in range(2):
                    dst = self.LC[64 * gl:64 * gl + 64, ri, pair, 32 * k + 16 * gl:32 * k + 16 * gl + 16]
                    src = BC[64 * gl:64 * gl + 64, 2 + ri, pair, :]
                    if ri == 0:
                        P.copy("act", dst, src)
                    else:
                        P.act(dst, src, AF.Identity, scale=-1.0)
        lam = self.col(C_LLAM, 4)
        P.act(self.CCOL, lam, AF.Exp, scale=-1.0)
        P.act(self.CCOL, self.CCOL, AF.Ln, bias=1.0)
        P.ts("dve", self.C2COL, self.CCOL, -16.0, ALU.mult)
        P.ts("dve", self.CCOL, self.CCOL, -8.0, ALU.mult)
        for ax in range(2):
            for c in range(4):
                for hh in range(2):
                    P.dma("pool", self.LW[64 * hh:64 * hh + 64, ax, c, 64 * hh:64 * hh + 64],
                          self.lruw[l, ax, 2 * c + hh], "lw")
        DL = A.alloc("dl", [128, 4, 64], F32)
        P.dma("sp", DL.v, self.dl[l:l + 1].partition_broadcast(128), "s5")
        PR = A.alloc("dlp", [128, 2, 64], F32)
        P.tt("dve", PR[:, 0], DL[:, 0], DL[:, 1], ALU.mult)
        P.tt("dve", PR[:, 1], DL[:, 2], DL[:, 3], ALU.mult)
        SM = D[:, 40:42]
        P.rsum(SM, PR.v)
        P.act(SM, SM, AF.Exp)
        lam_init = 0.8 - 0.6 * math.exp(-0.3 * l)
        P.tt("dve", self.NLAM, D[:, 41:42], D[:, 40:41], ALU.subtract)
        P.ts("dve", self.NLAM, self.NLAM, -lam_init, ALU.add)
        P.dma("sp", self.SUBG.v, self.subg[l:l + 1].partition_broadcast(128), "s5")
        P.ts("dve", self.SUBG.v, self.SUBG.v, 1.0 - lam_init, ALU.mult)
        self.dump("LB", self.LB[:, :, :, :])
        self.dump("DER", self.DER.v)
        A.release(m)
        m = A.mark()
        MT = A.alloc("memT", [128, 8, MEM_LEN], F32)
        MN = A.alloc("memn", [128, 8, MEM_LEN], BF16)
        SQ = [A.alloc(f"msq{i}", [128, MEM_LEN], BF16) for i in range(2)]
        RS = A.alloc("mrs", [128, MEM_LEN], F32)
        P.dma("sp", MT.v, self.memT[s], "mem")
        ps = self.bank()
        for c in range(8):
            P.act(SQ[c % 2].v, MT[:, c, :], AF.Square)
            P.mm(ps[:, 0:MEM_LEN], self.ONESB.v, SQ[c % 2].v, start=(c == 0), stop=(c == 7))
        self.rstd(RS.v, ps[:, 0:MEM_LEN])
        self.put(ps)
        for c in range(8):
            P.stt("dve", MN[:, c, :], MT[:, c, :], self.col(C_GMEM + c), RS.v, ALU.mult, ALU.mult)
        for g in range(2):
            w = self.wload(self.wkv[l][:, g * 512:(g + 1) * 512], 8, 512)
            for mi in range(4):
                ps = self.bank()
                for kc in range(8):
                    P.mm(ps[:, 0:MEM_LEN], w[:, kc, mi * 128:(mi + 1) * 128], MN[:, kc, :], start=(kc == 0), stop=(kc == 7))
                P.copy("act", self.KX[:, g * 4 + mi, :], ps[:, 0:MEM_LEN])
                self.put(ps)
        for g in range(2):
            w = self.wload(self.wkv[l][:, 1024 + g * 512:1024 + (g + 1) * 512], 8, 512)
            for kt in range(2):
                ps = self.bank()
                for kc in range(8):
                    P.mm(ps.v, MN[:, kc, kt * 128:(kt + 1) * 128], w[:, kc, :], start=(kc == 0), stop=(kc == 7))
                P.copy("act", self.VX[:, kt, g * 512:(g + 1) * 512], ps.v)
                self.put(ps)
        A.release(m)

    def mixer(self, s, l, b):
        P, A = self.P, self.A
        t0 = b * 512
        win = self.w_in[l]
        self.rmsnorm(b, C_GMIX)
        self.dump("HN", self.HN.v)
        m = A.mark()
        YG = A.alloc("YG", [128, 3, 512], BF16)
        U32 = A.alloc("U32", [128, 512], F32)
        UB = A.alloc("UB", [128, 512], BF16)
        COS = A.alloc("COS", [128, 512], F32)
        SIN = A.alloc("SIN", [128, 512], F32)
        M1 = A.alloc("M1", [128, 512], F32)
        M2 = A.alloc("M2", [128, 512], F32)
        WINR = A.alloc("WINR", [128, 512], F32)
        WINI = A.alloc("WINI", [128, 512], F32)
        WR = A.alloc("WR", [128, 512], F32)
        WI = A.alloc("WI", [128, 512], F32)
        SR = A.alloc("SR", [128, 512], BF16)
        SI = A.alloc("SI", [128, 512], BF16)
        YT = A.alloc("YT", [128, 512], F32)
        wu = self.wload(win[:, O_U:O_U + 384], 8, 384)
        t512 = self.CONST[:, K_T:K_T + 512]
        for c in range(3):
            ps = self.bank()
            for kc in range(8):
                P.mm(ps.v, wu[:, kc, c * 128:(c + 1) * 128], self.HN[:, kc, :], start=(kc == 0), stop=(kc == 7))
            P.copy("act", U32.v, ps.v)
            P.copy("act", UB.v, ps.v)
            self.put(ps)
            yacc = self.bank()
            for pr in range(4):
                pair = 4 * c + pr
                bur, bui = self.bank(), self.bank()
                P.mm(bur.v, self.LB[:, 0, pair, :], UB.v)
                P.mm(bui.v, self.LB[:, 1, pair, :], UB.v)
                P.ts("dve", M1.v, t512, float(t0), ALU.add, self.TH[:, pair:pair + 1], ALU.mult)
                self.sincos(COS.v, SIN.v, M1.v, M1.v, M2.v)
                P.tt("dve", M1.v, bur.v, COS.v, ALU.mult)
                P.tt("dve", M2.v, bui.v, SIN.v, ALU.mult)
                P.tt("dve", WINR.v, M1.v, M2.v, ALU.add)
                P.tt("dve", M1.v, bui.v, COS.v, ALU.mult)
                P.tt("dve", M2.v, bur.v, SIN.v, ALU.mult)
                P.tt("dve", WINI.v, M1.v, M2.v, ALU.subtract)
                self.put(bur, bui)
                rho = self.RHO[:, pair:pair + 1].bc([128, 512])
                P.scan(WR.v, rho, WINR.v, self.S5ST[:, pair, 0:1])
                P.scan(WI.v, rho, WINI.v, self.S5ST[:, pair, 1:2])
                P.copy("act", self.S5ST[:, pair, 0:1], WR[:, 511:512])
                P.copy("act", self.S5ST[:, pair, 1:2], WI[:, 511:512])
                P.tt("dve", M1.v, WR.v, COS.v, ALU.mult)
                P.tt("dve", M2.v, WI.v, SIN.v, ALU.mult)
                P.tt("dve", SR.v, M1.v, M2.v, ALU.subtract)
                P.tt("dve", M1.v, WR.v, SIN.v, ALU.mult)
                P.tt("dve", M2.v, WI.v, COS.v, ALU.mult)
                P.tt("dve", SI.v, M1.v, M2.v, ALU.add)
                P.mm(yacc.v, self.LC[:, 0, pair, :], SR.v, start=(pr == 0), stop=False)
                P.mm(yacc.v, self.LC[:, 1, pair, :], SI.v, start=False, stop=(pr == 3))
            P.stt("dve", YT.v, U32.v, self.col(C_SSMD + c), yacc.v, ALU.mult, ALU.add)
            self.put(yacc)
            P.act(YG[:, c, :], YT.v, AF.Gelu)
        wg = self.wload(self.w_glu[l], 3, 768)
        SG = M1
        for j in range(3):
            p1, p2 = self.bank(), self.bank()
            for kc in range(3):
                P.mm(p1.v, wg[:, kc, j * 128:(j + 1) * 128], YG[:, kc, :], start=(kc == 0), stop=(kc == 2))
            for kc in range(3):
                P.mm(p2.v, wg[:, kc, 384 + j * 128:384 + (j + 1) * 128], YG[:, kc, :], start=(kc == 0), stop=(kc == 2))
            P.act(SG.v, p2.v, AF.Sigmoid, bias=self.col(C_BGLU + 3 + j))
            P.stt("dve", self.YSSM[:, j, :], p1.v, self.col(C_BGLU + j), SG.v, ALU.add, ALU.mult)
            self.put(p1, p2)
        self.dump("YSSM", self.YSSM.v)
        A.release(m)
        if self.stop_after == "s5":
            return
        m = A.mark()
        Q = A.alloc("Q", [128, 4, 512], BF16)
        RC = A.alloc("RC", [128, 512], F32)
        RS = A.alloc("RS", [128, 512], F32)
        PI32 = A.alloc("PI32", [128, 512], I32)
        T1 = A.alloc("T1", [128, 512], F32)
        T2 = A.alloc("T2", [128, 512], F32)
        PT = [A.alloc(f"PT{i}", [128, 512], BF16) for i in range(16)]
        SMALL = A.alloc("SMALL", [128, 8], F32)
        OA = A.alloc("OA", [128, 128], F32)
        OB = A.alloc("OB", [128, 128], F32)
        ON = A.alloc("ON", [128, 128], F32)
        P.dma("sp", PI32.v, self.pos[s:s + 1, t0:t0 + 512].partition_broadcast(128), "pos")
        P.copy("dve", T1.v, PI32.v)
        P.ts("dve", T1.v, T1.v, self.CONST[:, K_INVF:K_INVF + 1], ALU.mult, 1.0 / (2 * PI), ALU.mult)
        self.sincos(RC.v, RS.v, T1.v, T1.v, T2.v, sign=True)
        for (o_a, o_s, dst) in ((O_Q, O_QS, None), (O_K, O_KS, "k")):
            wa = self.wload(win[:, o_a:o_a + 512], 8, 512)
            wsw = self.wload(win[:, o_s:o_s + 512], 8, 512)
            for j in range(4):
                pa, pb = self.bank(), self.bank()
                for kc in range(8):
                    P.mm(pa.v, wa[:, kc, j * 128:(j + 1) * 128], self.HN[:, kc, :], start=(kc == 0), stop=(kc == 7))
                for kc in range(8):
                    P.mm(pb.v, wsw[:, kc, j * 128:(j + 1) * 128], self.HN[:, kc, :], start=(kc == 0), stop=(kc == 7))
                P.tt("dve", T1.v, pa.v, RC.v, ALU.mult)
                P.tt("dve", T2.v, pb.v, RS.v, ALU.mult)
                o = Q[:, j, :] if dst is None else self.KC[:, j, t0:t0 + 512]
                P.tt("dve", o, T1.v, T2.v, ALU.add)
                self.put(pa, pb)
        wv = self.wload(win[:, O_V:O_V + 512], 8, 512)
        for i in range(4):
            ps = self.bank()
            for kc in range(8):
                P.mm(ps.v, self.HN[:, kc, i * 128:(i + 1) * 128], wv[:, kc, :], start=(kc == 0), stop=(kc == 7))
            P.copy("act", self.VC[:, 4 * b + i, :, 0:128], View(ps, ps.ap.rearrange("p (h e) -> p h e", h=4)))
            self.put(ps)
        self.dump("Q", Q.v)
        ident = self.CONST[:, K_ID:K_ID + 128]
        for h in range(4):
            ob = [self.bank() for _ in range(4)]
            for c in range(2):
                nk = 4 * (b + 1)
                for kt in range(nk):
                    i0 = max(0, kt - 4 * b)
                    qlo = 128 * i0
                    ps = self.bank()
                    P.mm(ps[:, qlo:512], self.KC[64 * c:64 * c + 64, h, 128 * kt:128 * kt + 128],
                         Q[64 * c:64 * c + 64, h, qlo:512])
                    P.act(PT[kt][:, qlo:512], ps[:, qlo:512], AF.Exp, scale=0.125)
                    self.put(ps)
                    if kt >= 4 * b:
                        P.tt("dve", PT[kt][:, qlo:qlo + 128], PT[kt][:, qlo:qlo + 128], self.CMB.v, ALU.mult)
                    for i in range(i0, 4):
                        P.mm(ob[i][:, 130 * c:130 * c + 129], PT[kt][:, 128 * i:128 * i + 128],
                             self.VC[:, kt, h, 0:129], start=(kt == 0), stop=(kt == 4 * b + i))
            tb = self.bank()
            for i in range(4):
                o = ob[i]
                P.recip(SMALL[:, 0:1], o[:, 128:129])
                P.recip(SMALL[:, 1:2], o[:, 258:259])
                P.tt("dve", SMALL[:, 2:3], SMALL[:, 1:2], self.NLAM, ALU.mult)
                P.act(OA.v, o[:, 0:128], AF.Identity, scale=SMALL[:, 0:1])
                P.stt("dve", OB.v, o[:, 130:258], SMALL[:, 2:3], OA.v, ALU.mult, ALU.add)
                P.act(OA.v, OB.v, AF.Square)
                P.rsum(SMALL[:, 3:4], OA.v)
                self.rstd(SMALL[:, 5:6], SMALL[:, 3:4], scale=1.0 / 128.0)
                P.stt("dve", ON.v, OB.v, SMALL[:, 5:6], self.SUBG.v, ALU.mult, ALU.mult)
                P.tr(tb[:, 128 * i:128 * i + 128], ON.v, ident)
            self.put(*ob)
            P.copy("act", self.YATT[:, h, :], tb.v)
            self.put(tb)
        self.dump("YATT", self.YATT.v)
        A.release(m)
        if self.stop_after == "att":
            return
        m = A.mark()
        XRH = A.alloc("XRH", [128, 515], F32)
        GG = A.alloc("GG", [128, 512], F32)
        XC = A.alloc("XC", [128, 512], F32)
        XCB = A.alloc("XCB", [128, 512], BF16)
        R = A.alloc("R", [128, 512], F32)
        IG = A.alloc("IG", [128, 512], F32)
        AA = A.alloc("AA", [128, 512], F32)
        A2 = A.alloc("A2", [128, 512], F32)
        GX = A.alloc("GX", [128, 512], F32)
        H = A.alloc("H", [128, 512], F32)
        wxr = self.wload(win[:, O_XR:O_XR + 512], 8, 512)
        wgr = self.wload(win[:, O_GR:O_GR + 512], 8, 512)
        for c in range(4):
            px, pg = self.bank(), self.bank()
            for kc in range(8):
                P.mm(px.v, wxr[:, kc, c * 128:(c + 1) * 128], self.HN[:, kc, :], start=(kc == 0), stop=(kc == 7))
            for kc in range(8):
                P.mm(pg.v, wgr[:, kc, c * 128:(c + 1) * 128], self.HN[:, kc, :], start=(kc == 0), stop=(kc == 7))
            P.copy("act", XRH[:, 0:3], self.XHALO[:, c, :])
            P.copy("act", XRH[:, 3:515], px.v)
            P.act(GG.v, pg.v, AF.Gelu)
            self.put(px, pg)
            P.copy("act", self.XHALO[:, c, :], XRH[:, 512:515])
            cw = C_LCW + 4 * c
            P.ts("dve", XC.v, XRH[:, 3:515], self.col(cw + 3), ALU.mult, self.col(C_LCB + c), ALU.add)
            for j in range(3):
                P.stt("dve", XC.v, XRH[:, j:j + 512], self.col(cw + j), XC.v, ALU.mult, ALU.add)
            P.copy("dve", XCB.v, XC.v)
            pa, pi = self.bank(), self.bank()
            P.mm(pa.v, self.LW[:, 0, c, :], XCB.v)
            P.mm(pi.v, self.LW[:, 1, c, :], XCB.v)
            P.act(R.v, pa.v, AF.Sigmoid, bias=self.col(C_LBA + c))
            P.act(IG.v, pi.v, AF.Sigmoid, bias=self.col(C_LBX + c))
            self.put(pa, pi)
            P.act(AA.v, R.v, AF.Exp, scale=self.CCOL[:, c:c + 1])
            P.act(A2.v, R.v, AF.Exp, scale=self.C2COL[:, c:c + 1])
            P.act(A2.v, A2.v, AF.Sqrt, bias=1.0, scale=-1.0)
            P.tt("dve", GX.v, IG.v, XC.v, ALU.mult)
            P.tt("dve", GX.v, GX.v, A2.v, ALU.mult)
            P.scan(H.v, AA.v, GX.v, self.LST[:, c:c + 1])
            P.copy("act", self.LST[:, c:c + 1], H[:, 511:512])
            P.tt("dve", self.YLRU[:, c, :], H.v, GG.v, ALU.mult)
        self.dump("YLRU", self.YLRU.v)
        A.release(m)
        if self.stop_after == "lru":
            return
        m = A.mark()
        MG = A.alloc("MG", [128, 8, 512], BF16)
        G = [A.alloc(f"G{i}", [128, 512], F32) for i in range(3)]
        TT = [A.alloc(f"TT{i}", [128, 512], F32) for i in range(2)]
        ys = [(self.YSSM, 3), (self.YATT, 4), (self.YLRU, 4)]
        for mo in range(8):
            wgt = self.wload(win[:, O_G + mo * 384:O_G + (mo + 1) * 384], 8, 384)
            if mo % 2 == 0:
                wbr = self.wload(self.w_br[l][:, mo * 128:(mo + 2) * 128], 11, 256)
            gps = [self.bank() for _ in range(3)]
            bps = [self.bank() for _ in range(3)]
            for i in range(3):
                for kc in range(8):
                    P.mm(gps[i].v, wgt[:, kc, i * 128:(i + 1) * 128], self.HN[:, kc, :], start=(kc == 0), stop=(kc == 7))
            k0 = 0
            for i in range(3):
                yt, nkc = ys[i]
                for kc in range(nkc):
                    P.mm(bps[i].v, wbr[:, k0 + kc, (mo % 2) * 128:(mo % 2) * 128 + 128], yt[:, kc, :],
                         start=(kc == 0), stop=(kc == nkc - 1))
                k0 += nkc
            for i in range(3):
                P.act(G[i].v, gps[i].v, AF.Sigmoid)
            P.tt("dve", TT[0].v, G[0].v, bps[0].v, ALU.mult)
            P.tt("dve", TT[1].v, G[1].v, bps[1].v, ALU.mult)
            P.tt("dve", TT[0].v, TT[0].v, TT[1].v, ALU.add)
            P.tt("dve", TT[1].v, G[2].v, bps[2].v, ALU.mult)
            P.tt("dve", MG[:, mo, :], TT[0].v, TT[1].v, ALU.add)
            self.put(*gps)
            self.put(*bps)
        self.proj_residual(self.w_out[l], MG, 8, t0)
        A.release(m)

    def proj_residual(self, w, act, nkc, t0):
        P = self.P
        if nkc * 512 <= 4096:
            groups = [(g * 512, 512) for g in range(2)]
        else:
            groups = [(g * 128, 128) for g in range(8)]
        for (m0, mw) in groups:
            wt = self.wload(w[:, m0:m0 + mw], nkc, mw)
            for mi in range(mw // 128):
                ps = self.bank()
                for kc in range(nkc):
                    P.mm(ps.v, wt[:, kc, mi * 128:(mi + 1) * 128], act[:, kc, :], start=(kc == 0), stop=(kc == nkc - 1))
                mo = m0 // 128 + mi
                P.tt("dve", self.X[:, mo, t0:t0 + 512], self.X[:, mo, t0:t0 + 512], ps.v, ALU.add)
                self.put(ps)

    def xattn(self, s, l, b):
        P, A = self.P, self.A
        t0 = b * 512
        self.rmsnorm(b, C_GXA)
        m = A.mark()
        QX = A.alloc("QX", [128, 8, 512], BF16)
        OX = A.alloc("OX", [128, 8, 512], BF16)
        PX = [A.alloc(f"PX{i}", [128, 512], BF16) for i in range(2)]
        RD = A.alloc("RD", [128, 512], F32)
        for g in range(2):
            wt = self.wload(self.wq[l][:, g * 512:(g + 1) * 512], 8, 512)
            for mi in range(4):
                ps = self.bank()
                for kc in range(8):
                    P.mm(ps.v, wt[:, kc, mi * 128:(mi + 1) * 128], self.HN[:, kc, :], start=(kc == 0), stop=(kc == 7))
                P.copy("act", QX[:, g * 4 + mi, :], ps.v)
                self.put(ps)
        for h in range(4):
            for kt in range(2):
                ps = self.bank()
                for dc in range(2):
                    P.mm(ps.v, self.KX[:, 2 * h + dc, 128 * kt:128 * kt + 128], QX[:, 2 * h + dc, :],
                         start=(dc == 0), stop=(dc == 1))
                P.act(PX[kt].v, ps.v, AF.Exp, scale=1.0 / 16.0)
                self.put(ps)
            pd = self.bank()
            for kt in range(2):
                P.mm(pd.v, self.ONE1.v, PX[kt].v, start=(kt == 0), stop=(kt == 1))
            P.recip(RD.v, pd.v)
            self.put(pd)
            for ec in range(2):
                po = self.bank()
                for kt in range(2):
                    P.mm(po.v, self.VX[:, kt, (2 * h + ec) * 128:(2 * h + ec) * 128 + 128], PX[kt].v,
                         start=(kt == 0), stop=(kt == 1))
                P.tt("dve", OX[:, 2 * h + ec, :], po.v, RD.v, ALU.mult)
                self.put(po)
        self.proj_residual(self.wo[l], OX, 8, t0)
        A.release(m)

    def ffn(self, s, l, b):
        P, A = self.P, self.A
        t0 = b * 512
        self.rmsnorm(b, C_GFFN)
        m = A.mark()
        HF = A.alloc("HF", [128, NFF, 512], BF16)
        UR = [A.alloc(f"UR{i}", [128, 514], F32) for i in range(2)]
        ACC = [A.alloc(f"ACC{i}", [128, 512], F32) for i in range(2)]
        SG = A.alloc("SGF", [128, 512], F32)
        for f in range(NFF):
            if f % 2 == 0:
                wt = self.wload(self.w_up[l][:, f * 256:(f + 2) * 256], 8, 512)
            pss = [self.bank(), self.bank()]
            for vg in range(2):
                c0 = (f % 2) * 256 + vg * 128
                for kc in range(8):
                    P.mm(pss[vg].v, wt[:, kc, c0:c0 + 128], self.HN[:, kc, :], start=(kc == 0), stop=(kc == 7))
            for vg in range(2):
                idx = 2 * f + vg
                ur, acc, ps = UR[vg], ACC[vg], pss[vg]
                cw = C_FCW + idx * 3
                P.copy("act", ur[:, 0:2], self.CF[:, idx, :])
                P.copy("act", ur[:, 2:514], ps.v)
                P.act(acc.v, ps.v, AF.Identity, bias=self.col(C_FCB + idx), scale=self.col(cw + 2))
                P.copy("act", self.CF[:, idx, :], ur[:, 512:514])
                P.stt("dve", acc.v, ur[:, 1:513], self.col(cw + 1), acc.v, ALU.mult, ALU.add)
                P.stt("dve", acc.v, ur[:, 0:512], self.col(cw + 0), acc.v, ALU.mult, ALU.add)
            self.put(*pss)
            P.act(SG.v, ACC[1].v, AF.Silu)
            P.tt("dve", HF[:, f, :], ACC[0].v, SG.v, ALU.mult)
        self.proj_residual(self.w_down[l], HF, NFF, t0)
        A.release(m)

    def final(self, s, b):
        P, A = self.P, self.A
        t0 = b * 512
        m = A.mark()
        SQ = [A.alloc(f"fSQ{i}", [128, 512], BF16) for i in range(2)]
        RSTD = A.alloc("fRSTD", [128, 512], F32)
        OT = A.alloc("OT", [128, 8, 512], F32)
        ps = self.bank()
        for c in range(8):
            P.act(SQ[c % 2].v, self.X[:, c, t0:t0 + 512], AF.Square)
            P.mm(ps.v, self.ONESB.v, SQ[c % 2].v, start=(c == 0), stop=(c == 7))
        self.rstd(RSTD.v, ps.v)
        self.put(ps)
        for c in range(8):
            P.stt("dve", OT[:, c, :], self.X[:, c, t0:t0 + 512], self.FNG[:, c:c + 1], RSTD.v, ALU.mult, ALU.mult)
        P.dma("sp", self.outT[s][:, :, t0:t0 + 512], OT.v, "out")
        A.release(m)


def _colT(v, n):
    return np.ascontiguousarray(np.asarray(v).reshape(n, 128).T)


def host_layout(inp):
    f32 = np.float32
    L = DEPTH
    g = {k: np.asarray(v) for k, v in inp.items()}
    consts = np.zeros((128, NCONST), f32)
    consts[:, K_ID:K_ID + 128] = np.eye(128, dtype=f32)
    kk, qq = np.meshgrid(np.arange(128), np.arange(128), indexing="ij")
    consts[:, K_CM:K_CM + 128] = (kk <= qq).astype(f32)
    consts[:, K_T:K_T + 512] = np.arange(512, dtype=f32)[None, :]
    inv = np.exp(np.float32(-math.log(10000.0)) * np.arange(0, 64, 2, dtype=f32) / np.float32(64)).astype(f32)
    rows = np.arange(128)
    consts[:, K_INVF] = inv[rows % 32]
    sign = np.where((rows % 64) < 32, -1.0, 1.0).astype(f32)
    consts[:, K_SIGN] = np.float32(2 * PI) * sign
    consts[:, K_NPS] = (-np.float32(PI)) * sign
    cols = np.zeros((L, 128, NCOL), f32)
    for l in range(L):
        cols[l, :, C_GMIX:C_GMIX + 8] = _colT(g["norm_mix_g"][l], 8)
        cols[l, :, C_GXA:C_GXA + 8] = _colT(g["norm_xattn_g"][l], 8)
        cols[l, :, C_GMEM:C_GMEM + 8] = _colT(g["norm_mem_g"][l], 8)
        cols[l, :, C_GFFN:C_GFFN + 8] = _colT(g["norm_ffn_g"][l], 8)
        cols[l, :, C_SSMD:C_SSMD + 3] = _colT(g["ssm_d"][l], 3)
        cols[l, :, C_BGLU:C_BGLU + 6] = _colT(g["ssm_b_glu"][l], 6)
        cw = g["lru_conv_w"][l]
        cols[l, :, C_LCW:C_LCW + 16] = cw.reshape(4, 4, 128).transpose(2, 1, 0).reshape(128, 16)
        cols[l, :, C_LCB:C_LCB + 4] = _colT(g["lru_conv_b"][l], 4)
        cols[l, :, C_LBA:C_LBA + 4] = _colT(g["lru_ba"][l], 4)
        cols[l, :, C_LBX:C_LBX + 4] = _colT(g["lru_bx"][l], 4)
        cols[l, :, C_LLAM:C_LLAM + 4] = _colT(g["lru_lambda"][l], 4)
        fw = g["ffn_conv_w"][l].reshape(3, 2, NFF, 128)
        cols[l, :, C_FCW:C_FCW + 132] = fw.transpose(3, 2, 1, 0).reshape(128, 132)
        fb = g["ffn_conv_b"][l].reshape(2, NFF, 128)
        cols[l, :, C_FCB:C_FCB + 44] = fb.transpose(2, 1, 0).reshape(128, 44)
        cols[l, :, C_LR:C_LR + 12] = g["ssm_lambda_re"][l].reshape(12, 2, 64).transpose(1, 2, 0).reshape(128, 12)
        cols[l, :, C_LI:C_LI + 12] = g["ssm_lambda_im"][l].reshape(12, 2, 64).transpose(1, 2, 0).reshape(128, 12)
        cols[l, :, C_LS:C_LS + 12] = np.repeat(g["ssm_log_step"][l].reshape(12, 2, 1), 64, axis=2).transpose(1, 2, 0).reshape(128, 12)
    w = g["w_in"]
    offs = np.cumsum([0, 384, 512, 512, 512, 512, 512, 3072])
    u, q, k, v, xr, gr, gates = [w[:, :, offs[i]:offs[i + 1]] for i in range(7)]

    def swap_halves(a):
        s = a.shape
        a = a.reshape(s[0], s[1], 8, 2, 32)
        return a[:, :, :, ::-1, :].reshape(s)
    gates_re = gates.reshape(L, 1024, 3, 8, 128).transpose(0, 1, 3, 2, 4).reshape(L, 1024, 3072)
    w_in_ext = np.concatenate([u, q, swap_halves(q), k, swap_halves(k), v, xr, gr, gates_re], axis=2)
    w_br = np.concatenate([g["w_br_ssm"], g["w_br_attn"], g["w_br_lru"]], axis=1)
    w_up = g["ffn_w_up"].reshape(L, 1024, 2, NFF, 128).transpose(0, 1, 3, 2, 4).reshape(L, 1024, 2 * D_FF)

    def s5lay(re, im, bside):
        out = np.zeros((L, 128, 2, 12, 16), f32)
        for i, a in enumerate((re, im)):
            if bside:
                out[:, :, i] = a.reshape(L, 12, 2, 64, 16).transpose(0, 2, 3, 1, 4).reshape(L, 128, 12, 16)
            else:
                out[:, :, i] = a.reshape(L, 12, 2, 16, 64).transpose(0, 2, 4, 1, 3).reshape(L, 128, 12, 16)
        return out
    shared = {
        "consts": consts, "cols": cols, "fng": _colT(g["final_norm_g"], 8).astype(f32),
        "w_in": np.ascontiguousarray(w_in_ext), "w_glu": g["ssm_w_glu"], "w_br": np.ascontiguousarray(w_br),
        "w_out": g["w_out"], "wq": g["xattn_wq"], "wkv": g["xattn_wkv"], "wo": g["xattn_wo"],
        "w_up": np.ascontiguousarray(w_up), "w_down": g["ffn_w_down"],
        "s5b": s5lay(g["ssm_b_re"], g["ssm_b_im"], True), "s5c": s5lay(g["ssm_c_re"], g["ssm_c_im"], False),
        "lruw": np.ascontiguousarray(np.stack([g["lru_wa"], g["lru_wx"]], axis=1)),
        "dl": np.ascontiguousarray(np.stack([g["diff_lq1"], g["diff_lk1"], g["diff_lq2"], g["diff_lk2"]], axis=1)),
        "subg": g["diff_subln_g"],
    }
    shared = {k: np.ascontiguousarray(v.astype(f32)) for k, v in shared.items()}
    x, mem, pos = g["x"], g["mem"], g["positions"]
    in_maps = []
    for c in range(8):
        xs = x[2 * c:2 * c + 2]
        ms = mem[2 * c:2 * c + 2]
        d = dict(shared)
        d["xT"] = np.ascontiguousarray(xs.reshape(2, SEQ, 8, 128).transpose(0, 3, 2, 1)).astype(f32)
        d["memT"] = np.ascontiguousarray(ms.reshape(2, MEM_LEN, 8, 128).transpose(0, 3, 2, 1)).astype(f32)
        d["pos"] = np.ascontiguousarray(pos[2 * c:2 * c + 2]).astype(np.int32)
        in_maps.append(d)
    return in_maps


_CACHE = {}


def kernel(**inputs):
    in_maps = host_layout(inputs)
    if "nc" not in _CACHE:
        _CACHE["nc"] = Builder().nc
    nc = _CACHE["nc"]
    res = run_bass_kernel_spmd(nc, in_maps, core_ids=list(range(8)))
    out = np.zeros((16, SEQ, D_MODEL), np.float32)
    for c in range(8):
        o = np.asarray(res.results[c]["outT"])
        out[2 * c:2 * c + 2] = o.transpose(0, 3, 2, 1).reshape(2, SEQ, D_MODEL)
    return out
```

```python
import math
import contextlib
import numpy as np
import concourse.bass as bass
import concourse.mybir as mybir
from concourse.bass_utils import run_bass_kernel_spmd

F32 = mybir.dt.float32
BF16 = mybir.dt.bfloat16
I32 = mybir.dt.int32
AF = mybir.ActivationFunctionType
ALU = mybir.AluOpType
AX = mybir.AxisListType

D_MODEL = 1024
SEQ = 2048
DEPTH = 4
MEM_LEN = 256
EPS = 1e-6
D_FF = 2816
NFF = 22
PI = math.pi


class View:
    __slots__ = ("t", "ap")

    def __init__(self, t, ap):
        self.t = t
        self.ap = ap

    def __getitem__(self, idx):
        return View(self.t, self.ap[idx])

    def bc(self, shape):
        return View(self.t, self.ap.to_broadcast(shape))


class T:
    __slots__ = ("ap", "name", "last_w", "readers", "lo", "hi")

    def __init__(self, ap, name="", lo=0, hi=0):
        self.ap = ap
        self.name = name
        self.last_w = None
        self.readers = []
        self.lo = lo
        self.hi = hi

    def __getitem__(self, idx):
        return View(self, self.ap[idx])

    @property
    def v(self):
        return View(self, self.ap)


def _ap(x):
    return x.ap if isinstance(x, View) else x


class Prog:
    ENGS = ("pe", "act", "dve", "pool", "sp")

    def __init__(self, nc):
        self.nc = nc
        self.streams = {e: [] for e in self.ENGS}
        self.dma_keys = {}

    def _collect(self, eng, reads, writes):
        deps = set()
        for t in reads:
            d = t.last_w
            if d is not None:
                if not (d[0] == "e" and d[1] == eng and eng == "pe"):
                    deps.add(d)
        for t in writes:
            d = t.last_w
            if d is not None and not (d[0] == "e" and d[1] == eng):
                deps.add(d)
            for d in t.readers:
                if not (d[0] == "e" and d[1] == eng):
                    deps.add(d)
        return deps

    def op(self, eng, fn, reads=(), writes=()):
        reads = [v.t for v in reads if isinstance(v, View)]
        writes = [v.t for v in writes if isinstance(v, View)]
        deps = self._collect(eng, reads, writes)
        seq = len(self.streams[eng])
        me = ("e", eng, seq)
        self.streams[eng].append({"fn": fn, "deps": deps, "dma": None})
        for t in reads:
            t.readers.append(me)
        for t in writes:
            t.last_w = me
            t.readers = []
        return me

    def dma(self, q, out, in_, key, **kw):
        reads = [in_.t] if isinstance(in_, View) else []
        writes = [out.t] if isinstance(out, View) else []
        if writes:
            key = "t_" + writes[0].name
        deps = self._collect("dma", reads, writes)
        deps = set(d for d in deps if not (d[0] == "d" and d[1] == key and writes))
        cnt = self.dma_keys.get(key, 0) + 1
        self.dma_keys[key] = cnt
        me = ("d", key, cnt * 16)
        o, i = _ap(out), _ap(in_)
        self.streams[q].append({
            "fn": (lambda e, o=o, i=i, kw=kw: e.dma_start(out=o, in_=i, **kw)),
            "deps": deps, "dma": me})
        for t in reads:
            t.readers.append(me)
        for t in writes:
            t.last_w = me
            t.readers = []
        return me

    def wait_all_dma(self, q, keys):
        deps = set(("d", k, self.dma_keys[k] * 16) for k in keys if k in self.dma_keys)
        self.streams[q].append({"fn": None, "deps": deps, "dma": None})

    def mm(self, out, lhsT, rhs, start=True, stop=True):
        return self.op("pe", lambda e: e.matmul(out.ap, lhsT.ap, rhs.ap, start=start, stop=stop),
                       reads=[lhsT, rhs], writes=[out])

    def tr(self, out, in_, ident):
        return self.op("pe", lambda e: e.transpose(out.ap, in_.ap, ident.ap), reads=[in_, ident], writes=[out])

    def act(self, out, in_, func, bias=None, scale=None, accum=None):
        kw = {}
        if bias is not None:
            kw["bias"] = _ap(bias)
        if scale is not None:
            kw["scale"] = _ap(scale)
        if accum is not None:
            kw["accum_out"] = _ap(accum)
        w = [out] + ([accum] if accum is not None else [])
        return self.op("act", lambda e: e.activation(out=out.ap, in_=in_.ap, func=func, **kw),
                       reads=[in_, bias, scale], writes=w)

    def tt(self, eng, out, in0, in1, op):
        return self.op(eng, lambda e: e.tensor_tensor(out=out.ap, in0=in0.ap, in1=in1.ap, op=op),
                       reads=[in0, in1], writes=[out])

    def ts(self, eng, out, in0, s1, op0, s2=None, op1=None):
        kw = {}
        if op1 is not None:
            kw["op1"] = op1
        return self.op(eng, lambda e: e.tensor_scalar(out=out.ap, in0=in0.ap, scalar1=_ap(s1), scalar2=_ap(s2),
                                                      op0=op0, **kw),
                       reads=[in0, s1, s2], writes=[out])

    def stt(self, eng, out, in0, scalar, in1, op0, op1):
        return self.op(eng, lambda e: e.scalar_tensor_tensor(out=out.ap, in0=in0.ap, scalar=_ap(scalar),
                                                             in1=in1.ap, op0=op0, op1=op1),
                       reads=[in0, scalar, in1], writes=[out])

    def scan(self, out, d0, d1, init, op0=ALU.mult, op1=ALU.add):
        return self.op("dve", lambda e: e.tensor_tensor_scan(out=out.ap, data0=d0.ap, data1=d1.ap,
                                                             initial=_ap(init), op0=op0, op1=op1),
                       reads=[d0, d1, init], writes=[out])

    def copy(self, eng, out, in_):
        if eng == "act":
            return self.op("act", lambda e: e.copy(out=out.ap, in_=in_.ap), reads=[in_], writes=[out])
        return self.op(eng, lambda e: e.tensor_copy(out=out.ap, in_=in_.ap), reads=[in_], writes=[out])

    def memset(self, eng, out, val):
        return self.op(eng, lambda e: e.memset(out.ap, val), writes=[out])

    def recip(self, out, in_):
        return self.op("dve", lambda e: e.reciprocal(out=out.ap, in_=in_.ap), reads=[in_], writes=[out])

    def rsum(self, out, in_):
        return self.op("dve", lambda e: e.reduce_sum(out=out.ap, in_=in_.ap, axis=AX.X), reads=[in_], writes=[out])

    def emit(self):
        nc = self.nc
        needed = {e: set() for e in self.ENGS}
        for e in self.ENGS:
            for ins in self.streams[e]:
                for d in ins["deps"]:
                    if d[0] == "e":
                        needed[d[1]].add(d[2])
        cum = {}
        for e in self.ENGS:
            c = 0
            arr = []
            nd = needed[e]
            for i in range(len(self.streams[e])):
                if i in nd:
                    c += 1
                arr.append(c)
            cum[e] = arr
        with contextlib.ExitStack() as st:
            esem = {e: st.enter_context(nc.semaphore("s_" + e)) for e in self.ENGS}
            dsem = {k: st.enter_context(nc.semaphore("d_" + str(k))) for k in self.dma_keys}
            block = st.enter_context(nc.Block())
            streams = self.streams

            def run_stream(e, eng):
                waited = {}
                nd = needed[e]
                for i, ins in enumerate(streams[e]):
                    w = {}
                    for d in ins["deps"]:
                        if d[0] == "e":
                            k = ("e", d[1])
                            v = cum[d[1]][d[2]]
                        else:
                            k = ("d", d[1])
                            v = d[2]
                        if v > w.get(k, 0):
                            w[k] = v
                    for k, v in w.items():
                        if waited.get(k, 0) >= v:
                            continue
                        waited[k] = v
                        eng.wait_ge(esem[k[1]] if k[0] == "e" else dsem[k[1]], v)
                    if ins["fn"] is None:
                        continue
                    bi = ins["fn"](eng)
                    if ins["dma"] is not None:
                        bi.then_inc(dsem[ins["dma"][1]], 16)
                    elif i in nd:
                        bi.then_inc(esem[e], 1)

            @block.tensor
            def _(eng):
                run_stream("pe", eng)

            @block.scalar
            def _(eng):
                run_stream("act", eng)

            @block.vector
            def _(eng):
                run_stream("dve", eng)

            @block.gpsimd
            def _(eng):
                run_stream("pool", eng)

            @block.sync
            def _(eng):
                run_stream("sp", eng)


class Arena:
    def __init__(self, nc, limit):
        self.nc = nc
        self.limit = limit
        self.top = 18432
        self.live = []
        self.dead = []
        self.n = 0

    def alloc(self, name, shape, dt):
        esz = 2 if dt == BF16 else 4
        nbytes = int(np.prod(shape[1:])) * esz
        nbytes = (nbytes + 31) // 32 * 32
        off = self.top
        self.top += nbytes
        assert off + nbytes <= self.limit, f"SBUF overflow at {name}: {off + nbytes}"
        self.n += 1
        h = self.nc.alloc_sbuf_tensor_at(f"{name}_{self.n}", list(shape), dt, offset=off)
        t = T(h.ap(), name, off, off + nbytes)
        keep = []
        for (lo, hi, deps) in self.dead:
            if lo < t.hi and t.lo < hi:
                t.readers.extend(deps)
                if t.lo <= lo and hi <= t.hi:
                    continue
            keep.append((lo, hi, deps))
        self.dead = keep
        self.live.append(t)
        return t

    def mark(self):
        return self.top

    def release(self, m):
        keep = []
        for t in self.live:
            if t.lo >= m:
                deps = list(t.readers)
                if t.last_w is not None:
                    deps.append(t.last_w)
                if deps:
                    self.dead.append((t.lo, t.hi, deps))
            else:
                keep.append(t)
        self.live = keep
        self.top = m


C_GMIX, C_GXA, C_GMEM, C_GFFN = 0, 8, 16, 24
C_SSMD, C_BGLU = 32, 35
C_LCW, C_LCB, C_LBA, C_LBX, C_LLAM = 41, 57, 61, 65, 69
C_FCW, C_FCB = 73, 205
C_LR, C_LI, C_LS = 249, 261, 273
NCOL = 288
K_ID, K_CM, K_T, K_INVF, K_SIGN, K_NPS = 0, 128, 256, 768, 769, 770
NCONST = 776
O_U, O_Q, O_QS, O_K, O_KS, O_V, O_XR, O_GR, O_G = 0, 384, 896, 1408, 1920, 2432, 2944, 3456, 3968
W_IN_EXT = 7040


class Builder:
    def __init__(self, nlayers=DEPTH, nseq=2, nblk=4, dbg=(), stop_after=None):
        self.L = nlayers
        self.NS = nseq
        self.NB = nblk
        self.dbg_names = dict(dbg)
        self.stop_after = stop_after
        nc = bass.Bass("TRN2", target_bir_lowering=False)
        self.nc = nc
        self.P = Prog(nc)
        L = nlayers

        def din(name, shape, dt=F32):
            return nc.dram_tensor(name, list(shape), dt, kind="ExternalInput").ap()
        self.xT = din("xT", [2, 128, 8, SEQ])
        self.memT = din("memT", [2, 128, 8, MEM_LEN])
        self.pos = din("pos", [2, SEQ], I32)
        self.consts = din("consts", [128, NCONST])
        self.cols = din("cols", [L, 128, NCOL])
        self.fng = din("fng", [128, 8])
        self.w_in = din("w_in", [L, 1024, W_IN_EXT])
        self.w_glu = din("w_glu", [L, 384, 768])
        self.w_br = din("w_br", [L, 1408, 1024])
        self.w_out = din("w_out", [L, 1024, 1024])
        self.wq = din("wq", [L, 1024, 1024])
        self.wkv = din("wkv", [L, 1024, 2048])
        self.wo = din("wo", [L, 1024, 1024])
        self.w_up = din("w_up", [L, 1024, 2 * D_FF])
        self.w_down = din("w_down", [L, D_FF, 1024])
        self.s5b = din("s5b", [L, 128, 2, 12, 16])
        self.s5c = din("s5c", [L, 128, 2, 12, 16])
        self.lruw = din("lruw", [L, 2, 8, 64, 64])
        self.dl = din("dl", [L, 4, 64])
        self.subg = din("subg", [L, 128])
        self.outT = nc.dram_tensor("outT", [2, 128, 8, SEQ], F32, kind="ExternalOutput").ap()
        self.dbg_out = {n: nc.dram_tensor("dbg_" + n, list(s), F32, kind="ExternalOutput").ap()
                        for n, s in self.dbg_names.items()}
        self.A = Arena(nc, 229376)
        self.banks = [T(nc.alloc_psum_tensor(f"psb{i}", [128, 512], F32).ap(), f"psb{i}") for i in range(8)]
        self.free_banks = list(range(8))
        self.wslot_i = 0
        self.build()
        self.P.emit()

    def bank(self):
        i = self.free_banks.pop(0)
        return self.banks[i]

    def put(self, *bs):
        for b in bs:
            self.free_banks.append(self.banks.index(b))

    def wload(self, src_ap, kc, mw):
        assert kc * mw <= 4096
        i = self.wslot_i
        self.wslot_i = (i + 1) % len(self.WS)
        slot = self.WS[i]
        v = View(slot, slot.ap[:, 0:kc * mw].rearrange("p (k m) -> p k m", k=kc))
        self.P.dma("pool", v, src_ap.rearrange("(k p) m -> p k m", p=128), f"w{i}")
        return v

    def dump(self, name, view):
        if name in self.dbg_out:
            self.P.dma("pool", self.dbg_out[name], view, "dbg")

    def build(self):
        P, A = self.P, self.A
        al = A.alloc
        self.X = al("X", [128, 8, SEQ], F32)
        self.CONST = al("CONST", [128, NCONST], F32)
        self.COLS = al("COLS", [128, NCOL], F32)
        self.FNG = al("FNG", [128, 8], F32)
        self.ONESB = al("ONESB", [128, 128], BF16)
        self.ONE1 = al("ONE1", [128, 128], BF16)
        self.CMB = al("CMB", [128, 128], BF16)
        self.WS = [al(f"WS{i}", [128, 4096], BF16) for i in range(3)]
        self.LB = al("LB", [128, 2, 12, 128], BF16)
        self.LC = al("LC", [128, 2, 12, 128], BF16)
        self.LW = al("LW", [128, 2, 4, 128], BF16)
        self.DER = al("DER", [128, 64], F32)
        self.SUBG = al("SUBG", [128, 128], F32)
        self.KX = al("KX", [128, 8, MEM_LEN], BF16)
        self.VX = al("VX", [128, 2, 1024], BF16)
        self.KC = al("KC", [128, 4, SEQ], BF16)
        self.VC = al("VC", [128, 16, 4, 130], BF16)
        self.CF = al("CF", [128, 44, 2], F32)
        self.XHALO = al("XHALO", [128, 4, 3], F32)
        self.S5ST = al("S5ST", [128, 12, 2], F32)
        self.LST = al("LST", [128, 4], F32)
        self.HN = al("HN", [128, 8, 512], BF16)
        self.YSSM = al("YSSM", [128, 3, 512], BF16)
        self.YATT = al("YATT", [128, 4, 512], BF16)
        self.YLRU = al("YLRU", [128, 4, 512], BF16)
        self.scr0 = A.mark()
        print("persistent bytes", self.scr0)

        P.dma("sp", self.CONST.v, self.consts, "c0")
        P.dma("sp", self.FNG.v, self.fng, "c0")
        P.memset("dve", self.ONESB.v, 1.0 / 1024.0)
        P.memset("dve", self.ONE1.v, 1.0)
        P.copy("dve", self.CMB.v, self.CONST[:, K_CM:K_CM + 128])
        P.memset("dve", self.VC.v, 1.0)
        P.memset("dve", self.LW.v, 0.0)
        P.memset("dve", self.LB.v, 0.0)
        P.memset("dve", self.LC.v, 0.0)

        for s in range(self.NS):
            P.dma("sp", self.X.v, self.xT[s], "x")
            for l in range(self.L):
                self.layer_setup(s, l)
                if self.stop_after == "setup":
                    break
                for b in range(self.NB):
                    self.mixer(s, l, b)
                    if self.stop_after in ("s5", "att", "lru", "merge"):
                        continue
                    self.xattn(s, l, b)
                    if self.stop_after == "xattn":
                        continue
                    self.ffn(s, l, b)
            if self.stop_after is None:
                for b in range(self.NB):
                    self.final(s, b)
            else:
                self.dump("X", self.X.v)
        P.wait_all_dma("sp", ["out", "dbg"])

    def col(self, c, n=1):
        return self.COLS[:, c:c + n]

    def rmsnorm(self, blk, gc, gt=None):
        P, A = self.P, self.A
        m = A.mark()
        SQ = [A.alloc(f"SQ{i}", [128, 512], BF16) for i in range(2)]
        RSTD = A.alloc("RSTD", [128, 512], F32)
        t0 = blk * 512
        ps = self.bank()
        for c in range(8):
            sq = SQ[c % 2]
            P.act(sq.v, self.X[:, c, t0:t0 + 512], AF.Square)
            P.mm(ps.v, self.ONESB.v, sq.v, start=(c == 0), stop=(c == 7))
        self.rstd(RSTD.v, ps.v)
        self.put(ps)
        gt = gt if gt is not None else self.COLS
        for c in range(8):
            P.stt("dve", self.HN[:, c, :], self.X[:, c, t0:t0 + 512], gt[:, gc + c:gc + c + 1], RSTD.v,
                  ALU.mult, ALU.mult)
        A.release(m)
        return RSTD

    def rstd(self, out, in_, scale=1.0):
        self.P.act(out, in_, AF.Sqrt, bias=self.EPSC, scale=scale)
        self.P.recip(out, out)

    def sincos(self, out_c, out_s, tcyc, tA, tB, sign=False):
        P = self.P
        MAGIC = 12582912.0
        P.ts("dve", tB, tcyc, MAGIC, ALU.add, MAGIC, ALU.subtract)
        P.tt("dve", tB, tcyc, tB, ALU.subtract)
        P.stt("dve", tA, tB, -1.0, tB, ALU.mult, ALU.max)
        if sign:
            P.act(out_s, tB, AF.Sin, scale=self.CONST[:, K_SIGN:K_SIGN + 1])
        else:
            P.act(out_s, tB, AF.Sin, scale=2 * PI)
        P.act(out_c, tA, AF.Sin, bias=self.HPI, scale=-2 * PI)

    def layer_setup(self, s, l):
        P, A = self.P, self.A
        m = A.mark()
        P.dma("sp", self.COLS.v, self.cols[l], "cols")
        for t in (self.S5ST, self.LST, self.XHALO, self.CF):
            P.memset("dve", t.v, 0.0)
        D = self.DER
        self.HPI = D[:, 60:61]
        self.EPSC = D[:, 61:62]
        P.memset("dve", self.HPI, PI / 2)
        P.memset("dve", self.EPSC, EPS)
        TH, RHO = D[:, 0:12], D[:, 12:24]
        self.TH, self.RHO = TH, RHO
        self.CCOL, self.C2COL, self.NLAM = D[:, 24:28], D[:, 28:32], D[:, 32:33]
        W = A.alloc("s5w", [128, 12, 12], F32)
        lr, li, ls = self.col(C_LR, 12), self.col(C_LI, 12), self.col(C_LS, 12)
        dt, a_, tmpa, tmpb = W[:, 0, :], W[:, 1, :], W[:, 2, :], W[:, 3, :]
        cs, sn, abr, abi = W[:, 4, :], W[:, 5, :], W[:, 6, :], W[:, 7, :]
        den, fr, fi, nr = W[:, 8, :], W[:, 9, :], W[:, 10, :], W[:, 11, :]
        P.act(dt, ls, AF.Exp)
        P.tt("dve", a_, lr, dt, ALU.mult)
        P.act(RHO, a_, AF.Exp)
        P.tt("dve", TH, li, dt, ALU.mult)
        P.ts("dve", TH, TH, 1.0 / (2 * PI), ALU.mult)
        self.sincos(cs, sn, TH, tmpa, tmpb)
        P.tt("dve", abr, RHO, cs, ALU.mult)
        P.tt("dve", abi, RHO, sn, ALU.mult)
        P.tt("dve", den, lr, lr, ALU.mult)
        P.tt("dve", tmpa, li, li, ALU.mult)
        P.tt("dve", den, den, tmpa, ALU.add)
        P.recip(den, den)
        P.ts("dve", nr, abr, -1.0, ALU.add)
        P.tt("dve", tmpa, nr, lr, ALU.mult)
        P.tt("dve", tmpb, abi, li, ALU.mult)
        P.tt("dve", tmpa, tmpa, tmpb, ALU.add)
        P.tt("dve", fr, tmpa, den, ALU.mult)
        P.tt("dve", tmpa, abi, lr, ALU.mult)
        P.tt("dve", tmpb, nr, li, ALU.mult)
        P.tt("dve", tmpa, tmpa, tmpb, ALU.subtract)
        P.tt("dve", fi, tmpa, den, ALU.mult)
        BC = A.alloc("s5bc", [128, 4, 12, 16], F32)
        P.dma("sp", BC[:, 0:2], self.s5b[l], "s5")
        P.dma("sp", BC[:, 2:4], self.s5c[l], "s5")
        BB = A.alloc("s5bb", [128, 2, 12, 16], F32)
        T1 = A.alloc("s5t1", [128, 12, 16], F32)
        frb, fib = fr.ap.unsqueeze(2).to_broadcast([128, 12, 16]), fi.ap.unsqueeze(2).to_broadcast([128, 12, 16])
        frb, fib = View(W, frb), View(W, fib)
        P.tt("dve", BB[:, 0], BC[:, 0], frb, ALU.mult)
        P.tt("dve", T1.v, BC[:, 1], fib, ALU.mult)
        P.tt("dve", BB[:, 0], BB[:, 0], T1.v, ALU.subtract)
        P.tt("dve", BB[:, 1], BC[:, 1], frb, ALU.mult)
        P.tt("dve", T1.v, BC[:, 0], fib, ALU.mult)
        P.tt("dve", BB[:, 1], BB[:, 1], T1.v, ALU.add)
        MP = [A.alloc(f"s5mp{i}", [128, 128], F32) for i in range(2)]
        for t in MP:
            P.memset("dve", t.v, 0.0)
        ident = self.CONST[:, K_ID:K_ID + 128]
        n = 0
        for ri in range(2):
            for pair in range(12):
                k = pair % 4
                mp = MP[n % 2]
                n += 1
                for gl in range(2):
                    P.copy("dve", mp[64 * gl:64 * gl + 64, 32 * k + 16 * gl:32 * k + 16 * gl + 16],
                           BB[64 * gl:64 * gl + 64, ri, pair, :])
                ps = self.bank()
                P.tr(ps[:, 0:128], mp.v, ident)
                P.copy("act", self.LB[:, ri, pair, :], ps[:, 0:128])
                self.put(ps)
                for gl in range(2):
                    P.memset("dve", mp[64 * gl:64 * gl + 64, 32 * k + 16 * gl:32 * k + 16 * gl + 16], 0.0)
                for gl in range(2):
                    dst = self.LC[64 * gl:64 * gl + 64, ri, pair, 32 * k + 16 * gl:32 * k + 16 * gl + 16]
                    src = BC[64 * gl:64 * gl + 64, 2 + ri, pair, :]
                    if ri == 0:
                        P.copy("act", dst, src)
                    else:
                        P.act(dst, src, AF.Identity, scale=-1.0)
        lam = self.col(C_LLAM, 4)
        P.act(self.CCOL, lam, AF.Exp, scale=-1.0)
        P.act(self.CCOL, self.CCOL, AF.Ln, bias=1.0)
        P.ts("dve", self.C2COL, self.CCOL, -16.0, ALU.mult)
        P.ts("dve", self.CCOL, self.CCOL, -8.0, ALU.mult)
        for ax in range(2):
            for c in range(4):
                for hh in range(2):
                    P.dma("pool", self.LW[64 * hh:64 * hh + 64, ax, c, 64 * hh:64 * hh + 64],
                          self.lruw[l, ax, 2 * c + hh], "lw")
        DL = A.alloc("dl", [128, 4, 64], F32)
        P.dma("sp", DL.v, self.dl[l:l + 1].partition_broadcast(128), "s5")
        PR = A.alloc("dlp", [128, 2, 64], F32)
        P.tt("dve", PR[:, 0], DL[:, 0], DL[:, 1], ALU.mult)
        P.tt("dve", PR[:, 1], DL[:, 2], DL[:, 3], ALU.mult)
        SM = D[:, 40:42]
        P.rsum(SM, PR.v)
        P.act(SM, SM, AF.Exp)
        lam_init = 0.8 - 0.6 * math.exp(-0.3 * l)
        P.tt("dve", self.NLAM, D[:, 41:42], D[:, 40:41], ALU.subtract)
        P.ts("dve", self.NLAM, self.NLAM, -lam_init, ALU.add)
        P.dma("sp", self.SUBG.v, self.subg[l:l + 1].partition_broadcast(128), "s5")
        P.ts("dve", self.SUBG.v, self.SUBG.v, 1.0 - lam_init, ALU.mult)
        self.dump("LB", self.LB[:, :, :, :])
        self.dump("DER", self.DER.v)
        A.release(m)
        m = A.mark()
        MT = A.alloc("memT", [128, 8, MEM_LEN], F32)
        MN = A.alloc("memn", [128, 8, MEM_LEN], BF16)
        SQ = [A.alloc(f"msq{i}", [128, MEM_LEN], BF16) for i in range(2)]
        RS = A.alloc("mrs", [128, MEM_LEN], F32)
        P.dma("sp", MT.v, self.memT[s], "mem")
        ps = self.bank()
        for c in range(8):
            P.act(SQ[c % 2].v, MT[:, c, :], AF.Square)
            P.mm(ps[:, 0:MEM_LEN], self.ONESB.v, SQ[c % 2].v, start=(c == 0), stop=(c == 7))
        self.rstd(RS.v, ps[:, 0:MEM_LEN])
        self.put(ps)
        for c in range(8):
            P.stt("dve", MN[:, c, :], MT[:, c, :], self.col(C_GMEM + c), RS.v, ALU.mult, ALU.mult)
        for g in range(2):
            w = self.wload(self.wkv[l][:, g * 512:(g + 1) * 512], 8, 512)
            for mi in range(4):
                ps = self.bank()
                for kc in range(8):
                    P.mm(ps[:, 0:MEM_LEN], w[:, kc, mi * 128:(mi + 1) * 128], MN[:, kc, :], start=(kc == 0), stop=(kc == 7))
                P.copy("act", self.KX[:, g * 4 + mi, :], ps[:, 0:MEM_LEN])
                self.put(ps)
        for g in range(2):
            w = self.wload(self.wkv[l][:, 1024 + g * 512:1024 + (g + 1) * 512], 8, 512)
            for kt in range(2):
                ps = self.bank()
                for kc in range(8):
                    P.mm(ps.v, MN[:, kc, kt * 128:(kt + 1) * 128], w[:, kc, :], start=(kc == 0), stop=(kc == 7))
                P.copy("act", self.VX[:, kt, g * 512:(g + 1) * 512], ps.v)
                self.put(ps)
        A.release(m)

    def mixer(self, s, l, b):
        P, A = self.P, self.A
        t0 = b * 512
        win = self.w_in[l]
        self.rmsnorm(b, C_GMIX)
        self.dump("HN", self.HN.v)
        m = A.mark()
        YG = A.alloc("YG", [128, 3, 512], BF16)
        U32 = A.alloc("U32", [128, 512], F32)
        UB = A.alloc("UB", [128, 512], BF16)
        COSs = [A.alloc(f"COS{i}", [128, 512], F32) for i in range(2)]
        SINs = [A.alloc(f"SIN{i}", [128, 512], F32) for i in range(2)]
        M1 = A.alloc("M1", [128, 512], F32)
        M2 = A.alloc("M2", [128, 512], F32)
        M3 = A.alloc("M3", [128, 512], F32)
        M4 = A.alloc("M4", [128, 512], F32)
        WINR = A.alloc("WINR", [128, 512], F32)
        WINI = A.alloc("WINI", [128, 512], F32)
        WRs = [A.alloc(f"WR{i}", [128, 512], F32) for i in range(2)]
        WIs = [A.alloc(f"WI{i}", [128, 512], F32) for i in range(2)]
        SR = A.alloc("SR", [128, 512], BF16)
        SI = A.alloc("SI", [128, 512], BF16)
        YT = A.alloc("YT", [128, 512], F32)
        wu = self.wload(win[:, O_U:O_U + 384], 8, 384)
        t512 = self.CONST[:, K_T:K_T + 512]
        THT0 = self.DER[:, 44:56]
        P.ts("dve", THT0, self.TH, float(t0), ALU.mult)
        for c in range(3):
            ps = self.bank()
            for kc in range(8):
                P.mm(ps.v, wu[:, kc, c * 128:(c + 1) * 128], self.HN[:, kc, :], start=(kc == 0), stop=(kc == 7))
            P.copy("act", U32.v, ps.v)
            P.copy("act", UB.v, ps.v)
            self.put(ps)
            yacc = self.bank()
            for pr in range(4):
                pair = 4 * c + pr
                COS, SIN, WR, WI = COSs[pair % 2], SINs[pair % 2], WRs[pair % 2], WIs[pair % 2]
                bur, bui = self.bank(), self.bank()
                P.mm(bur.v, self.LB[:, 0, pair, :], UB.v)
                P.mm(bui.v, self.LB[:, 1, pair, :], UB.v)
                P.act(M1.v, t512, AF.Identity, bias=THT0[:, pair:pair + 1], scale=self.TH[:, pair:pair + 1])
                self.sincos(COS.v, SIN.v, M1.v, M1.v, M2.v)
                P.tt("dve", M1.v, bur.v, COS.v, ALU.mult)
                P.tt("dve", M2.v, bui.v, SIN.v, ALU.mult)
                P.tt("dve", WINR.v, M1.v, M2.v, ALU.add)
                P.tt("dve", M1.v, bui.v, COS.v, ALU.mult)
                P.tt("dve", M2.v, bur.v, SIN.v, ALU.mult)
                P.tt("dve", WINI.v, M1.v, M2.v, ALU.subtract)
                self.put(bur, bui)
                rho = self.RHO[:, pair:pair + 1].bc([128, 512])
                P.scan(WR.v, rho, WINR.v, self.S5ST[:, pair, 0:1])
                P.scan(WI.v, rho, WINI.v, self.S5ST[:, pair, 1:2])
                P.copy("act", self.S5ST[:, pair, 0:1], WR[:, 511:512])
                P.copy("act", self.S5ST[:, pair, 1:2], WI[:, 511:512])
                P.tt("dve", M3.v, WR.v, COS.v, ALU.mult)
                P.tt("dve", M4.v, WI.v, SIN.v, ALU.mult)
                P.tt("dve", SR.v, M3.v, M4.v, ALU.subtract)
                P.tt("dve", M3.v, WR.v, SIN.v, ALU.mult)
                P.tt("dve", M4.v, WI.v, COS.v, ALU.mult)
                P.tt("dve", SI.v, M3.v, M4.v, ALU.add)
                P.mm(yacc.v, self.LC[:, 0, pair, :], SR.v, start=(pr == 0), stop=False)
                P.mm(yacc.v, self.LC[:, 1, pair, :], SI.v, start=False, stop=(pr == 3))
            P.stt("dve", YT.v, U32.v, self.col(C_SSMD + c), yacc.v, ALU.mult, ALU.add)
            self.put(yacc)
            P.act(YG[:, c, :], YT.v, AF.Gelu)
        wg = self.wload(self.w_glu[l], 3, 768)
        SG = M1
        for j in range(3):
            p1, p2 = self.bank(), self.bank()
            for kc in range(3):
                P.mm(p1.v, wg[:, kc, j * 128:(j + 1) * 128], YG[:, kc, :], start=(kc == 0), stop=(kc == 2))
            for kc in range(3):
                P.mm(p2.v, wg[:, kc, 384 + j * 128:384 + (j + 1) * 128], YG[:, kc, :], start=(kc == 0), stop=(kc == 2))
            P.act(SG.v, p2.v, AF.Sigmoid, bias=self.col(C_BGLU + 3 + j))
            P.stt("dve", self.YSSM[:, j, :], p1.v, self.col(C_BGLU + j), SG.v, ALU.add, ALU.mult)
            self.put(p1, p2)
        self.dump("YSSM", self.YSSM.v)
        A.release(m)
        if self.stop_after == "s5":
            return
        m = A.mark()
        Q = A.alloc("Q", [128, 4, 512], BF16)
        RC = A.alloc("RC", [128, 512], F32)
        RS = A.alloc("RS", [128, 512], F32)
        PI32 = A.alloc("PI32", [128, 512], I32)
        T1 = A.alloc("T1", [128, 512], F32)
        T2 = A.alloc("T2", [128, 512], F32)
        PT = [A.alloc(f"PT{i}", [128, 512], BF16) for i in range(16)]
        SMALL = [A.alloc(f"SMALL{i}", [128, 8], F32) for i in range(4)]
        OA = [A.alloc(f"OA{i}", [128, 128], F32) for i in range(4)]
        OB = [A.alloc(f"OB{i}", [128, 128], F32) for i in range(4)]
        P.dma("sp", PI32.v, self.pos[s:s + 1, t0:t0 + 512].partition_broadcast(128), "pos")
        P.copy("dve", T1.v, PI32.v)
        P.ts("dve", T1.v, T1.v, self.CONST[:, K_INVF:K_INVF + 1], ALU.mult, 1.0 / (2 * PI), ALU.mult)
        self.sincos(RC.v, RS.v, T1.v, T1.v, T2.v, sign=True)
        for (o_a, o_s, dst) in ((O_Q, O_QS, None), (O_K, O_KS, "k")):
            wa = self.wload(win[:, o_a:o_a + 512], 8, 512)
            wsw = self.wload(win[:, o_s:o_s + 512], 8, 512)
            for j in range(4):
                pa, pb = self.bank(), self.bank()
                for kc in range(8):
                    P.mm(pa.v, wa[:, kc, j * 128:(j + 1) * 128], self.HN[:, kc, :], start=(kc == 0), stop=(kc == 7))
                for kc in range(8):
                    P.mm(pb.v, wsw[:, kc, j * 128:(j + 1) * 128], self.HN[:, kc, :], start=(kc == 0), stop=(kc == 7))
                P.tt("dve", T1.v, pa.v, RC.v, ALU.mult)
                P.tt("dve", T2.v, pb.v, RS.v, ALU.mult)
                o = Q[:, j, :] if dst is None else self.KC[:, j, t0:t0 + 512]
                P.tt("dve", o, T1.v, T2.v, ALU.add)
                self.put(pa, pb)
        wv = self.wload(win[:, O_V:O_V + 512], 8, 512)
        for i in range(4):
            ps = self.bank()
            for kc in range(8):
                P.mm(ps.v, self.HN[:, kc, i * 128:(i + 1) * 128], wv[:, kc, :], start=(kc == 0), stop=(kc == 7))
            P.copy("act", self.VC[:, 4 * b + i, :, 0:128], View(ps, ps.ap.rearrange("p (h e) -> p h e", h=4)))
            self.put(ps)
        self.dump("Q", Q.v)
        ident = self.CONST[:, K_ID:K_ID + 128]
        LOOK = 2
        nk = 4 * (b + 1)
        for h in range(4):
            ob = [self.bank() for _ in range(4)]
            steps = [(c, kt) for c in range(2) for kt in range(nk)]

            def score(c, kt):
                i0 = max(0, kt - 4 * b)
                qlo = 128 * i0
                ps = self.bank()
                P.mm(ps[:, qlo:512], self.KC[64 * c:64 * c + 64, h, 128 * kt:128 * kt + 128],
                     Q[64 * c:64 * c + 64, h, qlo:512])
                P.act(PT[kt][:, qlo:512], ps[:, qlo:512], AF.Exp, scale=0.125)
                self.put(ps)
                if kt >= 4 * b:
                    P.tt("dve", PT[kt][:, qlo:qlo + 128], PT[kt][:, qlo:qlo + 128], self.CMB.v, ALU.mult)

            for si in range(min(LOOK, len(steps))):
                score(*steps[si])
            for si, (c, kt) in enumerate(steps):
                if si + LOOK < len(steps):
                    score(*steps[si + LOOK])
                i0 = max(0, kt - 4 * b)
                for i in range(i0, 4):
                    P.mm(ob[i][:, 130 * c:130 * c + 129], PT[kt][:, 128 * i:128 * i + 128],
                         self.VC[:, kt, h, 0:129], start=(kt == 0), stop=(kt == 4 * b + i))
            tb = self.bank()
            R4 = range(4)
            for i in R4:
                P.recip(SMALL[i][:, 0:1], ob[i][:, 128:129])
            for i in R4:
                P.recip(SMALL[i][:, 1:2], ob[i][:, 258:259])
            for i in R4:
                P.tt("dve", SMALL[i][:, 2:3], SMALL[i][:, 1:2], self.NLAM, ALU.mult)
            for i in R4:
                P.act(OA[i].v, ob[i][:, 0:128], AF.Identity, scale=SMALL[i][:, 0:1])
            for i in R4:
                P.stt("dve", OB[i].v, ob[i][:, 130:258], SMALL[i][:, 2:3], OA[i].v, ALU.mult, ALU.add)
            self.put(*ob)
            for i in R4:
                P.act(OA[i].v, OB[i].v, AF.Square)
            for i in R4:
                P.rsum(SMALL[i][:, 3:4], OA[i].v)
            for i in R4:
                P.act(SMALL[i][:, 5:6], SMALL[i][:, 3:4], AF.Sqrt, bias=self.EPSC, scale=1.0 / 128.0)
            for i in R4:
                P.recip(SMALL[i][:, 5:6], SMALL[i][:, 5:6])
            for i in R4:
                P.stt("dve", OA[i].v, OB[i].v, SMALL[i][:, 5:6], self.SUBG.v, ALU.mult, ALU.mult)
            for i in R4:
                P.tr(tb[:, 128 * i:128 * i + 128], OA[i].v, ident)
            P.copy("act", self.YATT[:, h, :], tb.v)
            self.put(tb)
        self.dump("YATT", self.YATT.v)
        A.release(m)
        if self.stop_after == "att":
            return
        m = A.mark()
        LT = []
        for p in range(2):
            LT.append(dict(
                XRH=A.alloc(f"XRH{p}", [128, 515], F32), GG=A.alloc(f"GG{p}", [128, 512], F32),
                XC=A.alloc(f"XC{p}", [128, 512], F32), XCB=A.alloc(f"XCB{p}", [128, 512], BF16),
                R=A.alloc(f"R{p}", [128, 512], F32), IG=A.alloc(f"IG{p}", [128, 512], F32),
                AA=A.alloc(f"AA{p}", [128, 512], F32), A2=A.alloc(f"A2{p}", [128, 512], F32),
                GX=A.alloc(f"GX{p}", [128, 512], F32), H=A.alloc(f"H{p}", [128, 512], F32)))
        wxr = self.wload(win[:, O_XR:O_XR + 512], 8, 512)
        wgr = self.wload(win[:, O_GR:O_GR + 512], 8, 512)
        for c in range(4):
            d_ = LT[c % 2]
            XRH, GG, XC, XCB, R, IG, AA, A2, GX, H = (d_[k] for k in ("XRH", "GG", "XC", "XCB", "R", "IG", "AA", "A2", "GX", "H"))
            px, pg = self.bank(), self.bank()
            for kc in range(8):
                P.mm(px.v, wxr[:, kc, c * 128:(c + 1) * 128], self.HN[:, kc, :], start=(kc == 0), stop=(kc == 7))
            for kc in range(8):
                P.mm(pg.v, wgr[:, kc, c * 128:(c + 1) * 128], self.HN[:, kc, :], start=(kc == 0), stop=(kc == 7))
            P.copy("act", XRH[:, 0:3], self.XHALO[:, c, :])
            P.copy("act", XRH[:, 3:515], px.v)
            P.act(GG.v, pg.v, AF.Gelu)
            self.put(px, pg)
            P.copy("act", self.XHALO[:, c, :], XRH[:, 512:515])
            cw = C_LCW + 4 * c
            P.ts("dve", XC.v, XRH[:, 3:515], self.col(cw + 3), ALU.mult, self.col(C_LCB + c), ALU.add)
            for j in range(3):
                P.stt("dve", XC.v, XRH[:, j:j + 512], self.col(cw + j), XC.v, ALU.mult, ALU.add)
            P.copy("dve", XCB.v, XC.v)
            pa, pi = self.bank(), self.bank()
            P.mm(pa.v, self.LW[:, 0, c, :], XCB.v)
            P.mm(pi.v, self.LW[:, 1, c, :], XCB.v)
            P.act(R.v, pa.v, AF.Sigmoid, bias=self.col(C_LBA + c))
            P.act(IG.v, pi.v, AF.Sigmoid, bias=self.col(C_LBX + c))
            self.put(pa, pi)
            P.act(AA.v, R.v, AF.Exp, scale=self.CCOL[:, c:c + 1])
            P.act(A2.v, R.v, AF.Exp, scale=self.C2COL[:, c:c + 1])
            P.act(A2.v, A2.v, AF.Sqrt, bias=1.0, scale=-1.0)
            P.tt("dve", GX.v, IG.v, XC.v, ALU.mult)
            P.tt("dve", GX.v, GX.v, A2.v, ALU.mult)
            P.scan(H.v, AA.v, GX.v, self.LST[:, c:c + 1])
            P.copy("act", self.LST[:, c:c + 1], H[:, 511:512])
            P.tt("dve", self.YLRU[:, c, :], H.v, GG.v, ALU.mult)
        self.dump("YLRU", self.YLRU.v)
        A.release(m)
        if self.stop_after == "lru":
            return
        m = A.mark()
        MG = A.alloc("MG", [128, 8, 512], BF16)
        Gs = [[A.alloc(f"G{p}{i}", [128, 512], F32) for i in range(3)] for p in range(2)]
        TTs = [[A.alloc(f"TT{p}{i}", [128, 512], F32) for i in range(2)] for p in range(2)]
        ys = [(self.YSSM, 3), (self.YATT, 4), (self.YLRU, 4)]
        for mo in range(8):
            wgt = self.wload(win[:, O_G + mo * 384:O_G + (mo + 1) * 384], 8, 384)
            if mo % 2 == 0:
                wbr = self.wload(self.w_br[l][:, mo * 128:(mo + 2) * 128], 11, 256)
            G, TT = Gs[mo % 2], TTs[mo % 2]
            gps = [self.bank() for _ in range(3)]
            bps = [self.bank() for _ in range(3)]
            for i in range(3):
                for kc in range(8):
                    P.mm(gps[i].v, wgt[:, kc, i * 128:(i + 1) * 128], self.HN[:, kc, :], start=(kc == 0), stop=(kc == 7))
            k0 = 0
            for i in range(3):
                yt, nkc = ys[i]
                for kc in range(nkc):
                    P.mm(bps[i].v, wbr[:, k0 + kc, (mo % 2) * 128:(mo % 2) * 128 + 128], yt[:, kc, :],
                         start=(kc == 0), stop=(kc == nkc - 1))
                k0 += nkc
            for i in range(3):
                P.act(G[i].v, gps[i].v, AF.Sigmoid)
            P.tt("dve", TT[0].v, G[0].v, bps[0].v, ALU.mult)
            P.tt("dve", TT[1].v, G[1].v, bps[1].v, ALU.mult)
            P.tt("dve", TT[0].v, TT[0].v, TT[1].v, ALU.add)
            P.tt("dve", TT[1].v, G[2].v, bps[2].v, ALU.mult)
            P.tt("dve", MG[:, mo, :], TT[0].v, TT[1].v, ALU.add)
            self.put(*gps)
            self.put(*bps)
        self.proj_residual(self.w_out[l], MG, 8, t0)
        A.release(m)

    def proj_residual(self, w, act, nkc, t0):
        P = self.P
        if nkc * 512 <= 4096:
            groups = [(g * 512, 512) for g in range(2)]
        else:
            groups = [(g * 128, 128) for g in range(8)]
        for (m0, mw) in groups:
            wt = self.wload(w[:, m0:m0 + mw], nkc, mw)
            for mi in range(mw // 128):
                ps = self.bank()
                for kc in range(nkc):
                    P.mm(ps.v, wt[:, kc, mi * 128:(mi + 1) * 128], act[:, kc, :], start=(kc == 0), stop=(kc == nkc - 1))
                mo = m0 // 128 + mi
                P.tt("dve", self.X[:, mo, t0:t0 + 512], self.X[:, mo, t0:t0 + 512], ps.v, ALU.add)
                self.put(ps)

    def xattn(self, s, l, b):
        P, A = self.P, self.A
        t0 = b * 512
        self.rmsnorm(b, C_GXA)
        m = A.mark()
        QX = A.alloc("QX", [128, 8, 512], BF16)
        OX = A.alloc("OX", [128, 8, 512], BF16)
        PXs = [[A.alloc(f"PX{p}{i}", [128, 512], BF16) for i in range(2)] for p in range(2)]
        RDs = [A.alloc(f"RD{p}", [128, 512], F32) for p in range(2)]
        for g in range(2):
            wt = self.wload(self.wq[l][:, g * 512:(g + 1) * 512], 8, 512)
            for mi in range(4):
                ps = self.bank()
                for kc in range(8):
                    P.mm(ps.v, wt[:, kc, mi * 128:(mi + 1) * 128], self.HN[:, kc, :], start=(kc == 0), stop=(kc == 7))
                P.copy("act", QX[:, g * 4 + mi, :], ps.v)
                self.put(ps)
        for h in range(4):
            PX, RD = PXs[h % 2], RDs[h % 2]
            for kt in range(2):
                ps = self.bank()
                for dc in range(2):
                    P.mm(ps.v, self.KX[:, 2 * h + dc, 128 * kt:128 * kt + 128], QX[:, 2 * h + dc, :],
                         start=(dc == 0), stop=(dc == 1))
                P.act(PX[kt].v, ps.v, AF.Exp, scale=1.0 / 16.0)
                self.put(ps)
            pd = self.bank()
            for kt in range(2):
                P.mm(pd.v, self.ONE1.v, PX[kt].v, start=(kt == 0), stop=(kt == 1))
            P.recip(RD.v, pd.v)
            self.put(pd)
            for ec in range(2):
                po = self.bank()
                for kt in range(2):
                    P.mm(po.v, self.VX[:, kt, (2 * h + ec) * 128:(2 * h + ec) * 128 + 128], PX[kt].v,
                         start=(kt == 0), stop=(kt == 1))
                P.tt("dve", OX[:, 2 * h + ec, :], po.v, RD.v, ALU.mult)
                self.put(po)
        self.proj_residual(self.wo[l], OX, 8, t0)
        A.release(m)

    def ffn(self, s, l, b):
        P, A = self.P, self.A
        t0 = b * 512
        self.rmsnorm(b, C_GFFN)
        m = A.mark()
        HF = A.alloc("HF", [128, NFF, 512], BF16)
        ACC = [[A.alloc(f"ACC{p}{i}", [128, 512], F32) for i in range(2)] for p in range(2)]
        SGs = [A.alloc(f"SGF{p}", [128, 512], F32) for p in range(2)]
        for f in range(NFF):
            if f % 2 == 0:
                wt = self.wload(self.w_up[l][:, f * 256:(f + 2) * 256], 8, 512)
            pss = [self.bank(), self.bank()]
            for vg in range(2):
                c0 = (f % 2) * 256 + vg * 128
                for kc in range(8):
                    P.mm(pss[vg].v, wt[:, kc, c0:c0 + 128], self.HN[:, kc, :], start=(kc == 0), stop=(kc == 7))
            for vg in range(2):
                idx = 2 * f + vg
                acc, ps = ACC[f % 2][vg], pss[vg]
                cw = C_FCW + idx * 3
                w0, w1 = self.col(cw + 0), self.col(cw + 1)
                P.act(acc.v, ps.v, AF.Identity, bias=self.col(C_FCB + idx), scale=self.col(cw + 2))
                P.stt("dve", acc[:, 1:512], ps[:, 0:511], w1, acc[:, 1:512], ALU.mult, ALU.add)
                P.stt("dve", acc[:, 2:512], ps[:, 0:510], w0, acc[:, 2:512], ALU.mult, ALU.add)
                P.stt("dve", acc[:, 0:2], self.CF[:, idx, :], w0, acc[:, 0:2], ALU.mult, ALU.add)
                P.stt("dve", acc[:, 0:1], self.CF[:, idx, 1:2], w1, acc[:, 0:1], ALU.mult, ALU.add)
                P.copy("act", self.CF[:, idx, :], ps[:, 510:512])
            self.put(*pss)
            SG = SGs[f % 2]
            P.act(SG.v, ACC[f % 2][1].v, AF.Silu)
            P.tt("dve", HF[:, f, :], ACC[f % 2][0].v, SG.v, ALU.mult)
        self.proj_residual(self.w_down[l], HF, NFF, t0)
        A.release(m)

    def final(self, s, b):
        P, A = self.P, self.A
        t0 = b * 512
        m = A.mark()
        SQ = [A.alloc(f"fSQ{i}", [128, 512], BF16) for i in range(2)]
        RSTD = A.alloc("fRSTD", [128, 512], F32)
        OT = A.alloc("OT", [128, 8, 512], F32)
        ps = self.bank()
        for c in range(8):
            P.act(SQ[c % 2].v, self.X[:, c, t0:t0 + 512], AF.Square)
            P.mm(ps.v, self.ONESB.v, SQ[c % 2].v, start=(c == 0), stop=(c == 7))
        self.rstd(RSTD.v, ps.v)
        self.put(ps)
        for c in range(8):
            P.stt("dve", OT[:, c, :], self.X[:, c, t0:t0 + 512], self.FNG[:, c:c + 1], RSTD.v, ALU.mult, ALU.mult)
        P.dma("sp", self.outT[s][:, :, t0:t0 + 512], OT.v, "out")
        A.release(m)


def _colT(v, n):
    return np.ascontiguousarray(np.asarray(v).reshape(n, 128).T)


def host_layout(inp):
    f32 = np.float32
    L = DEPTH
    g = {k: np.asarray(v) for k, v in inp.items()}
    consts = np.zeros((128, NCONST), f32)
    consts[:, K_ID:K_ID + 128] = np.eye(128, dtype=f32)
    kk, qq = np.meshgrid(np.arange(128), np.arange(128), indexing="ij")
    consts[:, K_CM:K_CM + 128] = (kk <= qq).astype(f32)
    consts[:, K_T:K_T + 512] = np.arange(512, dtype=f32)[None, :]
    inv = np.exp(np.float32(-math.log(10000.0)) * np.arange(0, 64, 2, dtype=f32) / np.float32(64)).astype(f32)
    rows = np.arange(128)
    consts[:, K_INVF] = inv[rows % 32]
    sign = np.where((rows % 64) < 32, -1.0, 1.0).astype(f32)
    consts[:, K_SIGN] = np.float32(2 * PI) * sign
    consts[:, K_NPS] = (-np.float32(PI)) * sign
    cols = np.zeros((L, 128, NCOL), f32)
    for l in range(L):
        cols[l, :, C_GMIX:C_GMIX + 8] = _colT(g["norm_mix_g"][l], 8)
        cols[l, :, C_GXA:C_GXA + 8] = _colT(g["norm_xattn_g"][l], 8)
        cols[l, :, C_GMEM:C_GMEM + 8] = _colT(g["norm_mem_g"][l], 8)
        cols[l, :, C_GFFN:C_GFFN + 8] = _colT(g["norm_ffn_g"][l], 8)
        cols[l, :, C_SSMD:C_SSMD + 3] = _colT(g["ssm_d"][l], 3)
        cols[l, :, C_BGLU:C_BGLU + 6] = _colT(g["ssm_b_glu"][l], 6)
        cw = g["lru_conv_w"][l]
        cols[l, :, C_LCW:C_LCW + 16] = cw.reshape(4, 4, 128).transpose(2, 1, 0).reshape(128, 16)
        cols[l, :, C_LCB:C_LCB + 4] = _colT(g["lru_conv_b"][l], 4)
        cols[l, :, C_LBA:C_LBA + 4] = _colT(g["lru_ba"][l], 4)
        cols[l, :, C_LBX:C_LBX + 4] = _colT(g["lru_bx"][l], 4)
        cols[l, :, C_LLAM:C_LLAM + 4] = _colT(g["lru_lambda"][l], 4)
        fw = g["ffn_conv_w"][l].reshape(3, 2, NFF, 128)
        cols[l, :, C_FCW:C_FCW + 132] = fw.transpose(3, 2, 1, 0).reshape(128, 132)
        fb = g["ffn_conv_b"][l].reshape(2, NFF, 128)
        cols[l, :, C_FCB:C_FCB + 44] = fb.transpose(2, 1, 0).reshape(128, 44)
        cols[l, :, C_LR:C_LR + 12] = g["ssm_lambda_re"][l].reshape(12, 2, 64).transpose(1, 2, 0).reshape(128, 12)
        cols[l, :, C_LI:C_LI + 12] = g["ssm_lambda_im"][l].reshape(12, 2, 64).transpose(1, 2, 0).reshape(128, 12)
        cols[l, :, C_LS:C_LS + 12] = np.repeat(g["ssm_log_step"][l].reshape(12, 2, 1), 64, axis=2).transpose(1, 2, 0).reshape(128, 12)
    w = g["w_in"]
    offs = np.cumsum([0, 384, 512, 512, 512, 512, 512, 3072])
    u, q, k, v, xr, gr, gates = [w[:, :, offs[i]:offs[i + 1]] for i in range(7)]

    def swap_halves(a):
        s = a.shape
        a = a.reshape(s[0], s[1], 8, 2, 32)
        return a[:, :, :, ::-1, :].reshape(s)
    gates_re = gates.reshape(L, 1024, 3, 8, 128).transpose(0, 1, 3, 2, 4).reshape(L, 1024, 3072)
    w_in_ext = np.concatenate([u, q, swap_halves(q), k, swap_halves(k), v, xr, gr, gates_re], axis=2)
    w_br = np.concatenate([g["w_br_ssm"], g["w_br_attn"], g["w_br_lru"]], axis=1)
    w_up = g["ffn_w_up"].reshape(L, 1024, 2, NFF, 128).transpose(0, 1, 3, 2, 4).reshape(L, 1024, 2 * D_FF)

    def s5lay(re, im, bside):
        out = np.zeros((L, 128, 2, 12, 16), f32)
        for i, a in enumerate((re, im)):
            if bside:
                out[:, :, i] = a.reshape(L, 12, 2, 64, 16).transpose(0, 2, 3, 1, 4).reshape(L, 128, 12, 16)
            else:
                out[:, :, i] = a.reshape(L, 12, 2, 16, 64).transpose(0, 2, 4, 1, 3).reshape(L, 128, 12, 16)
        return out
    shared = {
        "consts": consts, "cols": cols, "fng": _colT(g["final_norm_g"], 8).astype(f32),
        "w_in": np.ascontiguousarray(w_in_ext), "w_glu": g["ssm_w_glu"], "w_br": np.ascontiguousarray(w_br),
        "w_out": g["w_out"], "wq": g["xattn_wq"], "wkv": g["xattn_wkv"], "wo": g["xattn_wo"],
        "w_up": np.ascontiguousarray(w_up), "w_down": g["ffn_w_down"],
        "s5b": s5lay(g["ssm_b_re"], g["ssm_b_im"], True), "s5c": s5lay(g["ssm_c_re"], g["ssm_c_im"], False),
        "lruw": np.ascontiguousarray(np.stack([g["lru_wa"], g["lru_wx"]], axis=1)),
        "dl": np.ascontiguousarray(np.stack([g["diff_lq1"], g["diff_lk1"], g["diff_lq2"], g["diff_lk2"]], axis=1)),
        "subg": g["diff_subln_g"],
    }
    shared = {k: np.ascontiguousarray(v.astype(f32)) for k, v in shared.items()}
    x, mem, pos = g["x"], g["mem"], g["positions"]
    in_maps = []
    for c in range(8):
        xs = x[2 * c:2 * c + 2]
        ms = mem[2 * c:2 * c + 2]
        d = dict(shared)
        d["xT"] = np.ascontiguousarray(xs.reshape(2, SEQ, 8, 128).transpose(0, 3, 2, 1)).astype(f32)
        d["memT"] = np.ascontiguousarray(ms.reshape(2, MEM_LEN, 8, 128).transpose(0, 3, 2, 1)).astype(f32)
        d["pos"] = np.ascontiguousarray(pos[2 * c:2 * c + 2]).astype(np.int32)
        in_maps.append(d)
    return in_maps


_CACHE = {}


def kernel(**inputs):
    in_maps = host_layout(inputs)
    if "nc" not in _CACHE:
        _CACHE["nc"] = Builder().nc
    nc = _CACHE["nc"]
    res = run_bass_kernel_spmd(nc, in_maps, core_ids=list(range(8)))
    out = np.zeros((16, SEQ, D_MODEL), np.float32)
    for c in range(8):
        o = np.asarray(res.results[c]["outT"])
        out[2 * c:2 * c + 2] = o.transpose(0, 3, 2, 1).reshape(2, SEQ, D_MODEL)
    return out
```

```python
import math
import contextlib
import numpy as np
import concourse.bass as bass
import concourse.mybir as mybir
from concourse.bass_utils import run_bass_kernel_spmd

F32 = mybir.dt.float32
BF16 = mybir.dt.bfloat16
I32 = mybir.dt.int32
AF = mybir.ActivationFunctionType
ALU = mybir.AluOpType
AX = mybir.AxisListType

D_MODEL = 1024
SEQ = 2048
DEPTH = 4
MEM_LEN = 256
EPS = 1e-6
D_FF = 2816
NFF = 22
PI = math.pi


class View:
    __slots__ = ("t", "ap")

    def __init__(self, t, ap):
        self.t = t
        self.ap = ap

    def __getitem__(self, idx):
        return View(self.t, self.ap[idx])

    def bc(self, shape):
        return View(self.t, self.ap.to_broadcast(shape))


class T:
    __slots__ = ("ap", "name", "last_w", "readers", "lo", "hi")

    def __init__(self, ap, name="", lo=0, hi=0):
        self.ap = ap
        self.name = name
        self.last_w = None
        self.readers = []
        self.lo = lo
        self.hi = hi

    def __getitem__(self, idx):
        return View(self, self.ap[idx])

    @property
    def v(self):
        return View(self, self.ap)


def _ap(x):
    return x.ap if isinstance(x, View) else x


class Prog:
    ENGS = ("pe", "act", "dve", "pool", "sp")

    def __init__(self, nc):
        self.nc = nc
        self.streams = {e: [] for e in self.ENGS}
        self.dma_keys = {}

    def _collect(self, eng, reads, writes):
        deps = set()
        for t in reads:
            d = t.last_w
            if d is not None:
                if not (d[0] == "e" and d[1] == eng and eng == "pe"):
                    deps.add(d)
        for t in writes:
            d = t.last_w
            if d is not None and not (d[0] == "e" and d[1] == eng):
                deps.add(d)
            for d in t.readers:
                if not (d[0] == "e" and d[1] == eng):
                    deps.add(d)
        return deps

    def op(self, eng, fn, reads=(), writes=()):
        reads = [v.t for v in reads if isinstance(v, View)]
        writes = [v.t for v in writes if isinstance(v, View)]
        deps = self._collect(eng, reads, writes)
        seq = len(self.streams[eng])
        me = ("e", eng, seq)
        self.streams[eng].append({"fn": fn, "deps": deps, "dma": None})
        for t in reads:
            t.readers.append(me)
        for t in writes:
            t.last_w = me
            t.readers = []
        return me

    def dma(self, q, out, in_, key, **kw):
        reads = [in_.t] if isinstance(in_, View) else []
        writes = [out.t] if isinstance(out, View) else []
        if writes:
            key = "t_" + writes[0].name
        deps = self._collect("dma", reads, writes)
        deps = set(d for d in deps if not (d[0] == "d" and d[1] == key and writes))
        cnt = self.dma_keys.get(key, 0) + 1
        self.dma_keys[key] = cnt
        me = ("d", key, cnt * 16)
        o, i = _ap(out), _ap(in_)
        self.streams[q].append({
            "fn": (lambda e, o=o, i=i, kw=kw: e.dma_start(out=o, in_=i, **kw)),
            "deps": deps, "dma": me})
        for t in reads:
            t.readers.append(me)
        for t in writes:
            t.last_w = me
            t.readers = []
        return me

    def wait_all_dma(self, q, keys):
        deps = set(("d", k, self.dma_keys[k] * 16) for k in keys if k in self.dma_keys)
        self.streams[q].append({"fn": None, "deps": deps, "dma": None})

    def mm(self, out, lhsT, rhs, start=True, stop=True):
        return self.op("pe", lambda e: e.matmul(out.ap, lhsT.ap, rhs.ap, start=start, stop=stop),
                       reads=[lhsT, rhs], writes=[out])

    def tr(self, out, in_, ident):
        return self.op("pe", lambda e: e.transpose(out.ap, in_.ap, ident.ap), reads=[in_, ident], writes=[out])

    def act(self, out, in_, func, bias=None, scale=None, accum=None):
        kw = {}
        if bias is not None:
            kw["bias"] = _ap(bias)
        if scale is not None:
            kw["scale"] = _ap(scale)
        if accum is not None:
            kw["accum_out"] = _ap(accum)
        w = [out] + ([accum] if accum is not None else [])
        return self.op("act", lambda e: e.activation(out=out.ap, in_=in_.ap, func=func, **kw),
                       reads=[in_, bias, scale], writes=w)

    def tt(self, eng, out, in0, in1, op):
        return self.op(eng, lambda e: e.tensor_tensor(out=out.ap, in0=in0.ap, in1=in1.ap, op=op),
                       reads=[in0, in1], writes=[out])

    def ts(self, eng, out, in0, s1, op0, s2=None, op1=None):
        kw = {}
        if op1 is not None:
            kw["op1"] = op1
        return self.op(eng, lambda e: e.tensor_scalar(out=out.ap, in0=in0.ap, scalar1=_ap(s1), scalar2=_ap(s2),
                                                      op0=op0, **kw),
                       reads=[in0, s1, s2], writes=[out])

    def stt(self, eng, out, in0, scalar, in1, op0, op1):
        return self.op(eng, lambda e: e.scalar_tensor_tensor(out=out.ap, in0=in0.ap, scalar=_ap(scalar),
                                                             in1=in1.ap, op0=op0, op1=op1),
                       reads=[in0, scalar, in1], writes=[out])

    def scan(self, out, d0, d1, init, op0=ALU.mult, op1=ALU.add):
        return self.op("dve", lambda e: e.tensor_tensor_scan(out=out.ap, data0=d0.ap, data1=d1.ap,
                                                             initial=_ap(init), op0=op0, op1=op1),
                       reads=[d0, d1, init], writes=[out])

    def copy(self, eng, out, in_):
        if eng == "act":
            return self.op("act", lambda e: e.copy(out=out.ap, in_=in_.ap), reads=[in_], writes=[out])
        return self.op(eng, lambda e: e.tensor_copy(out=out.ap, in_=in_.ap), reads=[in_], writes=[out])

    def memset(self, eng, out, val):
        return self.op(eng, lambda e: e.memset(out.ap, val), writes=[out])

    def recip(self, out, in_):
        return self.op("dve", lambda e: e.reciprocal(out=out.ap, in_=in_.ap), reads=[in_], writes=[out])

    def rsum(self, out, in_):
        return self.op("dve", lambda e: e.reduce_sum(out=out.ap, in_=in_.ap, axis=AX.X), reads=[in_], writes=[out])

    def emit(self):
        nc = self.nc
        needed = {e: set() for e in self.ENGS}
        for e in self.ENGS:
            for ins in self.streams[e]:
                for d in ins["deps"]:
                    if d[0] == "e":
                        needed[d[1]].add(d[2])
        cum = {}
        for e in self.ENGS:
            c = 0
            arr = []
            nd = needed[e]
            for i in range(len(self.streams[e])):
                if i in nd:
                    c += 1
                arr.append(c)
            cum[e] = arr
        with contextlib.ExitStack() as st:
            esem = {e: st.enter_context(nc.semaphore("s_" + e)) for e in self.ENGS}
            dsem = {k: st.enter_context(nc.semaphore("d_" + str(k))) for k in self.dma_keys}
            block = st.enter_context(nc.Block())
            streams = self.streams

            def run_stream(e, eng):
                waited = {}
                nd = needed[e]
                for i, ins in enumerate(streams[e]):
                    w = {}
                    for d in ins["deps"]:
                        if d[0] == "e":
                            k = ("e", d[1])
                            v = cum[d[1]][d[2]]
                        else:
                            k = ("d", d[1])
                            v = d[2]
                        if v > w.get(k, 0):
                            w[k] = v
                    for k, v in w.items():
                        if waited.get(k, 0) >= v:
                            continue
                        waited[k] = v
                        eng.wait_ge(esem[k[1]] if k[0] == "e" else dsem[k[1]], v)
                    if ins["fn"] is None:
                        continue
                    bi = ins["fn"](eng)
                    if ins["dma"] is not None:
                        bi.then_inc(dsem[ins["dma"][1]], 16)
                    elif i in nd:
                        bi.then_inc(esem[e], 1)

            @block.tensor
            def _(eng):
                run_stream("pe", eng)

            @block.scalar
            def _(eng):
                run_stream("act", eng)

            @block.vector
            def _(eng):
                run_stream("dve", eng)

            @block.gpsimd
            def _(eng):
                run_stream("pool", eng)

            @block.sync
            def _(eng):
                run_stream("sp", eng)


class Arena:
    def __init__(self, nc, limit):
        self.nc = nc
        self.limit = limit
        self.top = 18432
        self.live = []
        self.dead = []
        self.n = 0

    def alloc(self, name, shape, dt):
        esz = 2 if dt == BF16 else 4
        nbytes = int(np.prod(shape[1:])) * esz
        nbytes = (nbytes + 31) // 32 * 32
        off = self.top
        self.top += nbytes
        assert off + nbytes <= self.limit, f"SBUF overflow at {name}: {off + nbytes}"
        self.n += 1
        h = self.nc.alloc_sbuf_tensor_at(f"{name}_{self.n}", list(shape), dt, offset=off)
        t = T(h.ap(), name, off, off + nbytes)
        keep = []
        for (lo, hi, deps) in self.dead:
            if lo < t.hi and t.lo < hi:
                t.readers.extend(deps)
                if t.lo <= lo and hi <= t.hi:
                    continue
            keep.append((lo, hi, deps))
        self.dead = keep
        self.live.append(t)
        return t

    def mark(self):
        return self.top

    def release(self, m):
        keep = []
        for t in self.live:
            if t.lo >= m:
                deps = list(t.readers)
                if t.last_w is not None:
                    deps.append(t.last_w)
                if deps:
                    self.dead.append((t.lo, t.hi, deps))
            else:
                keep.append(t)
        self.live = keep
        self.top = m


C_GMIX, C_GXA, C_GMEM, C_GFFN = 0, 8, 16, 24
C_SSMD, C_BGLU = 32, 35
C_LCW, C_LCB, C_LBA, C_LBX, C_LLAM = 41, 57, 61, 65, 69
C_FCW, C_FCB = 73, 205
C_LR, C_LI, C_LS = 249, 261, 273
NCOL = 288
K_ID, K_CM, K_T, K_INVF, K_SIGN, K_NPS = 0, 128, 256, 768, 769, 770
NCONST = 776
O_U, O_Q, O_QS, O_K, O_KS, O_V, O_XR, O_GR, O_G = 0, 384, 896, 1408, 1920, 2432, 2944, 3456, 3968
W_IN_EXT = 7040


class Builder:
    def __init__(self, nlayers=DEPTH, nseq=2, nblk=4, dbg=(), stop_after=None):
        self.L = nlayers
        self.NS = nseq
        self.NB = nblk
        self.dbg_names = dict(dbg)
        self.stop_after = stop_after
        nc = bass.Bass("TRN2", target_bir_lowering=False)
        self.nc = nc
        self.P = Prog(nc)
        L = nlayers

        def din(name, shape, dt=F32):
            return nc.dram_tensor(name, list(shape), dt, kind="ExternalInput").ap()
        self.xT = din("xT", [2, 128, 8, SEQ])
        self.memT = din("memT", [2, 128, 8, MEM_LEN])
        self.pos = din("pos", [2, SEQ], I32)
        self.consts = din("consts", [128, NCONST])
        self.cols = din("cols", [L, 128, NCOL])
        self.fng = din("fng", [128, 8])
        self.w_in = din("w_in", [L, 1024, W_IN_EXT])
        self.w_glu = din("w_glu", [L, 384, 768])
        self.w_br = din("w_br", [L, 1408, 1024])
        self.w_out = din("w_out", [L, 1024, 1024])
        self.wq = din("wq", [L, 1024, 1024])
        self.wkv = din("wkv", [L, 1024, 2048])
        self.wo = din("wo", [L, 1024, 1024])
        self.w_up = din("w_up", [L, 1024, 2 * D_FF])
        self.w_down = din("w_down", [L, D_FF, 1024])
        self.s5b = din("s5b", [L, 128, 2, 12, 16])
        self.s5c = din("s5c", [L, 128, 2, 12, 16])
        self.lruw = din("lruw", [L, 2, 8, 64, 64])
        self.dl = din("dl", [L, 4, 64])
        self.subg = din("subg", [L, 128])
        self.outT = nc.dram_tensor("outT", [2, 128, 8, SEQ], F32, kind="ExternalOutput").ap()
        self.dbg_out = {n: nc.dram_tensor("dbg_" + n, list(s), F32, kind="ExternalOutput").ap()
                        for n, s in self.dbg_names.items()}
        self.A = Arena(nc, 229376)
        self.banks = [T(nc.alloc_psum_tensor(f"psb{i}", [128, 512], F32).ap(), f"psb{i}") for i in range(8)]
        self.free_banks = list(range(8))
        self.wslot_i = 0
        self.build()
        self.P.emit()

    def bank(self):
        i = self.free_banks.pop(0)
        return self.banks[i]

    def put(self, *bs):
        for b in bs:
            self.free_banks.append(self.banks.index(b))

    def wload(self, src_ap, kc, mw):
        assert kc * mw <= 4096
        i = self.wslot_i
        self.wslot_i = (i + 1) % len(self.WS)
        slot = self.WS[i]
        v = View(slot, slot.ap[:, 0:kc * mw].rearrange("p (k m) -> p k m", k=kc))
        self.P.dma("pool", v, src_ap.rearrange("(k p) m -> p k m", p=128), f"w{i}")
        return v

    def dump(self, name, view):
        if name in self.dbg_out:
            self.P.dma("pool", self.dbg_out[name], view, "dbg")

    def build(self):
        P, A = self.P, self.A
        al = A.alloc
        self.X = al("X", [128, 8, SEQ], F32)
        self.CONST = al("CONST", [128, NCONST], F32)
        self.COLS = al("COLS", [128, NCOL], F32)
        self.FNG = al("FNG", [128, 8], F32)
        self.ONESB = al("ONESB", [128, 128], BF16)
        self.ONE1 = al("ONE1", [128, 128], BF16)
        self.CMB = al("CMB", [128, 128], BF16)
        self.WS = [al(f"WS{i}", [128, 4096], BF16) for i in range(3)]
        self.LB = al("LB", [128, 2, 12, 128], BF16)
        self.LC = al("LC", [128, 2, 12, 128], BF16)
        self.LW = al("LW", [128, 2, 4, 128], BF16)
        self.DER = al("DER", [128, 64], F32)
        self.SUBG = al("SUBG", [128, 128], F32)
        self.KX = al("KX", [128, 8, MEM_LEN], BF16)
        self.VX = al("VX", [128, 2, 1024], BF16)
        self.KC = al("KC", [128, 4, SEQ], BF16)
        self.VC = al("VC", [128, 16, 4, 130], BF16)
        self.CF = al("CF", [128, 44, 2], F32)
        self.XHALO = al("XHALO", [128, 4, 3], F32)
        self.S5ST = al("S5ST", [128, 12, 2], F32)
        self.LST = al("LST", [128, 4], F32)
        self.HN = al("HN", [128, 8, 512], BF16)
        self.YSSM = al("YSSM", [128, 3, 512], BF16)
        self.YATT = al("YATT", [128, 4, 512], BF16)
        self.YLRU = al("YLRU", [128, 4, 512], BF16)
        self.scr0 = A.mark()
        print("persistent bytes", self.scr0)

        P.dma("sp", self.CONST.v, self.consts, "c0")
        P.dma("sp", self.FNG.v, self.fng, "c0")
        P.memset("dve", self.ONESB.v, 1.0 / 1024.0)
        P.memset("dve", self.ONE1.v, 1.0)
        P.copy("dve", self.CMB.v, self.CONST[:, K_CM:K_CM + 128])
        P.memset("dve", self.VC.v, 1.0)
        P.memset("dve", self.LW.v, 0.0)
        P.memset("dve", self.LB.v, 0.0)
        P.memset("dve", self.LC.v, 0.0)

        for s in range(self.NS):
            P.dma("sp", self.X.v, self.xT[s], "x")
            for l in range(self.L):
                self.layer_setup(s, l)
                if self.stop_after == "setup":
                    break
                for b in range(self.NB):
                    self.mixer(s, l, b)
                    if self.stop_after in ("s5", "att", "lru", "merge"):
                        continue
                    self.xattn(s, l, b)
                    if self.stop_after == "xattn":
                        continue
                    self.ffn(s, l, b)
            if self.stop_after is None:
                for b in range(self.NB):
                    self.final(s, b)
            else:
                self.dump("X", self.X.v)
        P.wait_all_dma("sp", ["out", "dbg"])

    def col(self, c, n=1):
        return self.COLS[:, c:c + n]

    def rmsnorm(self, blk, gc, gt=None):
        P, A = self.P, self.A
        m = A.mark()
        SQ = [A.alloc(f"SQ{i}", [128, 512], BF16) for i in range(2)]
        RSTD = A.alloc("RSTD", [128, 512], F32)
        t0 = blk * 512
        ps = self.bank()
        for c in range(8):
            sq = SQ[c % 2]
            P.act(sq.v, self.X[:, c, t0:t0 + 512], AF.Square)
            P.mm(ps.v, self.ONESB.v, sq.v, start=(c == 0), stop=(c == 7))
        self.rstd(RSTD.v, ps.v)
        self.put(ps)
        gt = gt if gt is not None else self.COLS
        for c in range(8):
            P.stt("dve", self.HN[:, c, :], self.X[:, c, t0:t0 + 512], gt[:, gc + c:gc + c + 1], RSTD.v,
                  ALU.mult, ALU.mult)
        A.release(m)
        return RSTD

    def rstd(self, out, in_, scale=1.0):
        self.P.act(out, in_, AF.Sqrt, bias=self.EPSC, scale=scale)
        self.P.recip(out, out)

    def sincos(self, out_c, out_s, tcyc, tA, tB, sign=False):
        P = self.P
        MAGIC = 12582912.0
        P.ts("dve", tB, tcyc, MAGIC, ALU.add, MAGIC, ALU.subtract)
        P.tt("dve", tB, tcyc, tB, ALU.subtract)
        P.stt("dve", tA, tB, -1.0, tB, ALU.mult, ALU.max)
        if sign:
            P.act(out_s, tB, AF.Sin, scale=self.CONST[:, K_SIGN:K_SIGN + 1])
        else:
            P.act(out_s, tB, AF.Sin, scale=2 * PI)
        P.act(out_c, tA, AF.Sin, bias=self.HPI, scale=-2 * PI)

    def layer_setup(self, s, l):
        P, A = self.P, self.A
        m = A.mark()
        P.dma("sp", self.COLS.v, self.cols[l], "cols")
        for t in (self.S5ST, self.LST, self.XHALO, self.CF):
            P.memset("dve", t.v, 0.0)
        D = self.DER
        self.HPI = D[:, 60:61]
        self.EPSC = D[:, 61:62]
        P.memset("dve", self.HPI, PI / 2)
        P.memset("dve", self.EPSC, EPS)
        TH, RHO = D[:, 0:12], D[:, 12:24]
        self.TH, self.RHO = TH, RHO
        self.CCOL, self.C2COL, self.NLAM = D[:, 24:28], D[:, 28:32], D[:, 32:33]
        W = A.alloc("s5w", [128, 12, 12], F32)
        lr, li, ls = self.col(C_LR, 12), self.col(C_LI, 12), self.col(C_LS, 12)
        dt, a_, tmpa, tmpb = W[:, 0, :], W[:, 1, :], W[:, 2, :], W[:, 3, :]
        cs, sn, abr, abi = W[:, 4, :], W[:, 5, :], W[:, 6, :], W[:, 7, :]
        den, fr, fi, nr = W[:, 8, :], W[:, 9, :], W[:, 10, :], W[:, 11, :]
        P.act(dt, ls, AF.Exp)
        P.tt("dve", a_, lr, dt, ALU.mult)
        P.act(RHO, a_, AF.Exp)
        P.tt("dve", TH, li, dt, ALU.mult)
        P.ts("dve", TH, TH, 1.0 / (2 * PI), ALU.mult)
        self.sincos(cs, sn, TH, tmpa, tmpb)
        P.tt("dve", abr, RHO, cs, ALU.mult)
        P.tt("dve", abi, RHO, sn, ALU.mult)
        P.tt("dve", den, lr, lr, ALU.mult)
        P.tt("dve", tmpa, li, li, ALU.mult)
        P.tt("dve", den, den, tmpa, ALU.add)
        P.recip(den, den)
        P.ts("dve", nr, abr, -1.0, ALU.add)
        P.tt("dve", tmpa, nr, lr, ALU.mult)
        P.tt("dve", tmpb, abi, li, ALU.mult)
        P.tt("dve", tmpa, tmpa, tmpb, ALU.add)
        P.tt("dve", fr, tmpa, den, ALU.mult)
        P.tt("dve", tmpa, abi, lr, ALU.mult)
        P.tt("dve", tmpb, nr, li, ALU.mult)
        P.tt("dve", tmpa, tmpa, tmpb, ALU.subtract)
        P.tt("dve", fi, tmpa, den, ALU.mult)
        BC = A.alloc("s5bc", [128, 4, 12, 16], F32)
        P.dma("sp", BC[:, 0:2], self.s5b[l], "s5")
        P.dma("sp", BC[:, 2:4], self.s5c[l], "s5")
        BB = A.alloc("s5bb", [128, 2, 12, 16], F32)
        T1 = A.alloc("s5t1", [128, 12, 16], F32)
        frb, fib = fr.ap.unsqueeze(2).to_broadcast([128, 12, 16]), fi.ap.unsqueeze(2).to_broadcast([128, 12, 16])
        frb, fib = View(W, frb), View(W, fib)
        P.tt("dve", BB[:, 0], BC[:, 0], frb, ALU.mult)
        P.tt("dve", T1.v, BC[:, 1], fib, ALU.mult)
        P.tt("dve", BB[:, 0], BB[:, 0], T1.v, ALU.subtract)
        P.tt("dve", BB[:, 1], BC[:, 1], frb, ALU.mult)
        P.tt("dve", T1.v, BC[:, 0], fib, ALU.mult)
        P.tt("dve", BB[:, 1], BB[:, 1], T1.v, ALU.add)
        MP = [A.alloc(f"s5mp{i}", [128, 128], F32) for i in range(2)]
        for t in MP:
            P.memset("dve", t.v, 0.0)
        ident = self.CONST[:, K_ID:K_ID + 128]
        n = 0
        for ri in range(2):
            for pair in range(12):
                k = pair % 4
                mp = MP[n % 2]
                n += 1
                for gl in range(2):
                    P.copy("dve", mp[64 * gl:64 * gl + 64, 32 * k + 16 * gl:32 * k + 16 * gl + 16],
                           BB[64 * gl:64 * gl + 64, ri, pair, :])
                ps = self.bank()
                P.tr(ps[:, 0:128], mp.v, ident)
                P.copy("act", self.LB[:, ri, pair, :], ps[:, 0:128])
                self.put(ps)
                for gl in range(2):
                    P.memset("dve", mp[64 * gl:64 * gl + 64, 32 * k + 16 * gl:32 * k + 16 * gl + 16], 0.0)
                for gl in range(2):
                    dst = self.LC[64 * gl:64 * gl + 64, ri, pair, 32 * k + 16 * gl:32 * k + 16 * gl + 16]
                    src = BC[64 * gl:64 * gl + 64, 2 + ri, pair, :]
                    if ri == 0:
                        P.copy("act", dst, src)
                    else:
                        P.act(dst, src, AF.Identity, scale=-1.0)
        lam = self.col(C_LLAM, 4)
        P.act(self.CCOL, lam, AF.Exp, scale=-1.0)
        P.act(self.CCOL, self.CCOL, AF.Ln, bias=1.0)
        P.ts("dve", self.C2COL, self.CCOL, -16.0, ALU.mult)
        P.ts("dve", self.CCOL, self.CCOL, -8.0, ALU.mult)
        for ax in range(2):
            for c in range(4):
                for hh in range(2):
                    P.dma("pool", self.LW[64 * hh:64 * hh + 64, ax, c, 64 * hh:64 * hh + 64],
                          self.lruw[l, ax, 2 * c + hh], "lw")
        DL = A.alloc("dl", [128, 4, 64], F32)
        P.dma("sp", DL.v, self.dl[l:l + 1].partition_broadcast(128), "s5")
        PR = A.alloc("dlp", [128, 2, 64], F32)
        P.tt("dve", PR[:, 0], DL[:, 0], DL[:, 1], ALU.mult)
        P.tt("dve", PR[:, 1], DL[:, 2], DL[:, 3], ALU.mult)
        SM = D[:, 40:42]
        P.rsum(SM, PR.v)
        P.act(SM, SM, AF.Exp)
        lam_init = 0.8 - 0.6 * math.exp(-0.3 * l)
        P.tt("dve", self.NLAM, D[:, 41:42], D[:, 40:41], ALU.subtract)
        P.ts("dve", self.NLAM, self.NLAM, -lam_init, ALU.add)
        P.dma("sp", self.SUBG.v, self.subg[l:l + 1].partition_broadcast(128), "s5")
        P.ts("dve", self.SUBG.v, self.SUBG.v, 1.0 - lam_init, ALU.mult)
        self.dump("LB", self.LB[:, :, :, :])
        self.dump("DER", self.DER.v)
        A.release(m)
        m = A.mark()
        MT = A.alloc("memT", [128, 8, MEM_LEN], F32)
        MN = A.alloc("memn", [128, 8, MEM_LEN], BF16)
        SQ = [A.alloc(f"msq{i}", [128, MEM_LEN], BF16) for i in range(2)]
        RS = A.alloc("mrs", [128, MEM_LEN], F32)
        P.dma("sp", MT.v, self.memT[s], "mem")
        ps = self.bank()
        for c in range(8):
            P.act(SQ[c % 2].v, MT[:, c, :], AF.Square)
            P.mm(ps[:, 0:MEM_LEN], self.ONESB.v, SQ[c % 2].v, start=(c == 0), stop=(c == 7))
        self.rstd(RS.v, ps[:, 0:MEM_LEN])
        self.put(ps)
        for c in range(8):
            P.stt("dve", MN[:, c, :], MT[:, c, :], self.col(C_GMEM + c), RS.v, ALU.mult, ALU.mult)
        for g in range(2):
            w = self.wload(self.wkv[l][:, g * 512:(g + 1) * 512], 8, 512)
            for mi in range(4):
                ps = self.bank()
                for kc in range(8):
                    P.mm(ps[:, 0:MEM_LEN], w[:, kc, mi * 128:(mi + 1) * 128], MN[:, kc, :], start=(kc == 0), stop=(kc == 7))
                P.copy("act", self.KX[:, g * 4 + mi, :], ps[:, 0:MEM_LEN])
                self.put(ps)
        for g in range(2):
            w = self.wload(self.wkv[l][:, 1024 + g * 512:1024 + (g + 1) * 512], 8, 512)
            for kt in range(2):
                ps = self.bank()
                for kc in range(8):
                    P.mm(ps.v, MN[:, kc, kt * 128:(kt + 1) * 128], w[:, kc, :], start=(kc == 0), stop=(kc == 7))
                P.copy("act", self.VX[:, kt, g * 512:(g + 1) * 512], ps.v)
                self.put(ps)
        A.release(m)

    def mixer(self, s, l, b):
        P, A = self.P, self.A
        t0 = b * 512
        win = self.w_in[l]
        self.rmsnorm(b, C_GMIX)
        self.dump("HN", self.HN.v)
        m = A.mark()
        YG = A.alloc("YG", [128, 3, 512], BF16)
        U32 = A.alloc("U32", [128, 512], F32)
        UB = A.alloc("UB", [128, 512], BF16)
        COSs = [A.alloc(f"COS{i}", [128, 512], F32) for i in range(2)]
        SINs = [A.alloc(f"SIN{i}", [128, 512], F32) for i in range(2)]
        M1 = A.alloc("M1", [128, 512], F32)
        M2 = A.alloc("M2", [128, 512], F32)
        M3 = A.alloc("M3", [128, 512], F32)
        M4 = A.alloc("M4", [128, 512], F32)
        WINR = A.alloc("WINR", [128, 512], F32)
        WINI = A.alloc("WINI", [128, 512], F32)
        WRs = [A.alloc(f"WR{i}", [128, 512], F32) for i in range(2)]
        WIs = [A.alloc(f"WI{i}", [128, 512], F32) for i in range(2)]
        SR = A.alloc("SR", [128, 512], BF16)
        SI = A.alloc("SI", [128, 512], BF16)
        YT = A.alloc("YT", [128, 512], F32)
        wu = self.wload(win[:, O_U:O_U + 384], 8, 384)
        t512 = self.CONST[:, K_T:K_T + 512]
        THT0 = self.DER[:, 44:56]
        P.ts("dve", THT0, self.TH, float(t0), ALU.mult)
        for c in range(3):
            ps = self.bank()
            for kc in range(8):
                P.mm(ps.v, wu[:, kc, c * 128:(c + 1) * 128], self.HN[:, kc, :], start=(kc == 0), stop=(kc == 7))
            P.copy("act", U32.v, ps.v)
            P.copy("act", UB.v, ps.v)
            self.put(ps)
            yacc = self.bank()
            for pr in range(4):
                pair = 4 * c + pr
                COS, SIN, WR, WI = COSs[pair % 2], SINs[pair % 2], WRs[pair % 2], WIs[pair % 2]
                bur, bui = self.bank(), self.bank()
                P.mm(bur.v, self.LB[:, 0, pair, :], UB.v)
                P.mm(bui.v, self.LB[:, 1, pair, :], UB.v)
                P.act(M1.v, t512, AF.Identity, bias=THT0[:, pair:pair + 1], scale=self.TH[:, pair:pair + 1])
                self.sincos(COS.v, SIN.v, M1.v, M1.v, M2.v)
                P.tt("dve", M1.v, bur.v, COS.v, ALU.mult)
                P.tt("dve", M2.v, bui.v, SIN.v, ALU.mult)
                P.tt("dve", WINR.v, M1.v, M2.v, ALU.add)
                P.tt("dve", M1.v, bui.v, COS.v, ALU.mult)
                P.tt("dve", M2.v, bur.v, SIN.v, ALU.mult)
                P.tt("dve", WINI.v, M1.v, M2.v, ALU.subtract)
                self.put(bur, bui)
                rho = self.RHO[:, pair:pair + 1].bc([128, 512])
                P.scan(WR.v, rho, WINR.v, self.S5ST[:, pair, 0:1])
                P.scan(WI.v, rho, WINI.v, self.S5ST[:, pair, 1:2])
                P.copy("act", self.S5ST[:, pair, 0:1], WR[:, 511:512])
                P.copy("act", self.S5ST[:, pair, 1:2], WI[:, 511:512])
                P.tt("dve", M3.v, WR.v, COS.v, ALU.mult)
                P.tt("dve", M4.v, WI.v, SIN.v, ALU.mult)
                P.tt("dve", SR.v, M3.v, M4.v, ALU.subtract)
                P.tt("dve", M3.v, WR.v, SIN.v, ALU.mult)
                P.tt("dve", M4.v, WI.v, COS.v, ALU.mult)
                P.tt("dve", SI.v, M3.v, M4.v, ALU.add)
                P.mm(yacc.v, self.LC[:, 0, pair, :], SR.v, start=(pr == 0), stop=False)
                P.mm(yacc.v, self.LC[:, 1, pair, :], SI.v, start=False, stop=(pr == 3))
            P.stt("dve", YT.v, U32.v, self.col(C_SSMD + c), yacc.v, ALU.mult, ALU.add)
            self.put(yacc)
            P.act(YG[:, c, :], YT.v, AF.Gelu)
        wg = self.wload(self.w_glu[l], 3, 768)
        SG = M1
        for j in range(3):
            p1, p2 = self.bank(), self.bank()
            for kc in range(3):
                P.mm(p1.v, wg[:, kc, j * 128:(j + 1) * 128], YG[:, kc, :], start=(kc == 0), stop=(kc == 2))
            for kc in range(3):
                P.mm(p2.v, wg[:, kc, 384 + j * 128:384 + (j + 1) * 128], YG[:, kc, :], start=(kc == 0), stop=(kc == 2))
            P.act(SG.v, p2.v, AF.Sigmoid, bias=self.col(C_BGLU + 3 + j))
            P.stt("dve", self.YSSM[:, j, :], p1.v, self.col(C_BGLU + j), SG.v, ALU.add, ALU.mult)
            self.put(p1, p2)
        self.dump("YSSM", self.YSSM.v)
        A.release(m)
        if self.stop_after == "s5":
            return
        m = A.mark()
        Q = A.alloc("Q", [128, 4, 512], BF16)
        RC = A.alloc("RC", [128, 512], F32)
        RS = A.alloc("RS", [128, 512], F32)
        PI32 = A.alloc("PI32", [128, 512], I32)
        T1 = A.alloc("T1", [128, 512], F32)
        T2 = A.alloc("T2", [128, 512], F32)
        PT = [A.alloc(f"PT{i}", [128, 512], BF16) for i in range(16)]
        SMALL = [A.alloc(f"SMALL{i}", [128, 8], F32) for i in range(4)]
        OA = [A.alloc(f"OA{i}", [128, 128], F32) for i in range(4)]
        OB = [A.alloc(f"OB{i}", [128, 128], F32) for i in range(4)]
        P.dma("sp", PI32.v, self.pos[s:s + 1, t0:t0 + 512].partition_broadcast(128), "pos")
        P.copy("dve", T1.v, PI32.v)
        P.ts("dve", T1.v, T1.v, self.CONST[:, K_INVF:K_INVF + 1], ALU.mult, 1.0 / (2 * PI), ALU.mult)
        self.sincos(RC.v, RS.v, T1.v, T1.v, T2.v, sign=True)
        for (o_a, o_s, dst) in ((O_Q, O_QS, None), (O_K, O_KS, "k")):
            wa = self.wload(win[:, o_a:o_a + 512], 8, 512)
            wsw = self.wload(win[:, o_s:o_s + 512], 8, 512)
            for j in range(4):
                pa, pb = self.bank(), self.bank()
                for kc in range(8):
                    P.mm(pa.v, wa[:, kc, j * 128:(j + 1) * 128], self.HN[:, kc, :], start=(kc == 0), stop=(kc == 7))
                for kc in range(8):
                    P.mm(pb.v, wsw[:, kc, j * 128:(j + 1) * 128], self.HN[:, kc, :], start=(kc == 0), stop=(kc == 7))
                P.tt("dve", T1.v, pa.v, RC.v, ALU.mult)
                P.tt("dve", T2.v, pb.v, RS.v, ALU.mult)
                o = Q[:, j, :] if dst is None else self.KC[:, j, t0:t0 + 512]
                P.tt("dve", o, T1.v, T2.v, ALU.add)
                self.put(pa, pb)
        wv = self.wload(win[:, O_V:O_V + 512], 8, 512)
        for i in range(4):
            ps = self.bank()
            for kc in range(8):
                P.mm(ps.v, self.HN[:, kc, i * 128:(i + 1) * 128], wv[:, kc, :], start=(kc == 0), stop=(kc == 7))
            P.copy("act", self.VC[:, 4 * b + i, :, 0:128], View(ps, ps.ap.rearrange("p (h e) -> p h e", h=4)))
            self.put(ps)
        self.dump("Q", Q.v)
        ident = self.CONST[:, K_ID:K_ID + 128]
        LOOK = 2
        nk = 4 * (b + 1)
        for h in range(4):
            ob = [self.bank() for _ in range(4)]
            steps = [(c, kt) for c in range(2) for kt in range(nk)]

            def score(c, kt):
                i0 = max(0, kt - 4 * b)
                qlo = 128 * i0
                ps = self.bank()
                P.mm(ps[:, qlo:512], self.KC[64 * c:64 * c + 64, h, 128 * kt:128 * kt + 128],
                     Q[64 * c:64 * c + 64, h, qlo:512])
                P.act(PT[kt][:, qlo:512], ps[:, qlo:512], AF.Exp, scale=0.125)
                self.put(ps)
                if kt >= 4 * b:
                    P.tt("dve", PT[kt][:, qlo:qlo + 128], PT[kt][:, qlo:qlo + 128], self.CMB.v, ALU.mult)

            for si in range(min(LOOK, len(steps))):
                score(*steps[si])
            for si, (c, kt) in enumerate(steps):
                if si + LOOK < len(steps):
                    score(*steps[si + LOOK])
                i0 = max(0, kt - 4 * b)
                for i in range(i0, 4):
                    P.mm(ob[i][:, 130 * c:130 * c + 129], PT[kt][:, 128 * i:128 * i + 128],
                         self.VC[:, kt, h, 0:129], start=(kt == 0), stop=(kt == 4 * b + i))
            tb = self.bank()
            R4 = range(4)
            for i in R4:
                P.recip(SMALL[i][:, 0:1], ob[i][:, 128:129])
            for i in R4:
                P.recip(SMALL[i][:, 1:2], ob[i][:, 258:259])
            for i in R4:
                P.tt("dve", SMALL[i][:, 2:3], SMALL[i][:, 1:2], self.NLAM, ALU.mult)
            for i in R4:
                P.act(OA[i].v, ob[i][:, 0:128], AF.Identity, scale=SMALL[i][:, 0:1])
            for i in R4:
                P.stt("dve", OB[i].v, ob[i][:, 130:258], SMALL[i][:, 2:3], OA[i].v, ALU.mult, ALU.add)
            self.put(*ob)
            for i in R4:
                P.act(OA[i].v, OB[i].v, AF.Square)
            for i in R4:
                P.rsum(SMALL[i][:, 3:4], OA[i].v)
            for i in R4:
                P.act(SMALL[i][:, 5:6], SMALL[i][:, 3:4], AF.Sqrt, bias=self.EPSC, scale=1.0 / 128.0)
            for i in R4:
                P.recip(SMALL[i][:, 5:6], SMALL[i][:, 5:6])
            for i in R4:
                P.stt("dve", OA[i].v, OB[i].v, SMALL[i][:, 5:6], self.SUBG.v, ALU.mult, ALU.mult)
            for i in R4:
                P.tr(tb[:, 128 * i:128 * i + 128], OA[i].v, ident)
            P.copy("act", self.YATT[:, h, :], tb.v)
            self.put(tb)
        self.dump("YATT", self.YATT.v)
        A.release(m)
        if self.stop_after == "att":
            return
        m = A.mark()
        LT = []
        for p in range(2):
            LT.append(dict(
                XRH=A.alloc(f"XRH{p}", [128, 515], F32), GG=A.alloc(f"GG{p}", [128, 512], F32),
                XC=A.alloc(f"XC{p}", [128, 512], F32), XCB=A.alloc(f"XCB{p}", [128, 512], BF16),
                R=A.alloc(f"R{p}", [128, 512], F32), IG=A.alloc(f"IG{p}", [128, 512], F32),
                AA=A.alloc(f"AA{p}", [128, 512], F32), A2=A.alloc(f"A2{p}", [128, 512], F32),
                GX=A.alloc(f"GX{p}", [128, 512], F32), H=A.alloc(f"H{p}", [128, 512], F32)))
        wxr = self.wload(win[:, O_XR:O_XR + 512], 8, 512)
        wgr = self.wload(win[:, O_GR:O_GR + 512], 8, 512)
        for c in range(4):
            d_ = LT[c % 2]
            XRH, GG, XC, XCB, R, IG, AA, A2, GX, H = (d_[k] for k in ("XRH", "GG", "XC", "XCB", "R", "IG", "AA", "A2", "GX", "H"))
            px, pg = self.bank(), self.bank()
            for kc in range(8):
                P.mm(px.v, wxr[:, kc, c * 128:(c + 1) * 128], self.HN[:, kc, :], start=(kc == 0), stop=(kc == 7))
            for kc in range(8):
                P.mm(pg.v, wgr[:, kc, c * 128:(c + 1) * 128], self.HN[:, kc, :], start=(kc == 0), stop=(kc == 7))
            P.copy("act", XRH[:, 0:3], self.XHALO[:, c, :])
            P.copy("act", XRH[:, 3:515], px.v)
            P.act(GG.v, pg.v, AF.Gelu)
            self.put(px, pg)
            P.copy("act", self.XHALO[:, c, :], XRH[:, 512:515])
            cw = C_LCW + 4 * c
            P.ts("dve", XC.v, XRH[:, 3:515], self.col(cw + 3), ALU.mult, self.col(C_LCB + c), ALU.add)
            for j in range(3):
                P.stt("dve", XC.v, XRH[:, j:j + 512], self.col(cw + j), XC.v, ALU.mult, ALU.add)
            P.copy("dve", XCB.v, XC.v)
            pa, pi = self.bank(), self.bank()
            P.mm(pa.v, self.LW[:, 0, c, :], XCB.v)
            P.mm(pi.v, self.LW[:, 1, c, :], XCB.v)
            P.act(R.v, pa.v, AF.Sigmoid, bias=self.col(C_LBA + c))
            P.act(IG.v, pi.v, AF.Sigmoid, bias=self.col(C_LBX + c))
            self.put(pa, pi)
            P.act(AA.v, R.v, AF.Exp, scale=self.CCOL[:, c:c + 1])
            P.act(A2.v, R.v, AF.Exp, scale=self.C2COL[:, c:c + 1])
            P.act(A2.v, A2.v, AF.Sqrt, bias=1.0, scale=-1.0)
            P.tt("dve", GX.v, IG.v, XC.v, ALU.mult)
            P.tt("dve", GX.v, GX.v, A2.v, ALU.mult)
            P.scan(H.v, AA.v, GX.v, self.LST[:, c:c + 1])
            P.copy("act", self.LST[:, c:c + 1], H[:, 511:512])
            P.tt("dve", self.YLRU[:, c, :], H.v, GG.v, ALU.mult)
        self.dump("YLRU", self.YLRU.v)
        A.release(m)
        if self.stop_after == "lru":
            return
        m = A.mark()
        MG = A.alloc("MG", [128, 8, 512], BF16)
        Gs = [[A.alloc(f"G{p}{i}", [128, 512], F32) for i in range(3)] for p in range(2)]
        TTs = [[A.alloc(f"TT{p}{i}", [128, 512], F32) for i in range(2)] for p in range(2)]
        ys = [(self.YSSM, 3), (self.YATT, 4), (self.YLRU, 4)]
        for mo in range(8):
            wgt = self.wload(win[:, O_G + mo * 384:O_G + (mo + 1) * 384], 8, 384)
            if mo % 2 == 0:
                wbr = self.wload(self.w_br[l][:, mo * 128:(mo + 2) * 128], 11, 256)
            G, TT = Gs[mo % 2], TTs[mo % 2]
            gps = [self.bank() for _ in range(3)]
            bps = [self.bank() for _ in range(3)]
            for i in range(3):
                for kc in range(8):
                    P.mm(gps[i].v, wgt[:, kc, i * 128:(i + 1) * 128], self.HN[:, kc, :], start=(kc == 0), stop=(kc == 7))
            k0 = 0
            for i in range(3):
                yt, nkc = ys[i]
                for kc in range(nkc):
                    P.mm(bps[i].v, wbr[:, k0 + kc, (mo % 2) * 128:(mo % 2) * 128 + 128], yt[:, kc, :],
                         start=(kc == 0), stop=(kc == nkc - 1))
                k0 += nkc
            for i in range(3):
                P.act(G[i].v, gps[i].v, AF.Sigmoid)
            self.put(*gps)
            P.tt("dve", TT[0].v, G[0].v, bps[0].v, ALU.mult)
            P.tt("dve", TT[1].v, G[1].v, bps[1].v, ALU.mult)
            P.tt("dve", TT[0].v, TT[0].v, TT[1].v, ALU.add)
            P.tt("dve", TT[1].v, G[2].v, bps[2].v, ALU.mult)
            P.tt("dve", MG[:, mo, :], TT[0].v, TT[1].v, ALU.add)
            self.put(*bps)
        self.proj_residual(self.w_out[l], MG, 8, t0)
        A.release(m)

    def proj_residual(self, w, act, nkc, t0):
        P = self.P
        if nkc * 512 <= 4096:
            groups = [(g * 512, 512) for g in range(2)]
        else:
            groups = [(g * 128, 128) for g in range(8)]
        for (m0, mw) in groups:
            wt = self.wload(w[:, m0:m0 + mw], nkc, mw)
            for mi in range(mw // 128):
                ps = self.bank()
                for kc in range(nkc):
                    P.mm(ps.v, wt[:, kc, mi * 128:(mi + 1) * 128], act[:, kc, :], start=(kc == 0), stop=(kc == nkc - 1))
                mo = m0 // 128 + mi
                P.tt("dve", self.X[:, mo, t0:t0 + 512], self.X[:, mo, t0:t0 + 512], ps.v, ALU.add)
                self.put(ps)

    def xattn(self, s, l, b):
        P, A = self.P, self.A
        t0 = b * 512
        self.rmsnorm(b, C_GXA)
        m = A.mark()
        QX = A.alloc("QX", [128, 8, 512], BF16)
        OX = A.alloc("OX", [128, 8, 512], BF16)
        PXs = [[A.alloc(f"PX{p}{i}", [128, 512], BF16) for i in range(2)] for p in range(2)]
        RDs = [A.alloc(f"RD{p}", [128, 512], F32) for p in range(2)]
        for g in range(2):
            wt = self.wload(self.wq[l][:, g * 512:(g + 1) * 512], 8, 512)
            for mi in range(4):
                ps = self.bank()
                for kc in range(8):
                    P.mm(ps.v, wt[:, kc, mi * 128:(mi + 1) * 128], self.HN[:, kc, :], start=(kc == 0), stop=(kc == 7))
                P.copy("act", QX[:, g * 4 + mi, :], ps.v)
                self.put(ps)
        for h in range(4):
            PX, RD = PXs[h % 2], RDs[h % 2]
            for kt in range(2):
                ps = self.bank()
                for dc in range(2):
                    P.mm(ps.v, self.KX[:, 2 * h + dc, 128 * kt:128 * kt + 128], QX[:, 2 * h + dc, :],
                         start=(dc == 0), stop=(dc == 1))
                P.act(PX[kt].v, ps.v, AF.Exp, scale=1.0 / 16.0)
                self.put(ps)
            pd = self.bank()
            for kt in range(2):
                P.mm(pd.v, self.ONE1.v, PX[kt].v, start=(kt == 0), stop=(kt == 1))
            P.recip(RD.v, pd.v)
            self.put(pd)
            for ec in range(2):
                po = self.bank()
                for kt in range(2):
                    P.mm(po.v, self.VX[:, kt, (2 * h + ec) * 128:(2 * h + ec) * 128 + 128], PX[kt].v,
                         start=(kt == 0), stop=(kt == 1))
                P.tt("dve", OX[:, 2 * h + ec, :], po.v, RD.v, ALU.mult)
                self.put(po)
        self.proj_residual(self.wo[l], OX, 8, t0)
        A.release(m)

    def ffn(self, s, l, b):
        P, A = self.P, self.A
        t0 = b * 512
        self.rmsnorm(b, C_GFFN)
        m = A.mark()
        HF = A.alloc("HF", [128, NFF, 512], BF16)
        UR = [A.alloc(f"UR{i}", [128, 514], F32) for i in range(2)]
        ACC = [[A.alloc(f"ACC{p}{i}", [128, 512], F32) for i in range(2)] for p in range(2)]
        SG = A.alloc("SGF", [128, 512], F32)

        def stage_b(f):
            P.act(SG.v, ACC[f % 2][1].v, AF.Silu)
            P.tt("dve", HF[:, f, :], ACC[f % 2][0].v, SG.v, ALU.mult)

        for f in range(NFF):
            if f % 2 == 0:
                wt = self.wload(self.w_up[l][:, f * 256:(f + 2) * 256], 8, 512)
            pss = [self.bank(), self.bank()]
            for vg in range(2):
                c0 = (f % 2) * 256 + vg * 128
                for kc in range(8):
                    P.mm(pss[vg].v, wt[:, kc, c0:c0 + 128], self.HN[:, kc, :], start=(kc == 0), stop=(kc == 7))
            for vg in range(2):
                idx = 2 * f + vg
                ur, acc, ps = UR[vg], ACC[f % 2][vg], pss[vg]
                cw = C_FCW + idx * 3
                P.copy("act", ur[:, 0:2], self.CF[:, idx, :])
                P.copy("act", ur[:, 2:514], ps.v)
                P.act(acc.v, ps.v, AF.Identity, bias=self.col(C_FCB + idx), scale=self.col(cw + 2))
                P.copy("act", self.CF[:, idx, :], ur[:, 512:514])
                P.stt("dve", acc.v, ur[:, 1:513], self.col(cw + 1), acc.v, ALU.mult, ALU.add)
                P.stt("dve", acc.v, ur[:, 0:512], self.col(cw + 0), acc.v, ALU.mult, ALU.add)
            self.put(*pss)
            if f >= 1:
                stage_b(f - 1)
        stage_b(NFF - 1)
        self.proj_residual(self.w_down[l], HF, NFF, t0)
        A.release(m)

    def final(self, s, b):
        P, A = self.P, self.A
        t0 = b * 512
        m = A.mark()
        SQ = [A.alloc(f"fSQ{i}", [128, 512], BF16) for i in range(2)]
        RSTD = A.alloc("fRSTD", [128, 512], F32)
        OT = A.alloc("OT", [128, 8, 512], F32)
        ps = self.bank()
        for c in range(8):
            P.act(SQ[c % 2].v, self.X[:, c, t0:t0 + 512], AF.Square)
            P.mm(ps.v, self.ONESB.v, SQ[c % 2].v, start=(c == 0), stop=(c == 7))
        self.rstd(RSTD.v, ps.v)
        self.put(ps)
        for c in range(8):
            P.stt("dve", OT[:, c, :], self.X[:, c, t0:t0 + 512], self.FNG[:, c:c + 1], RSTD.v, ALU.mult, ALU.mult)
        P.dma("sp", self.outT[s][:, :, t0:t0 + 512], OT.v, "out")
        A.release(m)


def _colT(v, n):
    return np.ascontiguousarray(np.asarray(v).reshape(n, 128).T)


def host_layout(inp):
    f32 = np.float32
    L = DEPTH
    g = {k: np.asarray(v) for k, v in inp.items()}
    consts = np.zeros((128, NCONST), f32)
    consts[:, K_ID:K_ID + 128] = np.eye(128, dtype=f32)
    kk, qq = np.meshgrid(np.arange(128), np.arange(128), indexing="ij")
    consts[:, K_CM:K_CM + 128] = (kk <= qq).astype(f32)
    consts[:, K_T:K_T + 512] = np.arange(512, dtype=f32)[None, :]
    inv = np.exp(np.float32(-math.log(10000.0)) * np.arange(0, 64, 2, dtype=f32) / np.float32(64)).astype(f32)
    rows = np.arange(128)
    consts[:, K_INVF] = inv[rows % 32]
    sign = np.where((rows % 64) < 32, -1.0, 1.0).astype(f32)
    consts[:, K_SIGN] = np.float32(2 * PI) * sign
    consts[:, K_NPS] = (-np.float32(PI)) * sign
    cols = np.zeros((L, 128, NCOL), f32)
    for l in range(L):
        cols[l, :, C_GMIX:C_GMIX + 8] = _colT(g["norm_mix_g"][l], 8)
        cols[l, :, C_GXA:C_GXA + 8] = _colT(g["norm_xattn_g"][l], 8)
        cols[l, :, C_GMEM:C_GMEM + 8] = _colT(g["norm_mem_g"][l], 8)
        cols[l, :, C_GFFN:C_GFFN + 8] = _colT(g["norm_ffn_g"][l], 8)
        cols[l, :, C_SSMD:C_SSMD + 3] = _colT(g["ssm_d"][l], 3)
        cols[l, :, C_BGLU:C_BGLU + 6] = _colT(g["ssm_b_glu"][l], 6)
        cw = g["lru_conv_w"][l]
        cols[l, :, C_LCW:C_LCW + 16] = cw.reshape(4, 4, 128).transpose(2, 1, 0).reshape(128, 16)
        cols[l, :, C_LCB:C_LCB + 4] = _colT(g["lru_conv_b"][l], 4)
        cols[l, :, C_LBA:C_LBA + 4] = _colT(g["lru_ba"][l], 4)
        cols[l, :, C_LBX:C_LBX + 4] = _colT(g["lru_bx"][l], 4)
        cols[l, :, C_LLAM:C_LLAM + 4] = _colT(g["lru_lambda"][l], 4)
        fw = g["ffn_conv_w"][l].reshape(3, 2, NFF, 128)
        cols[l, :, C_FCW:C_FCW + 132] = fw.transpose(3, 2, 1, 0).reshape(128, 132)
        fb = g["ffn_conv_b"][l].reshape(2, NFF, 128)
        cols[l, :, C_FCB:C_FCB + 44] = fb.transpose(2, 1, 0).reshape(128, 44)
        cols[l, :, C_LR:C_LR + 12] = g["ssm_lambda_re"][l].reshape(12, 2, 64).transpose(1, 2, 0).reshape(128, 12)
        cols[l, :, C_LI:C_LI + 12] = g["ssm_lambda_im"][l].reshape(12, 2, 64).transpose(1, 2, 0).reshape(128, 12)
        cols[l, :, C_LS:C_LS + 12] = np.repeat(g["ssm_log_step"][l].reshape(12, 2, 1), 64, axis=2).transpose(1, 2, 0).reshape(128, 12)
    w = g["w_in"]
    offs = np.cumsum([0, 384, 512, 512, 512, 512, 512, 3072])
    u, q, k, v, xr, gr, gates = [w[:, :, offs[i]:offs[i + 1]] for i in range(7)]

    def swap_halves(a):
        s = a.shape
        a = a.reshape(s[0], s[1], 8, 2, 32)
        return a[:, :, :, ::-1, :].reshape(s)
    gates_re = gates.reshape(L, 1024, 3, 8, 128).transpose(0, 1, 3, 2, 4).reshape(L, 1024, 3072)
    w_in_ext = np.concatenate([u, q, swap_halves(q), k, swap_halves(k), v, xr, gr, gates_re], axis=2)
    w_br = np.concatenate([g["w_br_ssm"], g["w_br_attn"], g["w_br_lru"]], axis=1)
    w_up = g["ffn_w_up"].reshape(L, 1024, 2, NFF, 128).transpose(0, 1, 3, 2, 4).reshape(L, 1024, 2 * D_FF)

    def s5lay(re, im, bside):
        out = np.zeros((L, 128, 2, 12, 16), f32)
        for i, a in enumerate((re, im)):
            if bside:
                out[:, :, i] = a.reshape(L, 12, 2, 64, 16).transpose(0, 2, 3, 1, 4).reshape(L, 128, 12, 16)
            else:
                out[:, :, i] = a.reshape(L, 12, 2, 16, 64).transpose(0, 2, 4, 1, 3).reshape(L, 128, 12, 16)
        return out
    shared = {
        "consts": consts, "cols": cols, "fng": _colT(g["final_norm_g"], 8).astype(f32),
        "w_in": np.ascontiguousarray(w_in_ext), "w_glu": g["ssm_w_glu"], "w_br": np.ascontiguousarray(w_br),
        "w_out": g["w_out"], "wq": g["xattn_wq"], "wkv": g["xattn_wkv"], "wo": g["xattn_wo"],
        "w_up": np.ascontiguousarray(w_up), "w_down": g["ffn_w_down"],
        "s5b": s5lay(g["ssm_b_re"], g["ssm_b_im"], True), "s5c": s5lay(g["ssm_c_re"], g["ssm_c_im"], False),
        "lruw": np.ascontiguousarray(np.stack([g["lru_wa"], g["lru_wx"]], axis=1)),
        "dl": np.ascontiguousarray(np.stack([g["diff_lq1"], g["diff_lk1"], g["diff_lq2"], g["diff_lk2"]], axis=1)),
        "subg": g["diff_subln_g"],
    }
    shared = {k: np.ascontiguousarray(v.astype(f32)) for k, v in shared.items()}
    x, mem, pos = g["x"], g["mem"], g["positions"]
    in_maps = []
    for c in range(8):
        xs = x[2 * c:2 * c + 2]
        ms = mem[2 * c:2 * c + 2]
        d = dict(shared)
        d["xT"] = np.ascontiguousarray(xs.reshape(2, SEQ, 8, 128).transpose(0, 3, 2, 1)).astype(f32)
        d["memT"] = np.ascontiguousarray(ms.reshape(2, MEM_LEN, 8, 128).transpose(0, 3, 2, 1)).astype(f32)
        d["pos"] = np.ascontiguousarray(pos[2 * c:2 * c + 2]).astype(np.int32)
        in_maps.append(d)
    return in_maps


_CACHE = {}


def kernel(**inputs):
    in_maps = host_layout(inputs)
    if "nc" not in _CACHE:
        _CACHE["nc"] = Builder().nc
    nc = _CACHE["nc"]
    res = run_bass_kernel_spmd(nc, in_maps, core_ids=list(range(8)))
    out = np.zeros((16, SEQ, D_MODEL), np.float32)
    for c in range(8):
        o = np.asarray(res.results[c]["outT"])
        out[2 * c:2 * c + 2] = o.transpose(0, 3, 2, 1).reshape(2, SEQ, D_MODEL)
    return out
```

```python
import math
import contextlib
import numpy as np
import concourse.bass as bass
import concourse.mybir as mybir
from concourse.bass_utils import run_bass_kernel_spmd

F32 = mybir.dt.float32
BF16 = mybir.dt.bfloat16
I32 = mybir.dt.int32
AF = mybir.ActivationFunctionType
ALU = mybir.AluOpType
AX = mybir.AxisListType

D_MODEL = 1024
SEQ = 2048
DEPTH = 4
MEM_LEN = 256
EPS = 1e-6
D_FF = 2816
NFF = 22
PI = math.pi


class View:
    __slots__ = ("t", "ap")

    def __init__(self, t, ap):
        self.t = t
        self.ap = ap

    def __getitem__(self, idx):
        return View(self.t, self.ap[idx])

    def bc(self, shape):
        return View(self.t, self.ap.to_broadcast(shape))


class T:
    __slots__ = ("ap", "name", "last_w", "readers", "lo", "hi")

    def __init__(self, ap, name="", lo=0, hi=0):
        self.ap = ap
        self.name = name
        self.last_w = None
        self.readers = []
        self.lo = lo
        self.hi = hi

    def __getitem__(self, idx):
        return View(self, self.ap[idx])

    @property
    def v(self):
        return View(self, self.ap)


def _ap(x):
    return x.ap if isinstance(x, View) else x


class Prog:
    ENGS = ("pe", "act", "dve", "pool", "sp")

    def __init__(self, nc):
        self.nc = nc
        self.streams = {e: [] for e in self.ENGS}
        self.dma_keys = {}

    def _collect(self, eng, reads, writes):
        deps = set()
        for t in reads:
            d = t.last_w
            if d is not None:
                if not (d[0] == "e" and d[1] == eng and eng == "pe"):
                    deps.add(d)
        for t in writes:
            d = t.last_w
            if d is not None and not (d[0] == "e" and d[1] == eng):
                deps.add(d)
            for d in t.readers:
                if not (d[0] == "e" and d[1] == eng):
                    deps.add(d)
        return deps

    def op(self, eng, fn, reads=(), writes=()):
        reads = [v.t for v in reads if isinstance(v, View)]
        writes = [v.t for v in writes if isinstance(v, View)]
        deps = self._collect(eng, reads, writes)
        seq = len(self.streams[eng])
        me = ("e", eng, seq)
        self.streams[eng].append({"fn": fn, "deps": deps, "dma": None})
        for t in reads:
            t.readers.append(me)
        for t in writes:
            t.last_w = me
            t.readers = []
        return me

    def dma(self, q, out, in_, key, **kw):
        reads = [in_.t] if isinstance(in_, View) else []
        writes = [out.t] if isinstance(out, View) else []
        if writes:
            key = "t_" + writes[0].name
        deps = self._collect("dma", reads, writes)
        deps = set(d for d in deps if not (d[0] == "d" and d[1] == key and writes))
        cnt = self.dma_keys.get(key, 0) + 1
        self.dma_keys[key] = cnt
        me = ("d", key, cnt * 16)
        o, i = _ap(out), _ap(in_)
        self.streams[q].append({
            "fn": (lambda e, o=o, i=i, kw=kw: e.dma_start(out=o, in_=i, **kw)),
            "deps": deps, "dma": me})
        for t in reads:
            t.readers.append(me)
        for t in writes:
            t.last_w = me
            t.readers = []
        return me

    def wait_all_dma(self, q, keys):
        deps = set(("d", k, self.dma_keys[k] * 16) for k in keys if k in self.dma_keys)
        self.streams[q].append({"fn": None, "deps": deps, "dma": None})

    def mm(self, out, lhsT, rhs, start=True, stop=True):
        return self.op("pe", lambda e: e.matmul(out.ap, lhsT.ap, rhs.ap, start=start, stop=stop),
                       reads=[lhsT, rhs], writes=[out])

    def tr(self, out, in_, ident):
        return self.op("pe", lambda e: e.transpose(out.ap, in_.ap, ident.ap), reads=[in_, ident], writes=[out])

    def act(self, out, in_, func, bias=None, scale=None, accum=None):
        kw = {}
        if bias is not None:
            kw["bias"] = _ap(bias)
        if scale is not None:
            kw["scale"] = _ap(scale)
        if accum is not None:
            kw["accum_out"] = _ap(accum)
        w = [out] + ([accum] if accum is not None else [])
        return self.op("act", lambda e: e.activation(out=out.ap, in_=in_.ap, func=func, **kw),
                       reads=[in_, bias, scale], writes=w)

    def tt(self, eng, out, in0, in1, op):
        return self.op(eng, lambda e: e.tensor_tensor(out=out.ap, in0=in0.ap, in1=in1.ap, op=op),
                       reads=[in0, in1], writes=[out])

    def ts(self, eng, out, in0, s1, op0, s2=None, op1=None):
        kw = {}
        if op1 is not None:
            kw["op1"] = op1
        return self.op(eng, lambda e: e.tensor_scalar(out=out.ap, in0=in0.ap, scalar1=_ap(s1), scalar2=_ap(s2),
                                                      op0=op0, **kw),
                       reads=[in0, s1, s2], writes=[out])

    def stt(self, eng, out, in0, scalar, in1, op0, op1):
        return self.op(eng, lambda e: e.scalar_tensor_tensor(out=out.ap, in0=in0.ap, scalar=_ap(scalar),
                                                             in1=in1.ap, op0=op0, op1=op1),
                       reads=[in0, scalar, in1], writes=[out])

    def scan(self, out, d0, d1, init, op0=ALU.mult, op1=ALU.add):
        return self.op("dve", lambda e: e.tensor_tensor_scan(out=out.ap, data0=d0.ap, data1=d1.ap,
                                                             initial=_ap(init), op0=op0, op1=op1),
                       reads=[d0, d1, init], writes=[out])

    def copy(self, eng, out, in_):
        if eng == "act":
            return self.op("act", lambda e: e.copy(out=out.ap, in_=in_.ap), reads=[in_], writes=[out])
        return self.op(eng, lambda e: e.tensor_copy(out=out.ap, in_=in_.ap), reads=[in_], writes=[out])

    def memset(self, eng, out, val):
        return self.op(eng, lambda e: e.memset(out.ap, val), writes=[out])

    def recip(self, out, in_):
        return self.op("dve", lambda e: e.reciprocal(out=out.ap, in_=in_.ap), reads=[in_], writes=[out])

    def rsum(self, out, in_):
        return self.op("dve", lambda e: e.reduce_sum(out=out.ap, in_=in_.ap, axis=AX.X), reads=[in_], writes=[out])

    def emit(self):
        nc = self.nc
        needed = {e: set() for e in self.ENGS}
        for e in self.ENGS:
            for ins in self.streams[e]:
                for d in ins["deps"]:
                    if d[0] == "e":
                        needed[d[1]].add(d[2])
        cum = {}
        for e in self.ENGS:
            c = 0
            arr = []
            nd = needed[e]
            for i in range(len(self.streams[e])):
                if i in nd:
                    c += 1
                arr.append(c)
            cum[e] = arr
        with contextlib.ExitStack() as st:
            esem = {e: st.enter_context(nc.semaphore("s_" + e)) for e in self.ENGS}
            dsem = {k: st.enter_context(nc.semaphore("d_" + str(k))) for k in self.dma_keys}
            block = st.enter_context(nc.Block())
            streams = self.streams

            def run_stream(e, eng):
                waited = {}
                nd = needed[e]
                for i, ins in enumerate(streams[e]):
                    w = {}
                    for d in ins["deps"]:
                        if d[0] == "e":
                            k = ("e", d[1])
                            v = cum[d[1]][d[2]]
                        else:
                            k = ("d", d[1])
                            v = d[2]
                        if v > w.get(k, 0):
                            w[k] = v
                    for k, v in w.items():
                        if waited.get(k, 0) >= v:
                            continue
                        waited[k] = v
                        eng.wait_ge(esem[k[1]] if k[0] == "e" else dsem[k[1]], v)
                    if ins["fn"] is None:
                        continue
                    bi = ins["fn"](eng)
                    if ins["dma"] is not None:
                        bi.then_inc(dsem[ins["dma"][1]], 16)
                    elif i in nd:
                        bi.then_inc(esem[e], 1)

            @block.tensor
            def _(eng):
                run_stream("pe", eng)

            @block.scalar
            def _(eng):
                run_stream("act", eng)

            @block.vector
            def _(eng):
                run_stream("dve", eng)

            @block.gpsimd
            def _(eng):
                run_stream("pool", eng)

            @block.sync
            def _(eng):
                run_stream("sp", eng)


class Arena:
    def __init__(self, nc, limit):
        self.nc = nc
        self.limit = limit
        self.top = 18432
        self.live = []
        self.dead = []
        self.n = 0

    def alloc(self, name, shape, dt):
        esz = 2 if dt == BF16 else 4
        nbytes = int(np.prod(shape[1:])) * esz
        nbytes = (nbytes + 31) // 32 * 32
        off = self.top
        self.top += nbytes
        assert off + nbytes <= self.limit, f"SBUF overflow at {name}: {off + nbytes}"
        self.n += 1
        h = self.nc.alloc_sbuf_tensor_at(f"{name}_{self.n}", list(shape), dt, offset=off)
        t = T(h.ap(), name, off, off + nbytes)
        keep = []
        for (lo, hi, deps) in self.dead:
            if lo < t.hi and t.lo < hi:
                t.readers.extend(deps)
                if t.lo <= lo and hi <= t.hi:
                    continue
            keep.append((lo, hi, deps))
        self.dead = keep
        self.live.append(t)
        return t

    def mark(self):
        return self.top

    def release(self, m):
        keep = []
        for t in self.live:
            if t.lo >= m:
                deps = list(t.readers)
                if t.last_w is not None:
                    deps.append(t.last_w)
                if deps:
                    self.dead.append((t.lo, t.hi, deps))
            else:
                keep.append(t)
        self.live = keep
        self.top = m


C_GMIX, C_GXA, C_GMEM, C_GFFN = 0, 8, 16, 24
C_SSMD, C_BGLU = 32, 35
C_LCW, C_LCB, C_LBA, C_LBX, C_LLAM = 41, 57, 61, 65, 69
C_FCW, C_FCB = 73, 205
C_LR, C_LI, C_LS = 249, 261, 273
NCOL = 288
K_ID, K_CM, K_T, K_INVF, K_SIGN, K_NPS = 0, 128, 256, 768, 769, 770
NCONST = 776
O_U, O_Q, O_QS, O_K, O_KS, O_V, O_XR, O_GR, O_G = 0, 384, 896, 1408, 1920, 2432, 2944, 3456, 3968
W_IN_EXT = 7040


class Builder:
    def __init__(self, nlayers=DEPTH, nseq=2, nblk=4, dbg=(), stop_after=None):
        self.L = nlayers
        self.NS = nseq
        self.NB = nblk
        self.dbg_names = dict(dbg)
        self.stop_after = stop_after
        nc = bass.Bass("TRN2", target_bir_lowering=False)
        self.nc = nc
        self.P = Prog(nc)
        L = nlayers

        def din(name, shape, dt=F32):
            return nc.dram_tensor(name, list(shape), dt, kind="ExternalInput").ap()
        self.xT = din("xT", [2, 128, 8, SEQ])
        self.memT = din("memT", [2, 128, 8, MEM_LEN])
        self.pos = din("pos", [2, SEQ], I32)
        self.consts = din("consts", [128, NCONST])
        self.cols = din("cols", [L, 128, NCOL])
        self.fng = din("fng", [128, 8])
        self.w_in = din("w_in", [L, 1024, W_IN_EXT])
        self.w_glu = din("w_glu", [L, 384, 768])
        self.w_br = din("w_br", [L, 1408, 1024])
        self.w_out = din("w_out", [L, 1024, 1024])
        self.wq = din("wq", [L, 1024, 1024])
        self.wkv = din("wkv", [L, 1024, 2048])
        self.wo = din("wo", [L, 1024, 1024])
        self.w_up = din("w_up", [L, 1024, 2 * D_FF])
        self.w_down = din("w_down", [L, D_FF, 1024])
        self.s5b = din("s5b", [L, 128, 2, 12, 16])
        self.s5c = din("s5c", [L, 128, 2, 12, 16])
        self.lruw = din("lruw", [L, 2, 8, 64, 64])
        self.dl = din("dl", [L, 4, 64])
        self.subg = din("subg", [L, 128])
        self.outT = nc.dram_tensor("outT", [2, 128, 8, SEQ], F32, kind="ExternalOutput").ap()
        self.dbg_out = {n: nc.dram_tensor("dbg_" + n, list(s), F32, kind="ExternalOutput").ap()
                        for n, s in self.dbg_names.items()}
        self.A = Arena(nc, 229376)
        self.banks = [T(nc.alloc_psum_tensor(f"psb{i}", [128, 512], F32).ap(), f"psb{i}") for i in range(8)]
        self.free_banks = list(range(8))
        self.wslot_i = 0
        self.build()
        self.P.emit()

    def bank(self):
        i = self.free_banks.pop(0)
        return self.banks[i]

    def put(self, *bs):
        for b in bs:
            self.free_banks.append(self.banks.index(b))

    def wload(self, src_ap, kc, mw):
        assert kc * mw <= 4096
        i = self.wslot_i
        self.wslot_i = (i + 1) % len(self.WS)
        slot = self.WS[i]
        v = View(slot, slot.ap[:, 0:kc * mw].rearrange("p (k m) -> p k m", k=kc))
        self.P.dma("pool", v, src_ap.rearrange("(k p) m -> p k m", p=128), f"w{i}")
        return v

    def dump(self, name, view):
        if name in self.dbg_out:
            self.P.dma("pool", self.dbg_out[name], view, "dbg")

    def build(self):
        P, A = self.P, self.A
        al = A.alloc
        self.X = al("X", [128, 8, SEQ], F32)
        self.CONST = al("CONST", [128, NCONST], F32)
        self.COLS = al("COLS", [128, NCOL], F32)
        self.FNG = al("FNG", [128, 8], F32)
        self.ONESB = al("ONESB", [128, 128], BF16)
        self.ONE1 = al("ONE1", [128, 128], BF16)
        self.CMB = al("CMB", [128, 128], BF16)
        self.WS = [al(f"WS{i}", [128, 4096], BF16) for i in range(3)]
        self.LB = al("LB", [128, 2, 12, 128], BF16)
        self.LC = al("LC", [128, 2, 12, 128], BF16)
        self.LW = al("LW", [128, 2, 4, 128], BF16)
        self.DD = al("DD", [128, 3, 128], BF16)
        self.DER = al("DER", [128, 64], F32)
        self.SUBG = al("SUBG", [128, 128], F32)
        self.KX = al("KX", [128, 8, MEM_LEN], BF16)
        self.VX = al("VX", [128, 2, 1024], BF16)
        self.KC = al("KC", [128, 4, SEQ], BF16)
        self.VC = al("VC", [128, 16, 4, 130], BF16)
        self.CF = al("CF", [128, 44, 2], F32)
        self.XHALO = al("XHALO", [128, 4, 3], F32)
        self.S5ST = al("S5ST", [128, 12, 2], F32)
        self.LST = al("LST", [128, 4], F32)
        self.HN = al("HN", [128, 8, 512], BF16)
        self.YSSM = al("YSSM", [128, 3, 512], BF16)
        self.YATT = al("YATT", [128, 4, 512], BF16)
        self.YLRU = al("YLRU", [128, 4, 512], BF16)
        self.scr0 = A.mark()
        print("persistent bytes", self.scr0)

        P.dma("sp", self.CONST.v, self.consts, "c0")
        P.dma("sp", self.FNG.v, self.fng, "c0")
        P.memset("dve", self.ONESB.v, 1.0 / 1024.0)
        P.memset("dve", self.ONE1.v, 1.0)
        P.copy("dve", self.CMB.v, self.CONST[:, K_CM:K_CM + 128])
        P.memset("dve", self.VC.v, 1.0)
        P.memset("dve", self.LW.v, 0.0)
        P.memset("dve", self.LB.v, 0.0)
        P.memset("dve", self.LC.v, 0.0)

        for s in range(self.NS):
            P.dma("sp", self.X.v, self.xT[s], "x")
            for l in range(self.L):
                self.layer_setup(s, l)
                if self.stop_after == "setup":
                    break
                for b in range(self.NB):
                    self.mixer(s, l, b)
                    if self.stop_after in ("s5", "att", "lru", "merge"):
                        continue
                    self.xattn(s, l, b)
                    if self.stop_after == "xattn":
                        continue
                    self.ffn(s, l, b)
            if self.stop_after is None:
                for b in range(self.NB):
                    self.final(s, b)
            else:
                self.dump("X", self.X.v)
        P.wait_all_dma("sp", ["out", "dbg"])

    def col(self, c, n=1):
        return self.COLS[:, c:c + n]

    def rmsnorm(self, blk, gc, gt=None):
        P, A = self.P, self.A
        m = A.mark()
        SQ = [A.alloc(f"SQ{i}", [128, 512], BF16) for i in range(2)]
        RSTD = A.alloc("RSTD", [128, 512], F32)
        t0 = blk * 512
        ps = self.bank()
        for c in range(8):
            sq = SQ[c % 2]
            P.act(sq.v, self.X[:, c, t0:t0 + 512], AF.Square)
            P.mm(ps.v, self.ONESB.v, sq.v, start=(c == 0), stop=(c == 7))
        self.rstd(RSTD.v, ps.v)
        self.put(ps)
        gt = gt if gt is not None else self.COLS
        for c in range(8):
            P.stt("dve", self.HN[:, c, :], self.X[:, c, t0:t0 + 512], gt[:, gc + c:gc + c + 1], RSTD.v,
                  ALU.mult, ALU.mult)
        A.release(m)
        return RSTD

    def rstd(self, out, in_, scale=1.0):
        self.P.act(out, in_, AF.Sqrt, bias=self.EPSC, scale=scale)
        self.P.recip(out, out)

    def sincos(self, out_c, out_s, tcyc, tA, tB, sign=False):
        P = self.P
        MAGIC = 12582912.0
        P.ts("dve", tB, tcyc, MAGIC, ALU.add, MAGIC, ALU.subtract)
        P.tt("dve", tB, tcyc, tB, ALU.subtract)
        P.stt("dve", tA, tB, -1.0, tB, ALU.mult, ALU.max)
        if sign:
            P.act(out_s, tB, AF.Sin, scale=self.CONST[:, K_SIGN:K_SIGN + 1])
        else:
            P.act(out_s, tB, AF.Sin, scale=2 * PI)
        P.act(out_c, tA, AF.Sin, bias=self.HPI, scale=-2 * PI)

    def layer_setup(self, s, l):
        P, A = self.P, self.A
        m = A.mark()
        P.dma("sp", self.COLS.v, self.cols[l], "cols")
        for t in (self.S5ST, self.LST, self.XHALO, self.CF):
            P.memset("dve", t.v, 0.0)
        D = self.DER
        self.HPI = D[:, 60:61]
        self.EPSC = D[:, 61:62]
        P.memset("dve", self.HPI, PI / 2)
        P.memset("dve", self.EPSC, EPS)
        TH, RHO = D[:, 0:12], D[:, 12:24]
        self.TH, self.RHO = TH, RHO
        self.CCOL, self.C2COL, self.NLAM = D[:, 24:28], D[:, 28:32], D[:, 32:33]
        W = A.alloc("s5w", [128, 12, 12], F32)
        lr, li, ls = self.col(C_LR, 12), self.col(C_LI, 12), self.col(C_LS, 12)
        dt, a_, tmpa, tmpb = W[:, 0, :], W[:, 1, :], W[:, 2, :], W[:, 3, :]
        cs, sn, abr, abi = W[:, 4, :], W[:, 5, :], W[:, 6, :], W[:, 7, :]
        den, fr, fi, nr = W[:, 8, :], W[:, 9, :], W[:, 10, :], W[:, 11, :]
        P.act(dt, ls, AF.Exp)
        P.tt("dve", a_, lr, dt, ALU.mult)
        P.act(RHO, a_, AF.Exp)
        P.tt("dve", TH, li, dt, ALU.mult)
        P.ts("dve", TH, TH, 1.0 / (2 * PI), ALU.mult)
        self.sincos(cs, sn, TH, tmpa, tmpb)
        P.tt("dve", abr, RHO, cs, ALU.mult)
        P.tt("dve", abi, RHO, sn, ALU.mult)
        P.tt("dve", den, lr, lr, ALU.mult)
        P.tt("dve", tmpa, li, li, ALU.mult)
        P.tt("dve", den, den, tmpa, ALU.add)
        P.recip(den, den)
        P.ts("dve", nr, abr, -1.0, ALU.add)
        P.tt("dve", tmpa, nr, lr, ALU.mult)
        P.tt("dve", tmpb, abi, li, ALU.mult)
        P.tt("dve", tmpa, tmpa, tmpb, ALU.add)
        P.tt("dve", fr, tmpa, den, ALU.mult)
        P.tt("dve", tmpa, abi, lr, ALU.mult)
        P.tt("dve", tmpb, nr, li, ALU.mult)
        P.tt("dve", tmpa, tmpa, tmpb, ALU.subtract)
        P.tt("dve", fi, tmpa, den, ALU.mult)
        BC = A.alloc("s5bc", [128, 4, 12, 16], F32)
        P.dma("sp", BC[:, 0:2], self.s5b[l], "s5")
        P.dma("sp", BC[:, 2:4], self.s5c[l], "s5")
        BB = A.alloc("s5bb", [128, 2, 12, 16], F32)
        T1 = A.alloc("s5t1", [128, 12, 16], F32)
        frb, fib = fr.ap.unsqueeze(2).to_broadcast([128, 12, 16]), fi.ap.unsqueeze(2).to_broadcast([128, 12, 16])
        frb, fib = View(W, frb), View(W, fib)
        P.tt("dve", BB[:, 0], BC[:, 0], frb, ALU.mult)
        P.tt("dve", T1.v, BC[:, 1], fib, ALU.mult)
        P.tt("dve", BB[:, 0], BB[:, 0], T1.v, ALU.subtract)
        P.tt("dve", BB[:, 1], BC[:, 1], frb, ALU.mult)
        P.tt("dve", T1.v, BC[:, 0], fib, ALU.mult)
        P.tt("dve", BB[:, 1], BB[:, 1], T1.v, ALU.add)
        MP = [A.alloc(f"s5mp{i}", [128, 128], F32) for i in range(2)]
        for t in MP:
            P.memset("dve", t.v, 0.0)
        ident = self.CONST[:, K_ID:K_ID + 128]
        n = 0
        for ri in range(2):
            for pair in range(12):
                k = pair % 4
                mp = MP[n % 2]
                n += 1
                for gl in range(2):
                    P.copy("dve", mp[64 * gl:64 * gl + 64, 32 * k + 16 * gl:32 * k + 16 * gl + 16],
                           BB[64 * gl:64 * gl + 64, ri, pair, :])
                ps = self.bank()
                P.tr(ps[:, 0:128], mp.v, ident)
                P.copy("act", self.LB[:, ri, pair, :], ps[:, 0:128])
                self.put(ps)
                for gl in range(2):
                    P.memset("dve", mp[64 * gl:64 * gl + 64, 32 * k + 16 * gl:32 * k + 16 * gl + 16], 0.0)
                for gl in range(2):
                    dst = self.LC[64 * gl:64 * gl + 64, ri, pair, 32 * k + 16 * gl:32 * k + 16 * gl + 16]
                    src = BC[64 * gl:64 * gl + 64, 2 + ri, pair, :]
                    if ri == 0:
                        P.copy("act", dst, src)
                    else:
                        P.act(dst, src, AF.Identity, scale=-1.0)
        for c in range(3):
            P.ts("dve", self.DD[:, c, :], self.CONST[:, K_ID:K_ID + 128], self.col(C_SSMD + c), ALU.mult)
        lam = self.col(C_LLAM, 4)
        P.act(self.CCOL, lam, AF.Exp, scale=-1.0)
        P.act(self.CCOL, self.CCOL, AF.Ln, bias=1.0)
        P.ts("dve", self.C2COL, self.CCOL, -16.0, ALU.mult)
        P.ts("dve", self.CCOL, self.CCOL, -8.0, ALU.mult)
        for ax in range(2):
            for c in range(4):
                for hh in range(2):
                    P.dma("pool", self.LW[64 * hh:64 * hh + 64, ax, c, 64 * hh:64 * hh + 64],
                          self.lruw[l, ax, 2 * c + hh], "lw")
        DL = A.alloc("dl", [128, 4, 64], F32)
        P.dma("sp", DL.v, self.dl[l:l + 1].partition_broadcast(128), "s5")
        PR = A.alloc("dlp", [128, 2, 64], F32)
        P.tt("dve", PR[:, 0], DL[:, 0], DL[:, 1], ALU.mult)
        P.tt("dve", PR[:, 1], DL[:, 2], DL[:, 3], ALU.mult)
        SM = D[:, 40:42]
        P.rsum(SM, PR.v)
        P.act(SM, SM, AF.Exp)
        lam_init = 0.8 - 0.6 * math.exp(-0.3 * l)
        P.tt("dve", self.NLAM, D[:, 41:42], D[:, 40:41], ALU.subtract)
        P.ts("dve", self.NLAM, self.NLAM, -lam_init, ALU.add)
        P.dma("sp", self.SUBG.v, self.subg[l:l + 1].partition_broadcast(128), "s5")
        P.ts("dve", self.SUBG.v, self.SUBG.v, 1.0 - lam_init, ALU.mult)
        self.dump("LB", self.LB[:, :, :, :])
        self.dump("DER", self.DER.v)
        A.release(m)
        m = A.mark()
        MT = A.alloc("memT", [128, 8, MEM_LEN], F32)
        MN = A.alloc("memn", [128, 8, MEM_LEN], BF16)
        SQ = [A.alloc(f"msq{i}", [128, MEM_LEN], BF16) for i in range(2)]
        RS = A.alloc("mrs", [128, MEM_LEN], F32)
        P.dma("sp", MT.v, self.memT[s], "mem")
        ps = self.bank()
        for c in range(8):
            P.act(SQ[c % 2].v, MT[:, c, :], AF.Square)
            P.mm(ps[:, 0:MEM_LEN], self.ONESB.v, SQ[c % 2].v, start=(c == 0), stop=(c == 7))
        self.rstd(RS.v, ps[:, 0:MEM_LEN])
        self.put(ps)
        for c in range(8):
            P.stt("dve", MN[:, c, :], MT[:, c, :], self.col(C_GMEM + c), RS.v, ALU.mult, ALU.mult)
        for g in range(2):
            w = self.wload(self.wkv[l][:, g * 512:(g + 1) * 512], 8, 512)
            for mi in range(4):
                ps = self.bank()
                for kc in range(8):
                    P.mm(ps[:, 0:MEM_LEN], w[:, kc, mi * 128:(mi + 1) * 128], MN[:, kc, :], start=(kc == 0), stop=(kc == 7))
                P.copy("act", self.KX[:, g * 4 + mi, :], ps[:, 0:MEM_LEN])
                self.put(ps)
        for g in range(2):
            w = self.wload(self.wkv[l][:, 1024 + g * 512:1024 + (g + 1) * 512], 8, 512)
            for kt in range(2):
                ps = self.bank()
                for kc in range(8):
                    P.mm(ps.v, MN[:, kc, kt * 128:(kt + 1) * 128], w[:, kc, :], start=(kc == 0), stop=(kc == 7))
                P.copy("act", self.VX[:, kt, g * 512:(g + 1) * 512], ps.v)
                self.put(ps)
        A.release(m)

    def mixer(self, s, l, b):
        P, A = self.P, self.A
        t0 = b * 512
        win = self.w_in[l]
        self.rmsnorm(b, C_GMIX)
        self.dump("HN", self.HN.v)
        m = A.mark()
        Q = A.alloc("Q", [128, 4, 512], BF16)
        RC = A.alloc("RC", [128, 512], F32)
        RS = A.alloc("RS", [128, 512], F32)
        T1 = A.alloc("T1", [128, 512], F32)
        T2 = A.alloc("T2", [128, 512], F32)
        m_pi = A.mark()
        PI32 = A.alloc("PI32", [128, 512], I32)
        P.dma("sp", PI32.v, self.pos[s:s + 1, t0:t0 + 512].partition_broadcast(128), "pos")
        P.copy("dve", T1.v, PI32.v)
        P.ts("dve", T1.v, T1.v, self.CONST[:, K_INVF:K_INVF + 1], ALU.mult, 1.0 / (2 * PI), ALU.mult)
        self.sincos(RC.v, RS.v, T1.v, T1.v, T2.v, sign=True)
        A.release(m_pi)

        def proj_units():
            for (o_a, o_s, dst) in ((O_Q, O_QS, None), (O_K, O_KS, "k")):
                wa = self.wload(win[:, o_a:o_a + 512], 8, 512)
                wsw = self.wload(win[:, o_s:o_s + 512], 8, 512)
                for j in range(4):
                    pa, pb = self.bank(), self.bank()
                    for kc in range(8):
                        P.mm(pa.v, wa[:, kc, j * 128:(j + 1) * 128], self.HN[:, kc, :], start=(kc == 0), stop=(kc == 7))
                    for kc in range(8):
                        P.mm(pb.v, wsw[:, kc, j * 128:(j + 1) * 128], self.HN[:, kc, :], start=(kc == 0), stop=(kc == 7))
                    P.tt("dve", T1.v, pa.v, RC.v, ALU.mult)
                    P.tt("dve", T2.v, pb.v, RS.v, ALU.mult)
                    o = Q[:, j, :] if dst is None else self.KC[:, j, t0:t0 + 512]
                    P.tt("dve", o, T1.v, T2.v, ALU.add)
                    self.put(pa, pb)
                    yield
            wv = self.wload(win[:, O_V:O_V + 512], 8, 512)
            for i in range(4):
                ps = self.bank()
                for kc in range(8):
                    P.mm(ps.v, self.HN[:, kc, i * 128:(i + 1) * 128], wv[:, kc, :], start=(kc == 0), stop=(kc == 7))
                P.copy("act", self.VC[:, 4 * b + i, :, 0:128], View(ps, ps.ap.rearrange("p (h e) -> p h e", h=4)))
                self.put(ps)
                yield

        m5 = A.mark()
        YG = A.alloc("YG", [128, 3, 512], BF16)
        UB = A.alloc("UB", [128, 3, 512], BF16)
        COS = A.alloc("COS", [128, 512], F32)
        SIN = A.alloc("SIN", [128, 512], F32)
        M1 = A.alloc("M1", [128, 512], F32)
        M2 = A.alloc("M2", [128, 512], F32)
        WINR = A.alloc("WINR", [128, 512], F32)
        WINI = A.alloc("WINI", [128, 512], F32)
        WR = A.alloc("WR", [128, 512], F32)
        WI = A.alloc("WI", [128, 512], F32)
        SR = A.alloc("SR", [128, 512], BF16)
        SI = A.alloc("SI", [128, 512], BF16)
        wu = self.wload(win[:, O_U:O_U + 384], 8, 384)
        t512 = self.CONST[:, K_T:K_T + 512]
        THT0 = self.DER[:, 44:56]
        P.ts("dve", THT0, self.TH, float(t0), ALU.mult)
        for c in range(3):
            ps = self.bank()
            for kc in range(8):
                P.mm(ps.v, wu[:, kc, c * 128:(c + 1) * 128], self.HN[:, kc, :], start=(kc == 0), stop=(kc == 7))
            P.copy("act", UB[:, c, :], ps.v)
            self.put(ps)
        units = proj_units()
        for c in range(3):
            yacc = self.bank()
            P.mm(yacc.v, self.DD[:, c, :], UB[:, c, :], start=True, stop=False)
            for pr in range(4):
                pair = 4 * c + pr
                bur, bui = self.bank(), self.bank()
                P.mm(bur.v, self.LB[:, 0, pair, :], UB[:, c, :])
                P.mm(bui.v, self.LB[:, 1, pair, :], UB[:, c, :])
                P.act(M1.v, t512, AF.Identity, bias=THT0[:, pair:pair + 1], scale=self.TH[:, pair:pair + 1])
                self.sincos(COS.v, SIN.v, M1.v, M1.v, M2.v)
                P.tt("dve", M1.v, bur.v, COS.v, ALU.mult)
                P.tt("dve", M2.v, bui.v, SIN.v, ALU.mult)
                P.tt("dve", WINR.v, M1.v, M2.v, ALU.add)
                P.tt("dve", M1.v, bui.v, COS.v, ALU.mult)
                P.tt("dve", M2.v, bur.v, SIN.v, ALU.mult)
                P.tt("dve", WINI.v, M1.v, M2.v, ALU.subtract)
                self.put(bur, bui)
                next(units, None)
                rho = self.RHO[:, pair:pair + 1].bc([128, 512])
                P.scan(WR.v, rho, WINR.v, self.S5ST[:, pair, 0:1])
                P.scan(WI.v, rho, WINI.v, self.S5ST[:, pair, 1:2])
                P.copy("act", self.S5ST[:, pair, 0:1], WR[:, 511:512])
                P.copy("act", self.S5ST[:, pair, 1:2], WI[:, 511:512])
                P.tt("dve", M1.v, WR.v, COS.v, ALU.mult)
                P.tt("dve", M2.v, WI.v, SIN.v, ALU.mult)
                P.tt("dve", SR.v, M1.v, M2.v, ALU.subtract)
                P.tt("dve", M1.v, WR.v, SIN.v, ALU.mult)
                P.tt("dve", M2.v, WI.v, COS.v, ALU.mult)
                P.tt("dve", SI.v, M1.v, M2.v, ALU.add)
                P.mm(yacc.v, self.LC[:, 0, pair, :], SR.v, start=False, stop=False)
                P.mm(yacc.v, self.LC[:, 1, pair, :], SI.v, start=False, stop=(pr == 3))
            P.act(YG[:, c, :], yacc.v, AF.Gelu)
            self.put(yacc)
        for _ in units:
            pass
        wg = self.wload(self.w_glu[l], 3, 768)
        SG = M1
        for j in range(3):
            p1, p2 = self.bank(), self.bank()
            for kc in range(3):
                P.mm(p1.v, wg[:, kc, j * 128:(j + 1) * 128], YG[:, kc, :], start=(kc == 0), stop=(kc == 2))
            for kc in range(3):
                P.mm(p2.v, wg[:, kc, 384 + j * 128:384 + (j + 1) * 128], YG[:, kc, :], start=(kc == 0), stop=(kc == 2))
            P.act(SG.v, p2.v, AF.Sigmoid, bias=self.col(C_BGLU + 3 + j))
            P.stt("dve", self.YSSM[:, j, :], p1.v, self.col(C_BGLU + j), SG.v, ALU.add, ALU.mult)
            self.put(p1, p2)
        self.dump("YSSM", self.YSSM.v)
        A.release(m5)
        PT = [A.alloc(f"PT{i}", [128, 512], BF16) for i in range(16)]
        SMALL = [A.alloc(f"SMALL{i}", [128, 8], F32) for i in range(4)]
        OA = [A.alloc(f"OA{i}", [128, 128], F32) for i in range(4)]
        OB = [A.alloc(f"OB{i}", [128, 128], F32) for i in range(4)]
        self.dump("Q", Q.v)
        ident = self.CONST[:, K_ID:K_ID + 128]
        LOOK = 2
        nk = 4 * (b + 1)
        for h in range(4):
            ob = [self.bank() for _ in range(4)]
            steps = [(c, kt) for c in range(2) for kt in range(nk)]

            def score(c, kt):
                i0 = max(0, kt - 4 * b)
                qlo = 128 * i0
                ps = self.bank()
                P.mm(ps[:, qlo:512], self.KC[64 * c:64 * c + 64, h, 128 * kt:128 * kt + 128],
                     Q[64 * c:64 * c + 64, h, qlo:512])
                P.act(PT[kt][:, qlo:512], ps[:, qlo:512], AF.Exp, scale=0.125)
                self.put(ps)
                if kt >= 4 * b:
                    P.tt("dve", PT[kt][:, qlo:qlo + 128], PT[kt][:, qlo:qlo + 128], self.CMB.v, ALU.mult)

            for si in range(min(LOOK, len(steps))):
                score(*steps[si])
            for si, (c, kt) in enumerate(steps):
                if si + LOOK < len(steps):
                    score(*steps[si + LOOK])
                i0 = max(0, kt - 4 * b)
                for i in range(i0, 4):
                    P.mm(ob[i][:, 130 * c:130 * c + 129], PT[kt][:, 128 * i:128 * i + 128],
                         self.VC[:, kt, h, 0:129], start=(kt == 0), stop=(kt == 4 * b + i))
            tb = self.bank()
            R4 = range(4)
            for i in R4:
                P.recip(SMALL[i][:, 0:1], ob[i][:, 128:129])
            for i in R4:
                P.recip(SMALL[i][:, 1:2], ob[i][:, 258:259])
            for i in R4:
                P.tt("dve", SMALL[i][:, 2:3], SMALL[i][:, 1:2], self.NLAM, ALU.mult)
            for i in R4:
                P.act(OA[i].v, ob[i][:, 0:128], AF.Identity, scale=SMALL[i][:, 0:1])
            for i in R4:
                P.stt("dve", OB[i].v, ob[i][:, 130:258], SMALL[i][:, 2:3], OA[i].v, ALU.mult, ALU.add)
            self.put(*ob)
            for i in R4:
                P.act(OA[i].v, OB[i].v, AF.Square)
            for i in R4:
                P.rsum(SMALL[i][:, 3:4], OA[i].v)
            for i in R4:
                P.act(SMALL[i][:, 5:6], SMALL[i][:, 3:4], AF.Sqrt, bias=self.EPSC, scale=1.0 / 128.0)
            for i in R4:
                P.recip(SMALL[i][:, 5:6], SMALL[i][:, 5:6])
            for i in R4:
                P.stt("dve", OA[i].v, OB[i].v, SMALL[i][:, 5:6], self.SUBG.v, ALU.mult, ALU.mult)
            for i in R4:
                P.tr(tb[:, 128 * i:128 * i + 128], OA[i].v, ident)
            P.copy("act", self.YATT[:, h, :], tb.v)
            self.put(tb)
        self.dump("YATT", self.YATT.v)
        A.release(m)
        if self.stop_after == "att":
            return
        m = A.mark()
        LT = []
        XCB1 = A.alloc("XCB", [128, 512], BF16)
        for p in range(2):
            LT.append(dict(
                XRH=A.alloc(f"XRH{p}", [128, 515], F32), GG=A.alloc(f"GG{p}", [128, 512], F32),
                XC=A.alloc(f"XC{p}", [128, 512], F32), XCB=XCB1,
                R=A.alloc(f"R{p}", [128, 512], F32), IG=A.alloc(f"IG{p}", [128, 512], F32),
                AA=A.alloc(f"AA{p}", [128, 512], F32), A2=A.alloc(f"A2{p}", [128, 512], F32),
                GX=A.alloc(f"GX{p}", [128, 512], F32), H=A.alloc(f"H{p}", [128, 512], F32)))
        wxr = self.wload(win[:, O_XR:O_XR + 512], 8, 512)
        wgr = self.wload(win[:, O_GR:O_GR + 512], 8, 512)
        for c in range(4):
            d_ = LT[c % 2]
            XRH, GG, XC, XCB, R, IG, AA, A2, GX, H = (d_[k] for k in ("XRH", "GG", "XC", "XCB", "R", "IG", "AA", "A2", "GX", "H"))
            px, pg = self.bank(), self.bank()
            for kc in range(8):
                P.mm(px.v, wxr[:, kc, c * 128:(c + 1) * 128], self.HN[:, kc, :], start=(kc == 0), stop=(kc == 7))
            for kc in range(8):
                P.mm(pg.v, wgr[:, kc, c * 128:(c + 1) * 128], self.HN[:, kc, :], start=(kc == 0), stop=(kc == 7))
            P.copy("act", XRH[:, 0:3], self.XHALO[:, c, :])
            P.copy("act", XRH[:, 3:515], px.v)
            P.act(GG.v, pg.v, AF.Gelu)
            self.put(px, pg)
            P.copy("act", self.XHALO[:, c, :], XRH[:, 512:515])
            cw = C_LCW + 4 * c
            P.ts("dve", XC.v, XRH[:, 3:515], self.col(cw + 3), ALU.mult, self.col(C_LCB + c), ALU.add)
            for j in range(3):
                P.stt("dve", XC.v, XRH[:, j:j + 512], self.col(cw + j), XC.v, ALU.mult, ALU.add)
            P.copy("dve", XCB.v, XC.v)
            pa, pi = self.bank(), self.bank()
            P.mm(pa.v, self.LW[:, 0, c, :], XCB.v)
            P.mm(pi.v, self.LW[:, 1, c, :], XCB.v)
            P.act(R.v, pa.v, AF.Sigmoid, bias=self.col(C_LBA + c))
            P.act(IG.v, pi.v, AF.Sigmoid, bias=self.col(C_LBX + c))
            self.put(pa, pi)
            P.act(AA.v, R.v, AF.Exp, scale=self.CCOL[:, c:c + 1])
            P.act(A2.v, R.v, AF.Exp, scale=self.C2COL[:, c:c + 1])
            P.act(A2.v, A2.v, AF.Sqrt, bias=1.0, scale=-1.0)
            P.tt("dve", GX.v, IG.v, XC.v, ALU.mult)
            P.tt("dve", GX.v, GX.v, A2.v, ALU.mult)
            P.scan(H.v, AA.v, GX.v, self.LST[:, c:c + 1])
            P.copy("act", self.LST[:, c:c + 1], H[:, 511:512])
            P.tt("dve", self.YLRU[:, c, :], H.v, GG.v, ALU.mult)
        self.dump("YLRU", self.YLRU.v)
        A.release(m)
        if self.stop_after == "lru":
            return
        m = A.mark()
        MG = A.alloc("MG", [128, 8, 512], BF16)
        Gs = [[A.alloc(f"G{p}{i}", [128, 512], F32) for i in range(3)] for p in range(2)]
        TTs = [[A.alloc(f"TT{p}{i}", [128, 512], F32) for i in range(2)] for p in range(2)]
        ys = [(self.YSSM, 3), (self.YATT, 4), (self.YLRU, 4)]
        for mo in range(8):
            wgt = self.wload(win[:, O_G + mo * 384:O_G + (mo + 1) * 384], 8, 384)
            if mo % 2 == 0:
                wbr = self.wload(self.w_br[l][:, mo * 128:(mo + 2) * 128], 11, 256)
            G, TT = Gs[mo % 2], TTs[mo % 2]
            gps = [self.bank() for _ in range(3)]
            bps = [self.bank() for _ in range(3)]
            for i in range(3):
                for kc in range(8):
                    P.mm(gps[i].v, wgt[:, kc, i * 128:(i + 1) * 128], self.HN[:, kc, :], start=(kc == 0), stop=(kc == 7))
            k0 = 0
            for i in range(3):
                yt, nkc = ys[i]
                for kc in range(nkc):
                    P.mm(bps[i].v, wbr[:, k0 + kc, (mo % 2) * 128:(mo % 2) * 128 + 128], yt[:, kc, :],
                         start=(kc == 0), stop=(kc == nkc - 1))
                k0 += nkc
            for i in range(3):
                P.act(G[i].v, gps[i].v, AF.Sigmoid)
            self.put(*gps)
            P.tt("dve", TT[0].v, G[0].v, bps[0].v, ALU.mult)
            P.tt("dve", TT[1].v, G[1].v, bps[1].v, ALU.mult)
            P.tt("dve", TT[0].v, TT[0].v, TT[1].v, ALU.add)
            P.tt("dve", TT[1].v, G[2].v, bps[2].v, ALU.mult)
            P.tt("dve", MG[:, mo, :], TT[0].v, TT[1].v, ALU.add)
            self.put(*bps)
        self.proj_residual(self.w_out[l], MG, 8, t0)
        A.release(m)

    def proj_residual(self, w, act, nkc, t0):
        P = self.P
        if nkc * 512 <= 4096:
            groups = [(g * 512, 512) for g in range(2)]
        else:
            groups = [(g * 128, 128) for g in range(8)]
        for (m0, mw) in groups:
            wt = self.wload(w[:, m0:m0 + mw], nkc, mw)
            for mi in range(mw // 128):
                ps = self.bank()
                for kc in range(nkc):
                    P.mm(ps.v, wt[:, kc, mi * 128:(mi + 1) * 128], act[:, kc, :], start=(kc == 0), stop=(kc == nkc - 1))
                mo = m0 // 128 + mi
                P.tt("dve", self.X[:, mo, t0:t0 + 512], self.X[:, mo, t0:t0 + 512], ps.v, ALU.add)
                self.put(ps)

    def xattn(self, s, l, b):
        P, A = self.P, self.A
        t0 = b * 512
        self.rmsnorm(b, C_GXA)
        m = A.mark()
        QX = A.alloc("QX", [128, 8, 512], BF16)
        OX = A.alloc("OX", [128, 8, 512], BF16)
        PXs = [[A.alloc(f"PX{p}{i}", [128, 512], BF16) for i in range(2)] for p in range(2)]
        RDs = [A.alloc(f"RD{p}", [128, 512], F32) for p in range(2)]
        for g in range(2):
            wt = self.wload(self.wq[l][:, g * 512:(g + 1) * 512], 8, 512)
            for mi in range(4):
                ps = self.bank()
                for kc in range(8):
                    P.mm(ps.v, wt[:, kc, mi * 128:(mi + 1) * 128], self.HN[:, kc, :], start=(kc == 0), stop=(kc == 7))
                P.copy("act", QX[:, g * 4 + mi, :], ps.v)
                self.put(ps)
        def xa_scores(h):
            PX = PXs[h % 2]
            for kt in range(2):
                ps = self.bank()
                for dc in range(2):
                    P.mm(ps.v, self.KX[:, 2 * h + dc, 128 * kt:128 * kt + 128], QX[:, 2 * h + dc, :],
                         start=(dc == 0), stop=(dc == 1))
                P.act(PX[kt].v, ps.v, AF.Exp, scale=1.0 / 16.0)
                self.put(ps)

        def xa_rest(h):
            PX, RD = PXs[h % 2], RDs[h % 2]
            pd = self.bank()
            for kt in range(2):
                P.mm(pd.v, self.ONE1.v, PX[kt].v, start=(kt == 0), stop=(kt == 1))
            P.recip(RD.v, pd.v)
            self.put(pd)
            for ec in range(2):
                po = self.bank()
                for kt in range(2):
                    P.mm(po.v, self.VX[:, kt, (2 * h + ec) * 128:(2 * h + ec) * 128 + 128], PX[kt].v,
                         start=(kt == 0), stop=(kt == 1))
                P.tt("dve", OX[:, 2 * h + ec, :], po.v, RD.v, ALU.mult)
                self.put(po)

        xa_scores(0)
        for h in range(4):
            if h + 1 < 4:
                xa_scores(h + 1)
            xa_rest(h)
        self.proj_residual(self.wo[l], OX, 8, t0)
        A.release(m)

    def ffn(self, s, l, b):
        P, A = self.P, self.A
        t0 = b * 512
        self.rmsnorm(b, C_GFFN)
        m = A.mark()
        HF = A.alloc("HF", [128, NFF, 512], BF16)
        UR = [A.alloc(f"UR{i}", [128, 514], F32) for i in range(2)]
        ACC = [[A.alloc(f"ACC{p}{i}", [128, 512], F32) for i in range(2)] for p in range(2)]
        SG = A.alloc("SGF", [128, 512], F32)

        def stage_b(f):
            P.act(SG.v, ACC[f % 2][1].v, AF.Silu)
            P.tt("dve", HF[:, f, :], ACC[f % 2][0].v, SG.v, ALU.mult)

        for f in range(NFF):
            if f % 2 == 0:
                wt = self.wload(self.w_up[l][:, f * 256:(f + 2) * 256], 8, 512)
            pss = [self.bank(), self.bank()]
            for vg in range(2):
                c0 = (f % 2) * 256 + vg * 128
                for kc in range(8):
                    P.mm(pss[vg].v, wt[:, kc, c0:c0 + 128], self.HN[:, kc, :], start=(kc == 0), stop=(kc == 7))
            for vg in range(2):
                idx = 2 * f + vg
                ur, acc, ps = UR[vg], ACC[f % 2][vg], pss[vg]
                cw = C_FCW + idx * 3
                P.copy("act", ur[:, 0:2], self.CF[:, idx, :])
                P.copy("act", ur[:, 2:514], ps.v)
                P.act(acc.v, ps.v, AF.Identity, bias=self.col(C_FCB + idx), scale=self.col(cw + 2))
                P.copy("act", self.CF[:, idx, :], ur[:, 512:514])
                P.stt("dve", acc.v, ur[:, 1:513], self.col(cw + 1), acc.v, ALU.mult, ALU.add)
                P.stt("dve", acc.v, ur[:, 0:512], self.col(cw + 0), acc.v, ALU.mult, ALU.add)
            self.put(*pss)
            if f >= 1:
                stage_b(f - 1)
        stage_b(NFF - 1)
        self.proj_residual(self.w_down[l], HF, NFF, t0)
        A.release(m)

    def final(self, s, b):
        P, A = self.P, self.A
        t0 = b * 512
        m = A.mark()
        SQ = [A.alloc(f"fSQ{i}", [128, 512], BF16) for i in range(2)]
        RSTD = A.alloc("fRSTD", [128, 512], F32)
        OT = A.alloc("OT", [128, 8, 512], F32)
        ps = self.bank()
        for c in range(8):
            P.act(SQ[c % 2].v, self.X[:, c, t0:t0 + 512], AF.Square)
            P.mm(ps.v, self.ONESB.v, SQ[c % 2].v, start=(c == 0), stop=(c == 7))
        self.rstd(RSTD.v, ps.v)
        self.put(ps)
        for c in range(8):
            P.stt("dve", OT[:, c, :], self.X[:, c, t0:t0 + 512], self.FNG[:, c:c + 1], RSTD.v, ALU.mult, ALU.mult)
        P.dma("sp", self.outT[s][:, :, t0:t0 + 512], OT.v, "out")
        A.release(m)


def _colT(v, n):
    return np.ascontiguousarray(np.asarray(v).reshape(n, 128).T)


def host_layout(inp):
    f32 = np.float32
    L = DEPTH
    g = {k: np.asarray(v) for k, v in inp.items()}
    consts = np.zeros((128, NCONST), f32)
    consts[:, K_ID:K_ID + 128] = np.eye(128, dtype=f32)
    kk, qq = np.meshgrid(np.arange(128), np.arange(128), indexing="ij")
    consts[:, K_CM:K_CM + 128] = (kk <= qq).astype(f32)
    consts[:, K_T:K_T + 512] = np.arange(512, dtype=f32)[None, :]
    inv = np.exp(np.float32(-math.log(10000.0)) * np.arange(0, 64, 2, dtype=f32) / np.float32(64)).astype(f32)
    rows = np.arange(128)
    consts[:, K_INVF] = inv[rows % 32]
    sign = np.where((rows % 64) < 32, -1.0, 1.0).astype(f32)
    consts[:, K_SIGN] = np.float32(2 * PI) * sign
    consts[:, K_NPS] = (-np.float32(PI)) * sign
    cols = np.zeros((L, 128, NCOL), f32)
    for l in range(L):
        cols[l, :, C_GMIX:C_GMIX + 8] = _colT(g["norm_mix_g"][l], 8)
        cols[l, :, C_GXA:C_GXA + 8] = _colT(g["norm_xattn_g"][l], 8)
        cols[l, :, C_GMEM:C_GMEM + 8] = _colT(g["norm_mem_g"][l], 8)
        cols[l, :, C_GFFN:C_GFFN + 8] = _colT(g["norm_ffn_g"][l], 8)
        cols[l, :, C_SSMD:C_SSMD + 3] = _colT(g["ssm_d"][l], 3)
        cols[l, :, C_BGLU:C_BGLU + 6] = _colT(g["ssm_b_glu"][l], 6)
        cw = g["lru_conv_w"][l]
        cols[l, :, C_LCW:C_LCW + 16] = cw.reshape(4, 4, 128).transpose(2, 1, 0).reshape(128, 16)
        cols[l, :, C_LCB:C_LCB + 4] = _colT(g["lru_conv_b"][l], 4)
        cols[l, :, C_LBA:C_LBA + 4] = _colT(g["lru_ba"][l], 4)
        cols[l, :, C_LBX:C_LBX + 4] = _colT(g["lru_bx"][l], 4)
        cols[l, :, C_LLAM:C_LLAM + 4] = _colT(g["lru_lambda"][l], 4)
        fw = g["ffn_conv_w"][l].reshape(3, 2, NFF, 128)
        cols[l, :, C_FCW:C_FCW + 132] = fw.transpose(3, 2, 1, 0).reshape(128, 132)
        fb = g["ffn_conv_b"][l].reshape(2, NFF, 128)
        cols[l, :, C_FCB:C_FCB + 44] = fb.transpose(2, 1, 0).reshape(128, 44)
        cols[l, :, C_LR:C_LR + 12] = g["ssm_lambda_re"][l].reshape(12, 2, 64).transpose(1, 2, 0).reshape(128, 12)
        cols[l, :, C_LI:C_LI + 12] = g["ssm_lambda_im"][l].reshape(12, 2, 64).transpose(1, 2, 0).reshape(128, 12)
        cols[l, :, C_LS:C_LS + 12] = np.repeat(g["ssm_log_step"][l].reshape(12, 2, 1), 64, axis=2).transpose(1, 2, 0).reshape(128, 12)
    w = g["w_in"]
    offs = np.cumsum([0, 384, 512, 512, 512, 512, 512, 3072])
    u, q, k, v, xr, gr, gates = [w[:, :, offs[i]:offs[i + 1]] for i in range(7)]

    def swap_halves(a):
        s = a.shape
        a = a.reshape(s[0], s[1], 8, 2, 32)
        return a[:, :, :, ::-1, :].reshape(s)
    gates_re = gates.reshape(L, 1024, 3, 8, 128).transpose(0, 1, 3, 2, 4).reshape(L, 1024, 3072)
    w_in_ext = np.concatenate([u, q, swap_halves(q), k, swap_halves(k), v, xr, gr, gates_re], axis=2)
    w_br = np.concatenate([g["w_br_ssm"], g["w_br_attn"], g["w_br_lru"]], axis=1)
    w_up = g["ffn_w_up"].reshape(L, 1024, 2, NFF, 128).transpose(0, 1, 3, 2, 4).reshape(L, 1024, 2 * D_FF)

    def s5lay(re, im, bside):
        out = np.zeros((L, 128, 2, 12, 16), f32)
        for i, a in enumerate((re, im)):
            if bside:
                out[:, :, i] = a.reshape(L, 12, 2, 64, 16).transpose(0, 2, 3, 1, 4).reshape(L, 128, 12, 16)
            else:
                out[:, :, i] = a.reshape(L, 12, 2, 16, 64).transpose(0, 2, 4, 1, 3).reshape(L, 128, 12, 16)
        return out
    shared = {
        "consts": consts, "cols": cols, "fng": _colT(g["final_norm_g"], 8).astype(f32),
        "w_in": np.ascontiguousarray(w_in_ext), "w_glu": g["ssm_w_glu"], "w_br": np.ascontiguousarray(w_br),
        "w_out": g["w_out"], "wq": g["xattn_wq"], "wkv": g["xattn_wkv"], "wo": g["xattn_wo"],
        "w_up": np.ascontiguousarray(w_up), "w_down": g["ffn_w_down"],
        "s5b": s5lay(g["ssm_b_re"], g["ssm_b_im"], True), "s5c": s5lay(g["ssm_c_re"], g["ssm_c_im"], False),
        "lruw": np.ascontiguousarray(np.stack([g["lru_wa"], g["lru_wx"]], axis=1)),
        "dl": np.ascontiguousarray(np.stack([g["diff_lq1"], g["diff_lk1"], g["diff_lq2"], g["diff_lk2"]], axis=1)),
        "subg": g["diff_subln_g"],
    }
    shared = {k: np.ascontiguousarray(v.astype(f32)) for k, v in shared.items()}
    x, mem, pos = g["x"], g["mem"], g["positions"]
    in_maps = []
    for c in range(8):
        xs = x[2 * c:2 * c + 2]
        ms = mem[2 * c:2 * c + 2]
        d = dict(shared)
        d["xT"] = np.ascontiguousarray(xs.reshape(2, SEQ, 8, 128).transpose(0, 3, 2, 1)).astype(f32)
        d["memT"] = np.ascontiguousarray(ms.reshape(2, MEM_LEN, 8, 128).transpose(0, 3, 2, 1)).astype(f32)
        d["pos"] = np.ascontiguousarray(pos[2 * c:2 * c + 2]).astype(np.int32)
        in_maps.append(d)
    return in_maps


_CACHE = {}


def kernel(**inputs):
    in_maps = host_layout(inputs)
    if "nc" not in _CACHE:
        _CACHE["nc"] = Builder().nc
    nc = _CACHE["nc"]
    res = run_bass_kernel_spmd(nc, in_maps, core_ids=list(range(8)))
    out = np.zeros((16, SEQ, D_MODEL), np.float32)
    for c in range(8):
        o = np.asarray(res.results[c]["outT"])
        out[2 * c:2 * c + 2] = o.transpose(0, 3, 2, 1).reshape(2, SEQ, D_MODEL)
    return out
```
